# Optimizing a Trainium2 kernel written in Bass

```python
import jax, jax.numpy as jnp
from jax import lax
import numpy as np

D_MODEL = 2048
BATCH = 1
SEQ = 8192
DEPTH = 2

GRID_W = 64
HEAD_DIM = 128
NA_HEADS = 4
NA_WIN_ROWS = 8
NA_WIN_COLS = 16
MLA_HEADS = 6
MLA_Q_RANK = 512
MLA_KV_RANK = 256
MLA_NOPE = 128
MLA_ROPE = 64
MLA_V = 128
MLA_Q_BLOCK = 128
ROPE_THETA = 10000.0
SWA_HEADS = 6
SWA_KV_HEADS = 2
SWA_WINDOW = 128
SWA_BLOCK = 128

D_A = NA_HEADS * HEAD_DIM
D_B = MLA_HEADS * MLA_V
D_C = SWA_HEADS * HEAD_DIM
D_MIX = D_A + D_B + D_C
IN_A = 3 * D_A
IN_B = MLA_Q_RANK + MLA_KV_RANK + MLA_ROPE
IN_C = (SWA_HEADS + 2 * SWA_KV_HEADS) * HEAD_DIM
D_IN = IN_A + IN_B + IN_C
D_FF = -(-8 * D_MODEL // (3 * 256)) * 256

DEEPNORM_ALPHA = (2 * DEPTH) ** 0.25
DEEPNORM_BETA = (8 * DEPTH) ** -0.25
LN_EPS = 1e-5
RMS_EPS = 1e-6
NEG_INF = -1e30

kernel_name = "hybrid_parallel_heads_encoder"


def layer_norm(x, g, b):
    xf = x.astype(jnp.float32)
    mu = xf.mean(-1, keepdims=True)
    var = jnp.square(xf - mu).mean(-1, keepdims=True)
    return ((xf - mu) * lax.rsqrt(var + LN_EPS) * g.astype(jnp.float32) + b.astype(jnp.float32)).astype(x.dtype)


def rms_norm(x, g):
    xf = x.astype(jnp.float32)
    ms = jnp.square(xf).mean(-1, keepdims=True)
    return (xf * lax.rsqrt(ms + RMS_EPS) * g.astype(jnp.float32)).astype(x.dtype)


def rope(x, pos):
    half = x.shape[-1] // 2
    inv = ROPE_THETA ** (-jnp.arange(half, dtype=jnp.float32) / half)
    ang = pos.astype(jnp.float32)[:, None] * inv[None, :]
    cos = jnp.cos(ang)[:, None, :]
    sin = jnp.sin(ang)[:, None, :]
    x1 = x[..., :half].astype(jnp.float32)
    x2 = x[..., half:].astype(jnp.float32)
    return jnp.concatenate([x1 * cos - x2 * sin, x2 * cos + x1 * sin], axis=-1).astype(x.dtype)


def neighbourhood_attention(q, k, v, rpb):
    B, S, H, d = q.shape
    rows = S // GRID_W
    kh = min(NA_WIN_ROWS, rows)
    r = jnp.arange(rows)
    r0 = jnp.clip(r - kh // 2, 0, rows - kh)
    row_idx = r0[:, None] + jnp.arange(kh)[None, :]
    col = jnp.arange(GRID_W)
    c0 = jnp.clip(col - NA_WIN_COLS // 2, 0, GRID_W - NA_WIN_COLS)
    col_in = (col[None, :] >= c0[:, None]) & (col[None, :] < c0[:, None] + NA_WIN_COLS)
    qg = q.reshape(B, rows, GRID_W, H, d)
    kg = k.reshape(B, rows, GRID_W, H, d)[:, row_idx]
    vg = v.reshape(B, rows, GRID_W, H, d)[:, row_idx]
    s = jnp.einsum('brqhd,brkwhd->bhrqkw', qg, kg).astype(jnp.float32) * (d ** -0.5)
    roff = row_idx - r[:, None] + (NA_WIN_ROWS - 1)
    coff = jnp.clip(col[None, :] - col[:, None], -(NA_WIN_COLS - 1), NA_WIN_COLS - 1) + (NA_WIN_COLS - 1)
    bias = rpb.astype(jnp.float32)[:, roff[:, None, :, None], coff[None, :, None, :]]
    s = jnp.where(col_in[None, None, None, :, None, :], s + bias[None], NEG_INF)
    p = jax.nn.softmax(s.reshape(B, H, rows, GRID_W, kh * GRID_W), axis=-1)
    p = p.reshape(B, H, rows, GRID_W, kh, GRID_W).astype(v.dtype)
    o = jnp.einsum('bhrqkw,brkwhd->brqhd', p, vg)
    return o.reshape(B, S, H * d)


def latent_attention(c_q, c_kv, k_rope, q_norm_g, kv_norm_g, w_uq, w_ukv, pos):
    B, S, _ = c_q.shape
    H = MLA_HEADS
    q = (rms_norm(c_q, q_norm_g) @ w_uq).reshape(B, S, H, MLA_NOPE + MLA_ROPE)
    q_nope = q[..., :MLA_NOPE]
    q_pe = rope(q[..., MLA_NOPE:], pos)
    kv = (rms_norm(c_kv, kv_norm_g) @ w_ukv).reshape(B, S, H, MLA_NOPE + MLA_V)
    k_nope = kv[..., :MLA_NOPE]
    v = kv[..., MLA_NOPE:]
    k_pe = rope(k_rope[:, :, None, :], pos)[:, :, 0]
    scale = (MLA_NOPE + MLA_ROPE) ** -0.5
    nb = S // MLA_Q_BLOCK
    qn_blocks = q_nope.reshape(B, nb, MLA_Q_BLOCK, H, MLA_NOPE).transpose(1, 0, 2, 3, 4)
    qp_blocks = q_pe.reshape(B, nb, MLA_Q_BLOCK, H, MLA_ROPE).transpose(1, 0, 2, 3, 4)

    def one_block(blk):
        qn, qp = blk
        s = jnp.einsum('bqhd,bkhd->bhqk', qn, k_nope) + jnp.einsum('bqhr,bkr->bhqk', qp, k_pe)
        p = jax.nn.softmax(s.astype(jnp.float32) * scale, axis=-1).astype(v.dtype)
        return jnp.einsum('bhqk,bkhd->bqhd', p, v)

    o = lax.map(one_block, (qn_blocks, qp_blocks))
    return o.transpose(1, 0, 2, 3, 4).reshape(B, S, H * MLA_V)


def windowed_gqa(q, k, v, sink):
    B, S, H, d = q.shape
    hkv = k.shape[2]
    G = H // hkv
    T = SWA_BLOCK
    nb = S // T
    qb = q.reshape(B, nb, T, hkv, G, d)

    def band(x):
        xp = jnp.pad(x.reshape(B, nb, T, hkv, d), ((0, 0), (1, 1), (0, 0), (0, 0), (0, 0)))
        return jnp.concatenate([xp[:, :-2], xp[:, 1:-1], xp[:, 2:]], axis=2)

    kb = band(k)
    vb = band(v)
    s = jnp.einsum('bnqhgd,bnshd->bhgnqs', qb, kb).astype(jnp.float32) * (d ** -0.5)
    blk = jnp.arange(nb)[:, None]
    qpos = blk * T + jnp.arange(T)[None, :]
    kpos = (blk - 1) * T + jnp.arange(3 * T)[None, :]
    dist = jnp.abs(qpos[:, :, None] - kpos[:, None, :])
    valid = (dist <= SWA_WINDOW) & (kpos[:, None, :] >= 0) & (kpos[:, None, :] < S)
    slopes = jnp.asarray(np.array([2.0 ** (-8.0 * (i + 1) / H) for i in range(H)], dtype=np.float32))
    s = s - slopes.reshape(hkv, G)[None, :, :, None, None, None] * dist.astype(jnp.float32)[None, None, None]
    s = jnp.where(valid[None, None, None], s, NEG_INF)
    sink_l = jnp.broadcast_to(sink.astype(jnp.float32).reshape(1, hkv, G, 1, 1, 1), s.shape[:-1] + (1,))
    p = jax.nn.softmax(jnp.concatenate([s, sink_l], axis=-1), axis=-1)[..., :-1].astype(v.dtype)
    o = jnp.einsum('bhgnqs,bnshd->bnqhgd', p, vb)
    return o.reshape(B, S, H * d)


def setup_inputs(seed: int = 0) -> dict:
    key = jax.random.key(seed)
    ks = jax.random.split(key, 20)
    L, D = DEPTH, D_MODEL
    nrm = jax.random.normal
    beta = DEEPNORM_BETA
    col_scale = np.ones((D_IN,), dtype=np.float32)
    col_scale[2 * D_A:3 * D_A] = beta
    v_c0 = IN_A + IN_B + (SWA_HEADS + SWA_KV_HEADS) * HEAD_DIM
    col_scale[v_c0:v_c0 + SWA_KV_HEADS * HEAD_DIM] = beta
    ukv_scale = np.ones((MLA_HEADS, MLA_NOPE + MLA_V), dtype=np.float32)
    ukv_scale[:, MLA_NOPE:] = beta
    ukv_scale = ukv_scale.reshape(-1)
    return {
        "x": nrm(ks[0], (BATCH, SEQ, D), jnp.float32),
        "c": nrm(ks[1], (BATCH, D), jnp.float32),
        "w_ada": nrm(ks[2], (L, D, 6 * D), jnp.float32) * (0.1 * D ** -0.5),
        "b_ada": nrm(ks[3], (L, 6 * D), jnp.float32) * 0.01,
        "w_in": nrm(ks[4], (L, D, D_IN), jnp.float32) * (D ** -0.5) * jnp.asarray(col_scale),
        "na_rpb": nrm(ks[5], (L, NA_HEADS, 2 * NA_WIN_ROWS - 1, 2 * NA_WIN_COLS - 1), jnp.float32) * 0.1,
        "mla_q_norm": 1.0 + 0.02 * nrm(ks[6], (L, MLA_Q_RANK), jnp.float32),
        "mla_kv_norm": 1.0 + 0.02 * nrm(ks[7], (L, MLA_KV_RANK), jnp.float32),
        "mla_w_uq": nrm(ks[8], (L, MLA_Q_RANK, MLA_HEADS * (MLA_NOPE + MLA_ROPE)), jnp.float32) * (MLA_Q_RANK ** -0.5),
        "mla_w_ukv": nrm(ks[9], (L, MLA_KV_RANK, MLA_HEADS * (MLA_NOPE + MLA_V)), jnp.float32) * (MLA_KV_RANK ** -0.5) * jnp.asarray(ukv_scale),
        "swa_sink": nrm(ks[10], (L, SWA_HEADS), jnp.float32) * 0.5,
        "out_norm_g": 1.0 + 0.02 * nrm(ks[11], (L, D_MIX), jnp.float32),
        "w_o": nrm(ks[12], (L, D_MIX, D), jnp.float32) * (D_MIX ** -0.5) * beta,
        "ln1_g": 1.0 + 0.02 * nrm(ks[13], (L, D), jnp.float32),
        "ln1_b": 0.02 * nrm(ks[14], (L, D), jnp.float32),
        "w_gu": nrm(ks[15], (L, D, 2 * D_FF), jnp.float32) * (D ** -0.5),
        "w_down": nrm(ks[16], (L, D_FF, D), jnp.float32) * (D_FF ** -0.5) * beta,
        "ln2_g": 1.0 + 0.02 * nrm(ks[17], (L, D), jnp.float32),
        "ln2_b": 0.02 * nrm(ks[18], (L, D), jnp.float32),
    }


def reference(x, c, w_ada, b_ada, w_in, na_rpb, mla_q_norm, mla_kv_norm, mla_w_uq, mla_w_ukv,
              swa_sink, out_norm_g, w_o, ln1_g, ln1_b, w_gu, w_down, ln2_g, ln2_b):
    B, S, D = x.shape
    pos = jnp.arange(S, dtype=jnp.int32)
    cond = jax.nn.silu(c)
    for l in range(DEPTH):
        mod = cond @ w_ada[l] + b_ada[l]
        sh1, sc1, g1, sh2, sc2, g2 = [m[:, None, :] for m in jnp.split(mod, 6, axis=-1)]

        u = x * (1.0 + sc1) + sh1
        proj = u @ w_in[l]
        pa = proj[..., :IN_A]
        pb = proj[..., IN_A:IN_A + IN_B]
        pc = proj[..., IN_A + IN_B:]

        qa = pa[..., :D_A].reshape(B, S, NA_HEADS, HEAD_DIM)
        ka = pa[..., D_A:2 * D_A].reshape(B, S, NA_HEADS, HEAD_DIM)
        va = pa[..., 2 * D_A:].reshape(B, S, NA_HEADS, HEAD_DIM)
        ya = neighbourhood_attention(qa, ka, va, na_rpb[l])

        cq = pb[..., :MLA_Q_RANK]
        ckv = pb[..., MLA_Q_RANK:MLA_Q_RANK + MLA_KV_RANK]
        kr = pb[..., MLA_Q_RANK + MLA_KV_RANK:]
        yb = latent_attention(cq, ckv, kr, mla_q_norm[l], mla_kv_norm[l], mla_w_uq[l], mla_w_ukv[l], pos)

        dq = SWA_HEADS * HEAD_DIM
        dk = SWA_KV_HEADS * HEAD_DIM
        qc = pc[..., :dq].reshape(B, S, SWA_HEADS, HEAD_DIM)
        kc = pc[..., dq:dq + dk].reshape(B, S, SWA_KV_HEADS, HEAD_DIM)
        vc = pc[..., dq + dk:].reshape(B, S, SWA_KV_HEADS, HEAD_DIM)
        yc = windowed_gqa(qc, kc, vc, swa_sink[l])

        gn = out_norm_g[l]
        y = jnp.concatenate([rms_norm(ya, gn[:D_A]),
                             rms_norm(yb, gn[D_A:D_A + D_B]),
                             rms_norm(yc, gn[D_A + D_B:])], axis=-1)
        x = layer_norm(DEEPNORM_ALPHA * x + (1.0 + g1) * (y @ w_o[l]), ln1_g[l], ln1_b[l])

        u = x * (1.0 + sc2) + sh2
        gu = u @ w_gu[l]
        h = jax.nn.silu(gu[..., :D_FF]) * gu[..., D_FF:]
        x = layer_norm(DEEPNORM_ALPHA * x + (1.0 + g2) * (h @ w_down[l]), ln2_g[l], ln2_b[l])
    return x
```

```python
import numpy as np
from contextlib import ExitStack, contextmanager
import ml_dtypes
import concourse.bass as bass
import concourse.mybir as mybir
from concourse.bass_utils import run_bass_kernel_spmd

F32 = mybir.dt.float32
BF16 = mybir.dt.bfloat16
AF = mybir.ActivationFunctionType
ALU = mybir.AluOpType
NPBF = ml_dtypes.bfloat16

NCORE = 8
S = 8192
D = 2048
TPC = S // NCORE
NQT = TPC // 128
DFF = 5632
NFC = DFF // 128
ALPHA = float(4 ** 0.25)
LN_EPS = 1e-5
RMS_EPS = 1e-6
NEG = -30000.0
NA_KT = 7
NA_HALO = 3
SW_KT = 3
SW_HALO = 1


class Buf:
    __slots__ = ("name", "w", "r", "dsem", "dcnt")

    def __init__(self, name):
        self.name = name
        self.w = None
        self.r = {}
        self.dsem = None
        self.dcnt = 0


class Eng:
    def __init__(self, name, e, sem, is_pe=False):
        self.name = name
        self.e = e
        self.sem = sem
        self.cnt = 0
        self.waited = {}
        self.is_pe = is_pe
        self.pend_r = []
        self.pend_w = []
        self.nwait = 0
        self.nins = 0


class KB:
    def __init__(self, nc, stack):
        self.nc = nc
        self.root = stack
        self.stack = stack
        mk = lambda n: stack.enter_context(nc.semaphore(n))
        self.E = {
            "pe": Eng("pe", nc.tensor, mk("s_pe"), True),
            "act": Eng("act", nc.scalar, mk("s_act")),
            "dve": Eng("dve", nc.vector, mk("s_dve")),
            "pool": Eng("pool", nc.gpsimd, mk("s_pool")),
            "sp": Eng("sp", nc.sync, mk("s_sp")),
        }
        self.nsem = 5
        self.dsems = {}
        self.bufs = {}
        self.uid = 0

    def buf(self, name):
        b = self.bufs.get(name)
        if b is None:
            b = Buf(name)
            self.bufs[name] = b
        return b

    def sem(self, name):
        self.nsem += 1
        return self.root.enter_context(self.nc.semaphore(name))

    def sb(self, name, shape, dt):
        self.uid += 1
        return self.stack.enter_context(self.nc.sbuf_tensor(f"{name}_{self.uid}", list(shape), dt))

    def ps(self, name, shape, dt=F32):
        return self.root.enter_context(self.nc.psum_tensor(name, list(shape), dt))

    @contextmanager
    def scope(self):
        old = self.stack
        with ExitStack() as sub:
            self.stack = sub
            try:
                yield
            finally:
                self.stack = old
            self.barrier()

    def _wait(self, E, deps):
        for sem, val in deps:
            if E.is_pe and sem is E.sem:
                continue
            if E.waited.get(sem.num, 0) >= val:
                continue
            E.e.wait_ge(sem, val)
            E.waited[sem.num] = val
            E.nwait += 1

    @staticmethod
    def _deps(reads, writes):
        deps = []
        for b in reads:
            if b.w is not None:
                deps.append(b.w)
        for b in writes:
            deps.extend(b.r.values())
            if b.w is not None:
                deps.append(b.w)
        return deps

    def op(self, en, fn, reads=(), writes=(), inc=True):
        E = self.E[en]
        self._wait(E, self._deps(reads, writes))
        ins = fn(E.e)
        E.nins += 1
        if not inc:
            assert E.is_pe
            E.pend_r.extend(reads)
            E.pend_w.extend(writes)
            return ins
        E.cnt += 1
        ins.then_inc(E.sem, 1)
        comp = (E.sem, E.cnt)
        rr = list(reads) + E.pend_r
        ww = list(writes) + E.pend_w
        E.pend_r = []
        E.pend_w = []
        for b in rr:
            b.r[E.sem.num] = comp
        for b in ww:
            b.w = comp
            b.r = {}
        return ins

    def dma(self, qn, out, in_, reads=(), writes=(), owner=None):
        E = self.E[qn]
        self._wait(E, self._deps(reads, writes))
        if owner is None:
            owner = (list(writes) + list(reads))[0]
        if owner.dsem is None:
            owner.dsem = self.sem("d_" + owner.name)
        ins = E.e.dma_start(out=out, in_=in_)
        owner.dcnt += 16
        ins.then_inc(owner.dsem, 16)
        comp = (owner.dsem, owner.dcnt)
        self.dsems[owner.dsem.num] = comp
        for b in reads:
            b.r[owner.dsem.num] = comp
        for b in writes:
            b.w = comp
            b.r = {}
        E.nins += 1
        return ins

    def barrier(self):
        assert not self.E["pe"].pend_r and not self.E["pe"].pend_w
        deps = [(X.sem, X.cnt) for X in self.E.values() if X.cnt > 0]
        deps += list(self.dsems.values())
        for X in self.E.values():
            self._wait(X, [d for d in deps if d[0] is not X.sem])

    def finish(self):
        self.barrier()

    def stats(self):
        return {n: (X.nins, X.nwait) for n, X in self.E.items()}


class Ring:
    def __init__(self, kb, name, shape, dt, n):
        self.t = [kb.sb(f"{name}{i}", shape, dt) for i in range(n)]
        self.b = [kb.buf(f"{name}{i}") for i in range(n)]
        self.i = 0

    def next(self):
        i = self.i
        self.i = (i + 1) % len(self.t)
        return self.t[i], self.b[i]


class Prog:
    def __init__(self, kind):
        self.kind = kind
        self.nc = bass.Bass("TRN2", target_bir_lowering=False)
        self.dr = {}

    def din(self, name, shape, dt=F32):
        ap = self.nc.dram_tensor(name, list(shape), dt, kind="ExternalInput").ap()
        self.dr[name] = ap
        return ap

    def dout(self, name, shape, dt=F32):
        ap = self.nc.dram_tensor(name, list(shape), dt, kind="ExternalOutput").ap()
        self.dr[name] = ap
        return ap


def c1024(c, a, b):
    return slice(c * 1024 + a, c * 1024 + b)


def build(kind):
    P = Prog(kind)
    nc = P.nc
    doA = {"A0": 0, "B0A1": 1, "B1": None}[kind]
    doB = {"A0": None, "B0A1": 0, "B1": 1}[kind]
    with ExitStack() as st:
        st.enter_context(nc.allow_low_precision("bf16 matmul operands by design"))
        kb = KB(nc, st)
        P.kb = kb
        xT = kb.sb("xT", [128, 16 * 1024], F32)
        xB = [[kb.buf(f"x{c}_{tb}") for tb in range(2)] for c in range(16)]
        PS = kb.ps("PS", [128, 8 * 512])
        psB = [kb.buf(f"ps{i}") for i in range(8)]
        P.bank = lambda b, n=512, o=0, p=128: PS[0:p, b * 512 + o:b * 512 + o + n]
        ones = kb.sb("ones", [128, 128], BF16)
        onesB = kb.buf("ones")
        kb.op("dve", lambda e: e.memset(ones[:], 1.0), writes=[onesB])
        tmpR = Ring(kb, "tmp", [128, 512], F32, 6)
        tbfR = Ring(kb, "tbf", [128, 512], BF16, 4)
        P.xT, P.xB, P.psB, P.ones, P.onesB, P.tmpR, P.tbfR = xT, xB, psB, ones, onesB, tmpR, tbfR
        P.grot = 0

        xin = P.din("xT_in", [D, TPC])
        xl = kb.buf("xload")
        for c in range(16):
            kb.dma("sp", xT[:, c * 1024:(c + 1) * 1024], xin[c * 128:(c + 1) * 128, :],
                   writes=[xB[c][0], xB[c][1]], owner=xl)
        for c in range(16):
            for tb in range(2):
                xB[c][tb].w = (xl.dsem, xl.dcnt)

        P.mod = {}
        for l in sorted({x for x in (doA, doB) if x is not None}):
            mod = kb.sb(f"mod{l}", [128, 96], F32)
            modp = kb.sb(f"modp{l}", [128, 96], F32)
            P.mod[l] = (mod, modp, kb.buf(f"mod{l}"))
        with nc.named_scope('ada'):
            if kind == "A0":
                emit_ada(P)
            else:
                for l in P.mod:
                    emit_mod_load(P, l)

        if doB is not None:
            emit_B(P, doB)
        if doA is not None:
            with nc.named_scope('phaseA'):
                emit_A(P, doA)
        if kind == "B1" or kind == "B0A1":
            xo = P.dout("xT_out", [D, TPC])
            xs = kb.buf("xstore")
            for c in range(16):
                kb.dma("sp", xo[c * 128:(c + 1) * 128, :], xT[:, c * 1024:(c + 1) * 1024],
                       reads=[xB[c][0], xB[c][1]], owner=xs)
        kb.finish()
        P.stats = kb.stats()
        P.nsem = kb.nsem
    return P


def gbank(P):
    b = P.grot
    P.grot = (b + 1) % 4
    return b


NSHARE = 20


def emit_ada(P):
    kb, nc = P.kb, P.nc
    cin = P.din("cT", [128, 16])
    mso = P.dout("modshare", [128, NSHARE])
    with kb.scope():
        cT = kb.sb("cT", [128, 16], F32)
        cB = kb.buf("cT")
        cond = kb.sb("cond", [128, 16], F32)
        condB = kb.buf("cond")
        kb.dma("sp", cT[:], cin, writes=[cB])
        kb.op("act", lambda e: e.activation(out=cond[:], in_=cT[:], func=AF.Silu), reads=[cB], writes=[condB])
        wr = Ring(kb, "wada", [128, 2048], F32, 4)
        mod, modp, mB = P.mod[0]
        shr = kb.sb("mshare", [128, NSHARE], F32)
        shB = kb.buf("mshare")
        bank = 7
        col0 = 0
        for (wname, bname, ncol, dst, dB, dcol) in (("wada_own", "bada_own", 32, mod, mB, 0), ("wada_shr", "bada_shr", NSHARE, shr, shB, 0)):
            wd = P.din(wname, [ncol * 128, 2048])
            bd = P.din(bname, [128, ncol])
            bt = kb.sb(bname, [128, ncol], F32)
            btB = kb.buf(bname)
            kb.dma("sp", bt[:], bd, writes=[btB])
            for i in range(ncol):
                wt, wB = wr.next()
                kb.dma("sp", wt[:], wd[i * 128:(i + 1) * 128, :], writes=[wB])
                for kc in range(16):
                    kb.op("pe", lambda e: e.matmul(P.bank(bank, 1, col0 + i), wt[:, kc * 128:(kc + 1) * 128],
                                                   cond[:, kc:kc + 1], start=(kc == 0), stop=(kc == 15)),
                          reads=[wB, condB], writes=[P.psB[bank]], inc=(kc == 15))
            kb.op("dve", lambda e: e.tensor_tensor(out=dst[:, 0:ncol], in0=P.bank(bank, ncol, col0), in1=bt[:, 0:ncol], op=ALU.add),
                  reads=[P.psB[bank], btB], writes=[dB])
            col0 += ncol
        kb.op("dve", lambda e: e.tensor_scalar_add(out=modp[:, 0:32], in0=mod[:, 0:32], scalar1=1.0), reads=[mB], writes=[mB])
        kb.dma("sp", mso, shr[:], reads=[shB], owner=shB)


def emit_mod_load(P, l):
    kb = P.kb
    md = P.din(f"modfull{l}", [128, 96])
    mod, modp, mB = P.mod[l]
    kb.dma("sp", mod[:], md, writes=[mB])
    kb.op("dve", lambda e: e.tensor_scalar_add(out=modp[:], in0=mod[:], scalar1=1.0), reads=[mB], writes=[mB])


def mcol(P, l, j, c, plus1=False):
    mod, modp, _ = P.mod[l]
    t = modp if plus1 else mod
    return t[:, j * 16 + c:j * 16 + c + 1]


def emit_rstd(P, psbank, scale, eps, out_t, out_b):
    kb = P.kb
    t, tB = P.tmpR.next()
    kb.op("act", lambda e: e.activation(out=t[:], in_=P.bank(psbank), func=AF.Sqrt, bias=eps, scale=scale),
          reads=[P.psB[psbank]], writes=[tB])
    kb.op("dve", lambda e: e.reciprocal(out=out_t, in_=t[:]), reads=[tB], writes=[out_b])


def emit_A(P, l):
    kb, nc = P.kb, P.nc
    xT, xB = P.xT, P.xB
    mB = P.mod[l][2]
    o = {}
    for name, shp in [("qaT", [512, TPC]), ("kaT", [512, TPC]), ("va", [TPC, 512]),
                      ("qcT", [768, TPC]), ("kcT", [256, TPC]), ("vc", [TPC, 256]),
                      ("qnT", [768, TPC]), ("qpT", [384, TPC]), ("knT", [768, TPC]),
                      ("kpeT", [128, TPC]), ("vb", [TPC, 768])]:
        o[name] = P.dout(f"{name}", shp, BF16)
    win = P.din(f"winfm{l}", [24 * 128, 2048])
    wv = P.din(f"winv{l}", [128, 16 * 768])
    wuqn = P.din(f"wuqn{l}", [128, 4 * 768])
    wuqr = P.din(f"wuqr{l}", [128, 4 * 384])
    wuqs = P.din(f"wuqs{l}", [128, 4 * 384])
    wukk = P.din(f"wukk{l}", [128, 2 * 768])
    wukv = P.din(f"wukv{l}", [128, 2 * 768])
    gq = P.din(f"gq{l}", [128, 4])
    gkv = P.din(f"gkv{l}", [128, 2])
    cosd = P.din("cosT", [128, TPC])
    sind = P.din("ssinT", [128, TPC])
    with kb.scope():
        uT = kb.sb("uT", [128, 16 * 1024], BF16)
        uB = [[kb.buf(f"u{c}_{tb}") for tb in range(2)] for c in range(16)]
        for c in range(16):
            for tb in range(2):
                kb.op("dve", lambda e: e.tensor_scalar(out=uT[:, c1024(c, tb * 512, tb * 512 + 512)],
                                                       in0=xT[:, c1024(c, tb * 512, tb * 512 + 512)],
                                                       scalar1=mcol(P, l, 1, c, True), scalar2=mcol(P, l, 0, c),
                                                       op0=ALU.mult, op1=ALU.add),
                      reads=[xB[c][tb], mB], writes=[uB[c][tb]])
        with kb.scope():
            wvt = kb.sb("wv", [128, 16 * 768], BF16)
            wvB = kb.buf("A_wv")
            kb.dma("pool", wvt[:], wv, writes=[wvB])
            st2R = Ring(kb, "stg2", [128, 768], BF16, 2)
            for tt in range(8):
                tb, off = tt // 4, (tt % 4) * 128 + (tt // 4) * 512
                b1, b2 = gbank(P), gbank(P)
                for (bk, c0, n) in ((b1, 0, 512), (b2, 512, 256)):
                    for kc in range(16):
                        kb.op("pe", lambda e: e.matmul(P.bank(bk, n), uT[:, c1024(kc, off, off + 128)],
                                                       wvt[:, kc * 768 + c0:kc * 768 + c0 + n],
                                                       start=(kc == 0), stop=(kc == 15)),
                              reads=[wvB, uB[kc][tb]], writes=[P.psB[bk]], inc=(kc == 15))
                stg, sB = st2R.next()
                kb.op("act", lambda e: e.activation(out=stg[:, 0:512], in_=P.bank(b1), func=AF.Copy), reads=[P.psB[b1]], writes=[sB])
                kb.op("act", lambda e: e.activation(out=stg[:, 512:768], in_=P.bank(b2, 256), func=AF.Copy), reads=[P.psB[b2]], writes=[sB])
                kb.dma("sp", o["va"][tt * 128:(tt + 1) * 128, :], stg[:, 0:512], reads=[sB], owner=sB)
                kb.dma("sp", o["vc"][tt * 128:(tt + 1) * 128, :], stg[:, 512:768], reads=[sB], owner=sB)
        sm = {}
        for nm, dd, shp in [("gq", gq, [128, 4]), ("gkv", gkv, [128, 2]), ("cos", cosd, [128, TPC]), ("sin", sind, [128, TPC])]:
            t = kb.sb(nm, shp, F32)
            b = kb.buf("A_" + nm)
            kb.dma("sp", t[:], dd, writes=[b])
            sm[nm] = (t, b)
        wsm = {}
        for nm, dd, shp in [("wuqn", wuqn, [128, 4 * 768]), ("wuqr", wuqr, [128, 4 * 384]), ("wuqs", wuqs, [128, 4 * 384]),
                            ("wukk", wukk, [128, 2 * 768]), ("wukv", wukv, [128, 2 * 768])]:
            t = kb.sb(nm, shp, BF16)
            b = kb.buf("A_" + nm)
            kb.dma("pool", t[:], dd, writes=[b])
            wsm[nm] = (t, b)
        cqT = kb.sb("cqT", [128, 4 * 1024], F32)
        ckvT = kb.sb("ckvT", [128, 2 * 1024], F32)
        krT = kb.sb("krT", [128, 2 * 1024], F32)
        cqB = [[kb.buf(f"cq{c}_{tb}") for tb in range(2)] for c in range(4)]
        ckvB = [[kb.buf(f"ckv{c}_{tb}") for tb in range(2)] for c in range(2)]
        krB = [[kb.buf(f"kr{c}_{tb}") for tb in range(2)] for c in range(2)]
        wR = Ring(kb, "win", [128, 2048], BF16, 3)
        stR = Ring(kb, "stg", [128, 1024], BF16, 3)
        dest = ([("qaT", i) for i in range(4)] + [("kaT", i) for i in range(4)] + [("qcT", i) for i in range(6)] +
                [("kcT", i) for i in range(2)] + [("cq", i) for i in range(4)] + [("ckv", i) for i in range(2)] +
                [("kr", 0), ("kr", 1)])
        for oc, (dn, di) in enumerate(dest):
            wt, wB = wR.next()
            kb.dma("pool", wt[:], win[oc * 128:(oc + 1) * 128, :], writes=[wB])
            if dn in o:
                stg, sB = stR.next()
            for tb in range(2):
                bk = gbank(P)
                for kc in range(16):
                    kb.op("pe", lambda e: e.matmul(P.bank(bk), wt[:, kc * 128:(kc + 1) * 128],
                                                   uT[:, c1024(kc, tb * 512, tb * 512 + 512)],
                                                   start=(kc == 0), stop=(kc == 15)),
                          reads=[wB, uB[kc][tb]], writes=[P.psB[bk]], inc=(kc == 15))
                if dn in o:
                    kb.op("act", lambda e: e.activation(out=stg[:, tb * 512:(tb + 1) * 512], in_=P.bank(bk), func=AF.Copy),
                          reads=[P.psB[bk]], writes=[sB])
                else:
                    tt, bb = {"cq": (cqT, cqB), "ckv": (ckvT, ckvB), "kr": (krT, krB)}[dn]
                    kb.op("act", lambda e: e.activation(out=tt[:, c1024(di, tb * 512, tb * 512 + 512)], in_=P.bank(bk), func=AF.Copy),
                          reads=[P.psB[bk]], writes=[bb[di][tb]])
            if dn in o:
                kb.dma("sp", o[dn][di * 128:(di + 1) * 128, :], stg[:], reads=[sB], owner=sB)
        cqn = kb.sb("cqn", [128, 4 * 1024], BF16)
        ckvn = kb.sb("ckvn", [128, 2 * 1024], BF16)
        cqnB = [kb.buf(f"cqn{tb}") for tb in range(2)]
        ckvnB = [kb.buf(f"ckvn{tb}") for tb in range(2)]
        rstd = kb.sb("rstdA", [128, 512], F32)
        rB = kb.buf("rstdA")
        for (src, sB_, nchunk, dst, dB, gname) in ((cqT, cqB, 4, cqn, cqnB, "gq"), (ckvT, ckvB, 2, ckvn, ckvnB, "gkv")):
            gt, gB = sm[gname]
            for tb in range(2):
                bk = gbank(P)
                for c in range(nchunk):
                    t, tB = P.tbfR.next()
                    kb.op("act", lambda e: e.activation(out=t[:], in_=src[:, c1024(c, tb * 512, tb * 512 + 512)], func=AF.Square),
                          reads=[sB_[c][tb]], writes=[tB])
                    kb.op("pe", lambda e: e.matmul(P.bank(bk), P.ones[:], t[:], start=(c == 0), stop=(c == nchunk - 1)),
                          reads=[tB, P.onesB], writes=[P.psB[bk]], inc=True)
                emit_rstd(P, bk, 1.0 / (nchunk * 128), RMS_EPS, rstd[:], rB)
                for c in range(nchunk):
                    kb.op("dve", lambda e: e.scalar_tensor_tensor(out=dst[:, c1024(c, tb * 512, tb * 512 + 512)],
                                                                  in0=src[:, c1024(c, tb * 512, tb * 512 + 512)],
                                                                  scalar=gt[:, c:c + 1], in1=rstd[:],
                                                                  op0=ALU.mult, op1=ALU.mult),
                          reads=[sB_[c][tb], gB, rB], writes=[dB[tb]])
        cost, cosB = sm["cos"]
        sint, sinB = sm["sin"]

        def rope(out_ap, a_ap, s_ap, rows, tb, reads, wbuf):
            t1, t1B = P.tmpR.next()
            t2, t2B = P.tmpR.next()
            kb.op("dve", lambda e: e.tensor_tensor(out=t1[0:rows, :], in0=a_ap, in1=cost[0:rows, tb * 512:(tb + 1) * 512], op=ALU.mult),
                  reads=reads + [cosB], writes=[t1B])
            kb.op("dve", lambda e: e.tensor_tensor(out=t2[0:rows, :], in0=s_ap, in1=sint[0:rows, tb * 512:(tb + 1) * 512], op=ALU.mult),
                  reads=reads + [sinB], writes=[t2B])
            kb.op("dve", lambda e: e.tensor_tensor(out=out_ap, in0=t1[0:rows, :], in1=t2[0:rows, :], op=ALU.add),
                  reads=[t1B, t2B], writes=[wbuf])

        stg, sB = stR.next()
        for tb in range(2):
            rope(stg[:, tb * 512:(tb + 1) * 512], krT[:, c1024(0, tb * 512, tb * 512 + 512)],
                 krT[:, c1024(1, tb * 512, tb * 512 + 512)], 128, tb, [krB[0][tb], krB[1][tb]], sB)
        kb.dma("sp", o["kpeT"], stg[:], reads=[sB], owner=sB)
        for (wname, nk, srcn, srcB, oname) in (("wuqn", 4, cqn, cqnB, "qnT"), ("wukk", 2, ckvn, ckvnB, "knT")):
            wt, wB = wsm[wname]
            for h in range(6):
                stg, sB = stR.next()
                for tb in range(2):
                    bk = gbank(P)
                    for kc in range(nk):
                        kb.op("pe", lambda e: e.matmul(P.bank(bk), wt[:, kc * 768 + h * 128:kc * 768 + (h + 1) * 128],
                                                       srcn[:, c1024(kc, tb * 512, tb * 512 + 512)],
                                                       start=(kc == 0), stop=(kc == nk - 1)),
                              reads=[wB, srcB[tb]], writes=[P.psB[bk]], inc=(kc == nk - 1))
                    kb.op("act", lambda e: e.activation(out=stg[:, tb * 512:(tb + 1) * 512], in_=P.bank(bk), func=AF.Copy),
                          reads=[P.psB[bk]], writes=[sB])
                kb.dma("sp", o[oname][h * 128:(h + 1) * 128, :], stg[:], reads=[sB], owner=sB)
        wr_, wrB = wsm["wuqr"]
        ws_, wsB = wsm["wuqs"]
        for h in range(6):
            stg, sB = stR.next()
            for tb in range(2):
                b1, b2 = gbank(P), gbank(P)
                for (bk, wt, wB) in ((b1, wr_, wrB), (b2, ws_, wsB)):
                    for kc in range(4):
                        kb.op("pe", lambda e: e.matmul(P.bank(bk, 512, 0, 64), wt[:, kc * 384 + h * 64:kc * 384 + (h + 1) * 64],
                                                       cqn[:, c1024(kc, tb * 512, tb * 512 + 512)],
                                                       start=(kc == 0), stop=(kc == 3)),
                              reads=[wB, cqnB[tb]], writes=[P.psB[bk]], inc=(kc == 3))
                t3, t3B = P.tmpR.next()
                kb.op("act", lambda e: e.activation(out=t3[0:64, :], in_=P.bank(b2, 512, 0, 64), func=AF.Copy), reads=[P.psB[b2]], writes=[t3B])
                rope(stg[0:64, tb * 512:(tb + 1) * 512], P.bank(b1, 512, 0, 64), t3[0:64, :], 64, tb, [P.psB[b1], t3B], sB)
            kb.dma("sp", o["qpT"][h * 64:(h + 1) * 64, :], stg[0:64, :], reads=[sB], owner=sB)
        st2R = Ring(kb, "stg3", [128, 768], BF16, 2)
        wt, wB = wsm["wukv"]
        for tt in range(8):
            tb, off = tt // 4, (tt % 4) * 128 + (tt // 4) * 512
            b1, b2 = gbank(P), gbank(P)
            for (bk, c0, n) in ((b1, 0, 512), (b2, 512, 256)):
                for kc in range(2):
                    kb.op("pe", lambda e: e.matmul(P.bank(bk, n), ckvn[:, c1024(kc, off, off + 128)],
                                                   wt[:, kc * 768 + c0:kc * 768 + c0 + n],
                                                   start=(kc == 0), stop=(kc == 1)),
                          reads=[wB, ckvnB[tb]], writes=[P.psB[bk]], inc=(kc == 1))
            stg, sB = st2R.next()
            kb.op("act", lambda e: e.activation(out=stg[:, 0:512], in_=P.bank(b1), func=AF.Copy), reads=[P.psB[b1]], writes=[sB])
            kb.op("act", lambda e: e.activation(out=stg[:, 512:768], in_=P.bank(b2, 256), func=AF.Copy), reads=[P.psB[b2]], writes=[sB])
            kb.dma("sp", o["vb"][tt * 128:(tt + 1) * 128, :], stg[:], reads=[sB], owner=sB)


def emit_local_attn(P, tb, yT, yB, spec):
    kb = P.kb
    nkt = spec["nkt"]
    nwin = 4 + nkt - 1
    W = nkt * 128
    kR = Ring(kb, spec["n"] + "k", [128, nwin * 128], BF16, 2)
    vR = Ring(kb, spec["n"] + "v", [128, nwin * 128], BF16, 2)
    qR = Ring(kb, spec["n"] + "q", [128, 512], BF16, 2)
    bR = Ring(kb, spec["n"] + "b", [128, W], F32, 2)
    sR = Ring(kb, spec["n"] + "s", [128, W], F32, 2)
    pR = Ring(kb, spec["n"] + "p", [128, W], BF16, 2)
    rR = Ring(kb, spec["n"] + "r", [128, 128], F32, 2)
    prev_kv = None
    pair = 0
    for h in range(spec["nh"]):
        kvh = spec["kvh"](h)
        if kvh != prev_kv:
            kt_, kB_ = kR.next()
            vt_, vB_ = vR.next()
            kb.dma("sp", kt_[:], spec["kd"][kvh * 128:(kvh + 1) * 128, tb * 512:tb * 512 + nwin * 128], writes=[kB_])
            kb.dma("sp", vt_[:], spec["vd"][kvh * 128:(kvh + 1) * 128, tb * 512:tb * 512 + nwin * 128], writes=[vB_])
            prev_kv = kvh
        qt_, qB_ = qR.next()
        kb.dma("sp", qt_[:], spec["qd"][h * 128:(h + 1) * 128, tb * 512:(tb + 1) * 512], writes=[qB_])
        for qt in range(4):
            j = tb * 4 + qt
            bt_, bB_ = bR.next()
            row = (j * spec["nh"] + h) * 128
            kb.dma("sp", bt_[:], spec["bd"][row:row + 128, :], writes=[bB_])
            base = (pair % 2) * 2
            pair += 1
            for kt in range(nkt):
                bk = base + (kt // 4)
                kb.op("pe", lambda e: e.matmul(P.bank(bk, 128, (kt % 4) * 128),
                                               kt_[:, (qt + kt) * 128:(qt + kt + 1) * 128],
                                               qt_[:, qt * 128:(qt + 1) * 128], start=True, stop=True),
                      reads=[kB_, qB_], writes=[P.psB[bk]], inc=(kt == nkt - 1 or kt == 3))
            st_, sB_ = sR.next()
            n0 = min(W, 512)
            kb.op("dve", lambda e: e.scalar_tensor_tensor(out=st_[:, 0:n0], in0=P.bank(base, n0), scalar=spec["scale"],
                                                          in1=bt_[:, 0:n0], op0=ALU.mult, op1=ALU.add),
                  reads=[P.psB[base], bB_], writes=[sB_])
            if W > 512:
                kb.op("dve", lambda e: e.scalar_tensor_tensor(out=st_[:, 512:W], in0=P.bank(base + 1, W - 512), scalar=spec["scale"],
                                                              in1=bt_[:, 512:W], op0=ALU.mult, op1=ALU.add),
                      reads=[P.psB[base + 1], bB_], writes=[sB_])
            pt_, pB_ = pR.next()
            kb.op("act", lambda e: e.activation(out=pt_[:], in_=st_[:], func=AF.Exp), reads=[sB_], writes=[pB_])
            ob, db = 4 + (pair % 2), 6 + (pair % 2)
            for kt in range(nkt):
                kb.op("pe", lambda e: e.matmul(P.bank(ob, 128), vt_[:, (qt + kt) * 128:(qt + kt + 1) * 128],
                                               pt_[:, kt * 128:(kt + 1) * 128], start=(kt == 0), stop=(kt == nkt - 1)),
                      reads=[vB_, pB_], writes=[P.psB[ob]], inc=(kt == nkt - 1))
            for kt in range(nkt):
                kb.op("pe", lambda e: e.matmul(P.bank(db, 128), P.ones[:], pt_[:, kt * 128:(kt + 1) * 128],
                                               start=(kt == 0), stop=(kt == nkt - 1)),
                      reads=[P.onesB, pB_], writes=[P.psB[db]], inc=(kt == nkt - 1))
            rt_, rB_ = rR.next()
            if spec["esink"] is not None:
                es, esB = spec["esink"]
                kb.op("dve", lambda e: e.tensor_scalar_add(out=rt_[:], in0=P.bank(db, 128), scalar1=es[:, h:h + 1]),
                      reads=[P.psB[db], esB], writes=[rB_])
                kb.op("dve", lambda e: e.reciprocal(out=rt_[:], in_=rt_[:]), reads=[rB_], writes=[rB_])
            else:
                kb.op("dve", lambda e: e.reciprocal(out=rt_[:], in_=P.bank(db, 128)), reads=[P.psB[db]], writes=[rB_])
            ch = spec["ychunk0"] + h
            kb.op("dve", lambda e: e.tensor_tensor(out=yT[:, ch * 512 + qt * 128:ch * 512 + (qt + 1) * 128],
                                                   in0=P.bank(ob, 128), in1=rt_[:], op=ALU.mult),
                  reads=[P.psB[ob], rB_], writes=[yB[ch]])


def emit_mla(P, tb, yT, yB, dd, kpe, kpeB):
    kb = P.kb
    scale = float(192 ** -0.5)
    knR = Ring(kb, "mkn", [128, 4096], BF16, 2)
    vR = Ring(kb, "mv", [128, 4096], BF16, 2)
    qnR = Ring(kb, "mqn", [128, 512], BF16, 2)
    qpR = Ring(kb, "mqp", [64, 512], BF16, 2)
    pR = Ring(kb, "mp", [128, 512], BF16, 4)
    rt_ = kb.sb("mr", [128, 512], F32)
    rB_ = kb.buf("mr")
    srot = 0
    for h in range(6):
        qn, qnB = qnR.next()
        qp, qpB = qpR.next()
        kb.dma("sp", qn[:], dd["qnT"][h * 128:(h + 1) * 128, tb * 512:(tb + 1) * 512], writes=[qnB])
        kb.dma("sp", qp[:], dd["qpT"][h * 64:(h + 1) * 64, tb * 512:(tb + 1) * 512], writes=[qpB])
        ob, db = 4 + (h % 2), 6 + (h % 2)
        halves = []
        pending = None

        def pv(item, first, last):
            pt, pB, kt, vt, vB = item
            kb.op("pe", lambda e: e.matmul(P.bank(ob), vt[:, (kt % 32) * 128:(kt % 32 + 1) * 128], pt[:], start=first, stop=last),
                  reads=[vB, pB], writes=[P.psB[ob]], inc=False)
            kb.op("pe", lambda e: e.matmul(P.bank(db), P.ones[:], pt[:], start=first, stop=last),
                  reads=[P.onesB, pB], writes=[P.psB[db]], inc=True)

        for half in range(2):
            kn, knB = knR.next()
            vt, vB = vR.next()
            kb.dma("sp", kn[:], dd["knT_all"][h * 128:(h + 1) * 128, half * 4096:(half + 1) * 4096], writes=[knB])
            kb.dma("sp", vt[:], dd["vb_all"][h * 128:(h + 1) * 128, half * 4096:(half + 1) * 4096], writes=[vB])
            for k32 in range(32):
                kt = half * 32 + k32
                bk = srot % 3
                srot += 1
                kb.op("pe", lambda e: e.matmul(P.bank(bk), kn[:, k32 * 128:(k32 + 1) * 128], qn[:], start=True, stop=False),
                      reads=[knB, qnB], writes=[P.psB[bk]], inc=False)
                kb.op("pe", lambda e: e.matmul(P.bank(bk), kpe[0:64, kt * 128:(kt + 1) * 128], qp[:], start=False, stop=True),
                      reads=[kpeB, qpB], writes=[P.psB[bk]], inc=True)
                pt, pB = pR.next()
                kb.op("act", lambda e: e.activation(out=pt[:], in_=P.bank(bk), func=AF.Exp, scale=scale),
                      reads=[P.psB[bk]], writes=[pB])
                if pending is not None:
                    pv(pending, pending[2] == 0, False)
                pending = (pt, pB, kt, vt, vB)
        pv(pending, False, True)
        kb.op("dve", lambda e: e.reciprocal(out=rt_[:], in_=P.bank(db)), reads=[P.psB[db]], writes=[rB_])
        ch = 4 + h
        kb.op("dve", lambda e: e.tensor_tensor(out=yT[:, ch * 512:(ch + 1) * 512], in0=P.bank(ob), in1=rt_[:], op=ALU.mult),
              reads=[P.psB[ob], rB_], writes=[yB[ch]])


def emit_ln(P, l, tb, gt, bt, gbB):
    kb = P.kb
    xT, xB = P.xT, P.xB
    sb_, qb_ = 4, 6
    for c in range(16):
        t1, t1B = P.tbfR.next()
        t2, t2B = P.tbfR.next()
        xs = xT[:, c1024(c, tb * 512, tb * 512 + 512)]
        kb.op("act", lambda e: e.activation(out=t1[:], in_=xs, func=AF.Copy), reads=[xB[c][tb]], writes=[t1B])
        kb.op("act", lambda e: e.activation(out=t2[:], in_=xs, func=AF.Square), reads=[xB[c][tb]], writes=[t2B])
        kb.op("pe", lambda e: e.matmul(P.bank(sb_), P.ones[:], t1[:], start=(c == 0), stop=(c == 15)),
              reads=[P.onesB, t1B], writes=[P.psB[sb_]], inc=True)
        kb.op("pe", lambda e: e.matmul(P.bank(qb_), P.ones[:], t2[:], start=(c == 0), stop=(c == 15)),
              reads=[P.onesB, t2B], writes=[P.psB[qb_]], inc=True)
    mean, mB = P.tmpR.next()
    msq, qB = P.tmpR.next()
    var, vB = P.tmpR.next()
    rstd, rB = P.tmpR.next()
    kb.op("act", lambda e: e.mul(out=mean[:], in_=P.bank(sb_), mul=1.0 / D), reads=[P.psB[sb_]], writes=[mB])
    kb.op("dve", lambda e: e.tensor_tensor(out=msq[:], in0=mean[:], in1=mean[:], op=ALU.mult), reads=[mB], writes=[qB])
    kb.op("dve", lambda e: e.scalar_tensor_tensor(out=var[:], in0=P.bank(qb_), scalar=1.0 / D, in1=msq[:],
                                                  op0=ALU.mult, op1=ALU.subtract),
          reads=[P.psB[qb_], qB], writes=[vB])
    t, tB = P.tmpR.next()
    kb.op("act", lambda e: e.activation(out=t[:], in_=var[:], func=AF.Sqrt, bias=LN_EPS, scale=1.0), reads=[vB], writes=[tB])
    kb.op("dve", lambda e: e.reciprocal(out=rstd[:], in_=t[:]), reads=[tB], writes=[rB])
    for c in range(16):
        xs = xT[:, c1024(c, tb * 512, tb * 512 + 512)]
        kb.op("dve", lambda e: e.tensor_tensor(out=xs, in0=xs, in1=mean[:], op=ALU.subtract), reads=[xB[c][tb], mB], writes=[xB[c][tb]])
        kb.op("dve", lambda e: e.tensor_tensor(out=xs, in0=xs, in1=rstd[:], op=ALU.mult), reads=[xB[c][tb], rB], writes=[xB[c][tb]])
        kb.op("act", lambda e: e.activation(out=xs, in_=xs, func=AF.Identity, scale=gt[:, c:c + 1], bias=bt[:, c:c + 1]),
              reads=[xB[c][tb], gbB], writes=[xB[c][tb]])


def emit_resid(P, l, jg, oc, tb, bk):
    kb = P.kb
    xs = P.xT[:, c1024(oc, tb * 512, tb * 512 + 512)]
    t, tB = P.tmpR.next()
    kb.op("act", lambda e: e.mul(out=t[:], in_=xs, mul=ALPHA), reads=[P.xB[oc][tb]], writes=[tB])
    kb.op("dve", lambda e: e.scalar_tensor_tensor(out=xs, in0=P.bank(bk), scalar=mcol(P, l, jg, oc, True), in1=t[:],
                                                  op0=ALU.mult, op1=ALU.add),
          reads=[P.psB[bk], tB, P.mod[l][2]], writes=[P.xB[oc][tb]])


def emit_B(P, l):
    kb, nc = P.kb, P.nc
    xT, xB = P.xT, P.xB
    mB = P.mod[l][2]
    dd = {}
    for name, shp in [("qaT", [512, TPC]), ("qcT", [768, TPC]), ("qnT", [768, TPC]), ("qpT", [384, TPC]),
                      ("kaT_win", [512, 14 * 128]), ("va_win", [512, 14 * 128]),
                      ("kcT_win", [256, 10 * 128]), ("vc_win", [256, 10 * 128]),
                      ("knT_all", [768, S]), ("kpeT_all", [64, S]), ("vb_all", [768, S])]:
        dd[name] = P.din("B_" + name, shp, BF16)
    dd["nab"] = P.din("B_nab", [NQT * 4 * 128, NA_KT * 128])
    dd["swb"] = P.din("B_swb", [NQT * 6 * 128, SW_KT * 128])
    sinkd = P.din("B_sink", [128, 6])
    vecs = {}
    with kb.scope():
        for nm in ["gn", "ln1g", "ln1b", "ln2g", "ln2b"]:
            d_ = P.din(f"B_{nm}", [128, 16])
            t = kb.sb(nm, [128, 16], F32)
            b = kb.buf("B_" + nm)
            kb.dma("sp", t[:], d_, writes=[b])
            vecs[nm] = (t, b)
        sk = kb.sb("sink", [128, 6], F32)
        skB = kb.buf("B_sink")
        kb.dma("sp", sk[:], sinkd, writes=[skB])
        es = kb.sb("esink", [128, 6], F32)
        esB = kb.buf("B_esink")
        kb.op("act", lambda e: e.activation(out=es[:], in_=sk[:], func=AF.Exp), reads=[skB], writes=[esB])
        wo = P.din(f"wo{l}", [16 * 128, 2048])
        wgu = P.din(f"wgu{l}", [NFC * 128, 2 * 2048])
        wdn = P.din(f"wdn{l}", [2 * 16 * 128, (NFC // 2) * 128])
        ynB = [kb.buf(f"yn{c}") for c in range(16)]
        for tb in range(2):
          with kb.scope():
            ynT = kb.sb("ynT", [128, 16 * 512], BF16)
            with kb.scope():
                yT = kb.sb("yT", [128, 16 * 512], F32)
                yB = [kb.buf(f"y{c}") for c in range(16)]
                with kb.scope(), nc.named_scope('local_attn'):
                    emit_local_attn(P, tb, yT, yB, dict(n="na", nh=4, kvh=lambda h: h, nkt=NA_KT, qd=dd["qaT"], kd=dd["kaT_win"],
                                                        vd=dd["va_win"], bd=dd["nab"], scale=float(128 ** -0.5), ychunk0=0, esink=None))
                    emit_local_attn(P, tb, yT, yB, dict(n="sw", nh=6, kvh=lambda h: h // 3, nkt=SW_KT, qd=dd["qcT"], kd=dd["kcT_win"],
                                                        vd=dd["vc_win"], bd=dd["swb"], scale=float(128 ** -0.5), ychunk0=10, esink=(es, esB)))
                with kb.scope(), nc.named_scope('mla'):
                    kpe = kb.sb("kpe", [64, S], BF16)
                    kpeB = kb.buf("kpe")
                    kb.dma("sp", kpe[:], dd["kpeT_all"], writes=[kpeB])
                    emit_mla(P, tb, yT, yB, dd, kpe, kpeB)
                gt, gB = vecs["gn"]
                rstd = kb.sb("rstdB", [128, 512], F32)
                rB = kb.buf("rstdB")
                for (c0, c1) in ((0, 4), (4, 10), (10, 16)):
                    bk = gbank(P)
                    for c in range(c0, c1):
                        t, tB = P.tbfR.next()
                        kb.op("act", lambda e: e.activation(out=t[:], in_=yT[:, c * 512:(c + 1) * 512], func=AF.Square),
                              reads=[yB[c]], writes=[tB])
                        kb.op("pe", lambda e: e.matmul(P.bank(bk), P.ones[:], t[:], start=(c == c0), stop=(c == c1 - 1)),
                              reads=[P.onesB, tB], writes=[P.psB[bk]], inc=True)
                    emit_rstd(P, bk, 1.0 / ((c1 - c0) * 128), RMS_EPS, rstd[:], rB)
                    for c in range(c0, c1):
                        kb.op("dve", lambda e: e.scalar_tensor_tensor(out=ynT[:, c * 512:(c + 1) * 512], in0=yT[:, c * 512:(c + 1) * 512],
                                                                      scalar=gt[:, c:c + 1], in1=rstd[:], op0=ALU.mult, op1=ALU.mult),
                              reads=[yB[c], gB, rB], writes=[ynB[c]])
            with kb.scope(), nc.named_scope('oproj_ln1'):
                wR = Ring(kb, "wo", [128, 2048], BF16, 3)
                for oc in range(16):
                    wt, wB = wR.next()
                    kb.dma("pool", wt[:], wo[oc * 128:(oc + 1) * 128, :], writes=[wB])
                    bk = gbank(P)
                    for kc in range(16):
                        kb.op("pe", lambda e: e.matmul(P.bank(bk), wt[:, kc * 128:(kc + 1) * 128], ynT[:, kc * 512:(kc + 1) * 512],
                                                       start=(kc == 0), stop=(kc == 15)),
                              reads=[wB, ynB[kc]], writes=[P.psB[bk]], inc=(kc == 15))
                    emit_resid(P, l, 2, oc, tb, bk)
                emit_ln(P, l, tb, vecs["ln1g"][0], vecs["ln1b"][0], vecs["ln1g"][1])
        with kb.scope(), nc.named_scope('ffn'):
            NH = NFC // 2
            u2 = kb.sb("u2T", [128, 16 * 1024], BF16)
            u2B = [[kb.buf(f"u2_{c}_{tb}") for tb in range(2)] for c in range(16)]
            hT = kb.sb("hT", [128, NH * 1024], BF16)
            hB = [[kb.buf(f"h{c}_{tb}") for tb in range(2)] for c in range(NH)]
            for c in range(16):
                for tb in range(2):
                    kb.op("dve", lambda e: e.tensor_scalar(out=u2[:, c1024(c, tb * 512, tb * 512 + 512)],
                                                           in0=xT[:, c1024(c, tb * 512, tb * 512 + 512)],
                                                           scalar1=mcol(P, l, 4, c, True), scalar2=mcol(P, l, 3, c),
                                                           op0=ALU.mult, op1=ALU.add),
                          reads=[xB[c][tb], mB], writes=[u2B[c][tb]])
            wR = Ring(kb, "wgu", [128, 4096], BF16, 3)
            wR2 = Ring(kb, "wdn", [128, NH * 128], BF16, 3)
            for half in range(2):
                for f in range(NH):
                    fc = half * NH + f
                    wt, wB = wR.next()
                    kb.dma("pool", wt[:], wgu[fc * 128:(fc + 1) * 128, :], writes=[wB])
                    for tb in range(2):
                        bg, bu = gbank(P), gbank(P)
                        for (bk, off) in ((bg, 0), (bu, 2048)):
                            for kc in range(16):
                                kb.op("pe", lambda e: e.matmul(P.bank(bk), wt[:, off + kc * 128:off + (kc + 1) * 128],
                                                               u2[:, c1024(kc, tb * 512, tb * 512 + 512)],
                                                               start=(kc == 0), stop=(kc == 15)),
                                      reads=[wB, u2B[kc][tb]], writes=[P.psB[bk]], inc=(kc == 15))
                        t, tB = P.tmpR.next()
                        kb.op("act", lambda e: e.activation(out=t[:], in_=P.bank(bg), func=AF.Silu), reads=[P.psB[bg]], writes=[tB])
                        kb.op("dve", lambda e: e.tensor_tensor(out=hT[:, c1024(f, tb * 512, tb * 512 + 512)], in0=P.bank(bu), in1=t[:], op=ALU.mult),
                              reads=[P.psB[bu], tB], writes=[hB[f][tb]])
                for oc in range(16):
                    wt, wB = wR2.next()
                    row = (half * 16 + oc) * 128
                    kb.dma("pool", wt[:], wdn[row:row + 128, :], writes=[wB])
                    for tb in range(2):
                        bk = gbank(P)
                        for f in range(NH):
                            kb.op("pe", lambda e: e.matmul(P.bank(bk), wt[:, f * 128:(f + 1) * 128], hT[:, c1024(f, tb * 512, tb * 512 + 512)],
                                                           start=(f == 0), stop=(f == NH - 1)),
                                  reads=[wB, hB[f][tb]], writes=[P.psB[bk]], inc=(f == NH - 1))
                        if half == 0:
                            emit_resid(P, l, 5, oc, tb, bk)
                        else:
                            xs = xT[:, c1024(oc, tb * 512, tb * 512 + 512)]
                            kb.op("dve", lambda e: e.scalar_tensor_tensor(out=xs, in0=P.bank(bk), scalar=mcol(P, l, 5, oc, True), in1=xs,
                                                                          op0=ALU.mult, op1=ALU.add),
                                  reads=[P.psB[bk], xB[oc][tb], mB], writes=[xB[oc][tb]])
            for tb in range(2):
                emit_ln(P, l, tb, vecs["ln2g"][0], vecs["ln2b"][0], vecs["ln2g"][1])


def fm_vec(v):
    return np.ascontiguousarray(v.reshape(-1, 128).T)


def lhs_layout(w, cols=None):
    K, N = w.shape
    nk, no = K // 128, N // 128
    return np.ascontiguousarray(w.reshape(nk, 128, no, 128).transpose(2, 1, 0, 3).reshape(no * 128, nk * 128))


def rhs_layout(w):
    K, N = w.shape
    nk = K // 128
    return np.ascontiguousarray(w.reshape(nk, 128, N).transpose(1, 0, 2).reshape(128, nk * N))


def prep_layer(inp, l):
    w = {}
    w_in = inp["w_in"][l]
    swap64 = np.concatenate([np.arange(32, 64), np.arange(0, 32)])
    kr_cols = np.arange(2304, 2368)
    cols = np.concatenate([np.arange(0, 512), np.arange(512, 1024), np.arange(2368, 3136), np.arange(3136, 3392),
                           np.arange(1536, 2048), np.arange(2048, 2304), kr_cols, kr_cols, kr_cols[swap64], kr_cols[swap64]])
    w[f"winfm{l}"] = lhs_layout(w_in[:, cols])
    vcols = np.concatenate([np.arange(1024, 1536), np.arange(3392, 3648)])
    w[f"winv{l}"] = rhs_layout(w_in[:, vcols])
    uq = inp["mla_w_uq"][l].reshape(512, 6, 192)
    w[f"wuqn{l}"] = rhs_layout(np.ascontiguousarray(uq[:, :, :128]).reshape(512, 768))
    w[f"wuqr{l}"] = rhs_layout(np.ascontiguousarray(uq[:, :, 128:]).reshape(512, 384))
    w[f"wuqs{l}"] = rhs_layout(np.ascontiguousarray(uq[:, :, 128:][:, :, swap64]).reshape(512, 384))
    ukv = inp["mla_w_ukv"][l].reshape(256, 6, 256)
    w[f"wukk{l}"] = rhs_layout(np.ascontiguousarray(ukv[:, :, :128]).reshape(256, 768))
    w[f"wukv{l}"] = rhs_layout(np.ascontiguousarray(ukv[:, :, 128:]).reshape(256, 768))
    w[f"gq{l}"] = fm_vec(inp["mla_q_norm"][l])
    w[f"gkv{l}"] = fm_vec(inp["mla_kv_norm"][l])
    return w


def prep_layer_B(inp, l):
    w = {}
    w[f"wo{l}"] = lhs_layout(inp["w_o"][l])
    wg = lhs_layout(inp["w_gu"][l][:, :DFF]).reshape(NFC, 128, 2048)
    wu = lhs_layout(inp["w_gu"][l][:, DFF:]).reshape(NFC, 128, 2048)
    w[f"wgu{l}"] = np.ascontiguousarray(np.concatenate([wg, wu], axis=2).reshape(NFC * 128, 4096))
    hd = DFF // 2
    w[f"wdn{l}"] = np.ascontiguousarray(np.concatenate([lhs_layout(inp["w_down"][l][0:hd]), lhs_layout(inp["w_down"][l][hd:])], axis=0))
    for nm, key in [("gn", "out_norm_g"), ("ln1g", "ln1_g"), ("ln1b", "ln1_b"), ("ln2g", "ln2_g"), ("ln2b", "ln2_b")]:
        w["B_" + nm] = fm_vec(inp[key][l])
    w["B_sink"] = np.ascontiguousarray(np.broadcast_to(inp["swa_sink"][l][None, :], (128, 6)))
    return w


def ada_chunks(inp, chunks):
    w = np.concatenate([inp["w_ada"][l][:, c * 128:(c + 1) * 128] for l, c in chunks], axis=1)
    b = np.stack([inp["b_ada"][l][c * 128:(c + 1) * 128] for l, c in chunks], axis=1)
    return lhs_layout(w), np.ascontiguousarray(b)


ADA_REST = [(0, c) for c in range(32, 96)] + [(1, c) for c in range(96)]


def gather_mod(r1):
    allc = np.concatenate([r["modshare"] for r in r1], axis=1)
    m0 = np.zeros((128, 96), np.float32)
    m0[:, 32:96] = allc[:, 0:64]
    m1 = np.ascontiguousarray(allc[:, 64:160])
    return m0, m1


def rope_tables(core):
    half = 32
    inv = (10000.0 ** (-np.arange(half, dtype=np.float32) / half)).astype(np.float32)
    pos = np.arange(core * TPC, (core + 1) * TPC, dtype=np.float32)
    ang = (pos[:, None] * inv[None, :]).astype(np.float32)
    cos = np.cos(ang).astype(np.float32).T
    sin = np.sin(ang).astype(np.float32).T
    c64 = np.concatenate([cos, cos], 0)
    s64 = np.concatenate([-sin, sin], 0)
    return (np.ascontiguousarray(np.concatenate([c64, c64], 0)), np.ascontiguousarray(np.concatenate([s64, s64], 0)))


def na_bias_tables(rpb, core):
    out = np.full((NQT, 4, 128, NA_KT, 128), NEG, dtype=np.float32)
    qi = np.arange(128)
    for j in range(NQT):
        r_even = core * 16 + 2 * j
        qrow = r_even + qi // 64
        qcol = qi % 64
        r0 = np.clip(qrow - 4, 0, 120)
        c0 = np.clip(qcol - 8, 0, 48)
        for kt in range(NA_KT):
            ktile = core * 8 + j - NA_HALO + kt
            if ktile < 0 or ktile >= 64:
                continue
            ki = np.arange(128)
            krow = 2 * ktile + ki // 64
            kcol = ki % 64
            rin = (krow[:, None] >= r0[None, :]) & (krow[:, None] < r0[None, :] + 8)
            cin = (kcol[:, None] >= c0[None, :]) & (kcol[:, None] < c0[None, :] + 16)
            roff = np.clip(krow[:, None] - qrow[None, :] + 7, 0, 14)
            coff = np.clip(kcol[:, None] - qcol[None, :], -15, 15) + 15
            m = rin & cin
            for h in range(4):
                vals = rpb[h][roff, coff]
                out[j, h, :, kt, :] = np.where(m, vals, np.float32(NEG))
    return out.reshape(NQT * 4 * 128, NA_KT * 128)


def sw_bias_tables(core):
    out = np.full((NQT, 6, 128, SW_KT, 128), NEG, dtype=np.float32)
    slopes = np.array([2.0 ** (-8.0 * (i + 1) / 6) for i in range(6)], dtype=np.float32)
    qi = np.arange(128)
    for j in range(NQT):
        qpos = (core * 8 + j) * 128 + qi
        for kt in range(SW_KT):
            ktile = core * 8 + j - 1 + kt
            if ktile < 0 or ktile >= 64:
                continue
            kpos = ktile * 128 + np.arange(128)
            dist = np.abs(qpos[None, :] - kpos[:, None])
            m = dist <= 128
            for h in range(6):
                out[j, h, :, kt, :] = np.where(m, -slopes[h] * dist.astype(np.float32), np.float32(NEG))
    return out.reshape(NQT * 6 * 128, SW_KT * 128)


def v_layout(v_tok, nh):
    T = v_tok.shape[0]
    return np.ascontiguousarray(v_tok.reshape(T // 128, 128, nh, 128).transpose(2, 1, 0, 3).reshape(nh * 128, T))


def exchange(outs):
    cat = lambda k, ax: np.concatenate([o[k] for o in outs], axis=ax)
    kaT = cat("kaT", 1)
    kcT = cat("kcT", 1)
    va = v_layout(cat("va", 0), 4)
    vc = v_layout(cat("vc", 0), 2)
    knT = cat("knT", 1)
    kpeT = np.ascontiguousarray(cat("kpeT", 1)[0:64])
    vb = v_layout(cat("vb", 0), 6)

    def win(a, core, halo, n):
        t0 = core * 8 - halo
        res = np.zeros((a.shape[0], n * 128), dtype=a.dtype)
        lo, hi = max(t0, 0), min(t0 + n, 64)
        res[:, (lo - t0) * 128:(hi - t0) * 128] = a[:, lo * 128:hi * 128]
        return res

    res = []
    for i in range(NCORE):
        d = {"B_qaT": outs[i]["qaT"], "B_qcT": outs[i]["qcT"], "B_qnT": outs[i]["qnT"], "B_qpT": outs[i]["qpT"],
             "B_kaT_win": win(kaT, i, NA_HALO, 14), "B_va_win": win(va, i, NA_HALO, 14),
             "B_kcT_win": win(kcT, i, SW_HALO, 10), "B_vc_win": win(vc, i, SW_HALO, 10),
             "B_knT_all": knT, "B_kpeT_all": kpeT, "B_vb_all": vb}
        res.append(d)
    return res


_PROGS = {}


def get_prog(kind):
    if kind not in _PROGS:
        _PROGS[kind] = build(kind)
    return _PROGS[kind]


TRACE = False
LAST = {}


def run(kind, in_maps):
    P = get_prog(kind)
    maps = [{k: np.ascontiguousarray(m[k]) for k in P.dr if k in m} for m in in_maps]
    if TRACE:
        res = run_bass_kernel_spmd(P.nc, maps, core_ids=list(range(NCORE)), trace=True)
        LAST[kind] = res
    else:
        res = run_bass_kernel_spmd(P.nc, maps, core_ids=list(range(NCORE)))
    return res.results


def kernel(**inp):
    inp = {k: np.asarray(v) for k, v in inp.items()}
    x = inp["x"][0]
    xT = np.ascontiguousarray(x.T)
    cT = fm_vec(inp["c"][0])
    ropes = [rope_tables(i) for i in range(NCORE)]
    swb = [sw_bias_tables(i) for i in range(NCORE)]
    com = {"cT": cT}
    com["wada_own"], com["bada_own"] = ada_chunks(inp, [(0, c) for c in range(32)])
    com.update(prep_layer(inp, 0))
    maps = []
    for i in range(NCORE):
        m = dict(com)
        m["xT_in"] = xT[:, i * TPC:(i + 1) * TPC]
        m["cosT"], m["ssinT"] = ropes[i]
        m["wada_shr"], m["bada_shr"] = ada_chunks(inp, ADA_REST[i * NSHARE:(i + 1) * NSHARE])
        maps.append(m)
    r1 = run("A0", maps)
    modfull0, modfull1 = gather_mod(r1)
    ex = exchange(r1)
    com = {"modfull0": modfull0, "modfull1": modfull1}
    com.update(prep_layer_B(inp, 0))
    com.update(prep_layer(inp, 1))
    maps = []
    for i in range(NCORE):
        m = dict(com)
        m.update(ex[i])
        m["xT_in"] = xT[:, i * TPC:(i + 1) * TPC]
        m["cosT"], m["ssinT"] = ropes[i]
        m["B_nab"] = na_bias_tables(inp["na_rpb"][0], i)
        m["B_swb"] = swb[i]
        maps.append(m)
    r2 = run("B0A1", maps)
    ex = exchange(r2)
    com = {"modfull1": modfull1}
    com.update(prep_layer_B(inp, 1))
    maps = []
    for i in range(NCORE):
        m = dict(com)
        m.update(ex[i])
        m["xT_in"] = r2[i]["xT_out"]
        m["B_nab"] = na_bias_tables(inp["na_rpb"][1], i)
        m["B_swb"] = swb[i]
        maps.append(m)
    r3 = run("B1", maps)
    outT = np.concatenate([r["xT_out"] for r in r3], axis=1)
    return np.ascontiguousarray(outT.T)[None].astype(np.float32)
```

```python
import numpy as np
from contextlib import ExitStack, contextmanager
import ml_dtypes
import concourse.bass as bass
import concourse.mybir as mybir
from concourse.bass_utils import run_bass_kernel_spmd

F32 = mybir.dt.float32
BF16 = mybir.dt.bfloat16
AF = mybir.ActivationFunctionType
ALU = mybir.AluOpType
NPBF = ml_dtypes.bfloat16

NCORE = 8
S = 8192
D = 2048
TPC = S // NCORE
NQT = TPC // 128
DFF = 5632
NFC = DFF // 128
ALPHA = float(4 ** 0.25)
LN_EPS = 1e-5
RMS_EPS = 1e-6
NEG = -30000.0
NA_KT = 7
NA_HALO = 3
SW_KT = 3
SW_HALO = 1


class Buf:
    __slots__ = ("name", "w", "r", "dsem", "dcnt")

    def __init__(self, name):
        self.name = name
        self.w = None
        self.r = {}
        self.dsem = None
        self.dcnt = 0


class Eng:
    def __init__(self, name, e, sem, is_pe=False):
        self.name = name
        self.e = e
        self.sem = sem
        self.cnt = 0
        self.waited = {}
        self.is_pe = is_pe
        self.pend_r = []
        self.pend_w = []
        self.nwait = 0
        self.nins = 0


class KB:
    def __init__(self, nc, stack):
        self.nc = nc
        self.root = stack
        self.stack = stack
        mk = lambda n: stack.enter_context(nc.semaphore(n))
        self.E = {
            "pe": Eng("pe", nc.tensor, mk("s_pe"), True),
            "act": Eng("act", nc.scalar, mk("s_act")),
            "dve": Eng("dve", nc.vector, mk("s_dve")),
            "pool": Eng("pool", nc.gpsimd, mk("s_pool")),
            "sp": Eng("sp", nc.sync, mk("s_sp")),
        }
        self.nsem = 5
        self.dsems = {}
        self.bufs = {}
        self.uid = 0

    def buf(self, name):
        b = self.bufs.get(name)
        if b is None:
            b = Buf(name)
            self.bufs[name] = b
        return b

    def sem(self, name):
        self.nsem += 1
        return self.root.enter_context(self.nc.semaphore(name))

    def sb(self, name, shape, dt):
        self.uid += 1
        return self.stack.enter_context(self.nc.sbuf_tensor(f"{name}_{self.uid}", list(shape), dt))

    def ps(self, name, shape, dt=F32):
        return self.root.enter_context(self.nc.psum_tensor(name, list(shape), dt))

    @contextmanager
    def scope(self):
        old = self.stack
        with ExitStack() as sub:
            self.stack = sub
            try:
                yield
            finally:
                self.stack = old
            self.barrier()

    def _wait(self, E, deps):
        for sem, val in deps:
            if E.is_pe and sem is E.sem:
                continue
            if E.waited.get(sem.num, 0) >= val:
                continue
            E.e.wait_ge(sem, val)
            E.waited[sem.num] = val
            E.nwait += 1

    @staticmethod
    def _deps(reads, writes):
        deps = []
        for b in reads:
            if b.w is not None:
                deps.append(b.w)
        for b in writes:
            deps.extend(b.r.values())
            if b.w is not None:
                deps.append(b.w)
        return deps

    def op(self, en, fn, reads=(), writes=(), inc=True):
        E = self.E[en]
        self._wait(E, self._deps(reads, writes))
        ins = fn(E.e)
        E.nins += 1
        if not inc:
            assert E.is_pe
            E.pend_r.extend(reads)
            E.pend_w.extend(writes)
            return ins
        E.cnt += 1
        ins.then_inc(E.sem, 1)
        comp = (E.sem, E.cnt)
        rr = list(reads) + E.pend_r
        ww = list(writes) + E.pend_w
        E.pend_r = []
        E.pend_w = []
        for b in rr:
            b.r[E.sem.num] = comp
        for b in ww:
            b.w = comp
            b.r = {}
        return ins

    def dma(self, qn, out, in_, reads=(), writes=(), owner=None):
        E = self.E[qn]
        self._wait(E, self._deps(reads, writes))
        if owner is None:
            owner = (list(writes) + list(reads))[0]
        if owner.dsem is None:
            owner.dsem = self.sem("d_" + owner.name)
        ins = E.e.dma_start(out=out, in_=in_)
        owner.dcnt += 16
        ins.then_inc(owner.dsem, 16)
        comp = (owner.dsem, owner.dcnt)
        self.dsems[owner.dsem.num] = comp
        for b in reads:
            b.r[owner.dsem.num] = comp
        for b in writes:
            b.w = comp
            b.r = {}
        E.nins += 1
        return ins

    def barrier(self):
        assert not self.E["pe"].pend_r and not self.E["pe"].pend_w
        deps = [(X.sem, X.cnt) for X in self.E.values() if X.cnt > 0]
        deps += list(self.dsems.values())
        for X in self.E.values():
            self._wait(X, [d for d in deps if d[0] is not X.sem])

    def finish(self):
        self.barrier()

    def stats(self):
        return {n: (X.nins, X.nwait) for n, X in self.E.items()}


class Ring:
    def __init__(self, kb, name, shape, dt, n):
        self.t = [kb.sb(f"{name}{i}", shape, dt) for i in range(n)]
        self.b = [kb.buf(f"{name}{i}") for i in range(n)]
        self.i = 0

    def next(self):
        i = self.i
        self.i = (i + 1) % len(self.t)
        return self.t[i], self.b[i]


class Prog:
    def __init__(self, kind):
        self.kind = kind
        self.nc = bass.Bass("TRN2", target_bir_lowering=False)
        self.dr = {}

    def din(self, name, shape, dt=F32):
        ap = self.nc.dram_tensor(name, list(shape), dt, kind="ExternalInput").ap()
        self.dr[name] = ap
        return ap

    def dout(self, name, shape, dt=F32):
        ap = self.nc.dram_tensor(name, list(shape), dt, kind="ExternalOutput").ap()
        self.dr[name] = ap
        return ap


def c1024(c, a, b):
    return slice(c * 1024 + a, c * 1024 + b)


def build(kind):
    P = Prog(kind)
    nc = P.nc
    doA = {"A0": 0, "B0A1": 1, "B1": None}[kind]
    doB = {"A0": None, "B0A1": 0, "B1": 1}[kind]
    with ExitStack() as st:
        st.enter_context(nc.allow_low_precision("bf16 matmul operands by design"))
        kb = KB(nc, st)
        P.kb = kb
        xT = kb.sb("xT", [128, 16 * 1024], F32)
        xB = [[kb.buf(f"x{c}_{tb}") for tb in range(2)] for c in range(16)]
        PS = kb.ps("PS", [128, 8 * 512])
        psB = [kb.buf(f"ps{i}") for i in range(8)]
        P.bank = lambda b, n=512, o=0, p=128: PS[0:p, b * 512 + o:b * 512 + o + n]
        ones = kb.sb("ones", [128, 128], BF16)
        onesB = kb.buf("ones")
        kb.op("dve", lambda e: e.memset(ones[:], 1.0), writes=[onesB])
        ones32 = kb.sb("ones32", [128, 128], F32)
        P.ones32, P.ones32B = ones32, kb.buf("ones32")
        kb.op("dve", lambda e: e.memset(ones32[:], 1.0), writes=[P.ones32B])
        tmpR = Ring(kb, "tmp", [128, 512], F32, 6)
        tbfR = Ring(kb, "tbf", [128, 512], BF16, 4)
        P.xT, P.xB, P.psB, P.ones, P.onesB, P.tmpR, P.tbfR = xT, xB, psB, ones, onesB, tmpR, tbfR
        P.grot = 0

        xin = P.din("xT_in", [D, TPC])
        xl = kb.buf("xload")
        for c in range(16):
            kb.dma("sp", xT[:, c * 1024:(c + 1) * 1024], xin[c * 128:(c + 1) * 128, :],
                   writes=[xB[c][0], xB[c][1]], owner=xl)
        for c in range(16):
            for tb in range(2):
                xB[c][tb].w = (xl.dsem, xl.dcnt)

        P.mod = {}
        for l in sorted({x for x in (doA, doB) if x is not None}):
            mod = kb.sb(f"mod{l}", [128, 96], F32)
            modp = kb.sb(f"modp{l}", [128, 96], F32)
            P.mod[l] = (mod, modp, kb.buf(f"mod{l}"))
        with nc.named_scope('ada'):
            if kind == "A0":
                emit_ada(P)
            else:
                for l in P.mod:
                    emit_mod_load(P, l)

        if doB is not None:
            emit_B(P, doB)
        if doA is not None:
            with nc.named_scope('phaseA'):
                emit_A(P, doA)
        if kind == "B1" or kind == "B0A1":
            xo = P.dout("xT_out", [D, TPC])
            xs = kb.buf("xstore")
            for c in range(16):
                kb.dma("sp", xo[c * 128:(c + 1) * 128, :], xT[:, c * 1024:(c + 1) * 1024],
                       reads=[xB[c][0], xB[c][1]], owner=xs)
        kb.finish()
        P.stats = kb.stats()
        P.nsem = kb.nsem
    return P


def gbank(P):
    b = P.grot
    P.grot = (b + 1) % 4
    return b


NSHARE = 20


def emit_ada(P):
    kb, nc = P.kb, P.nc
    cin = P.din("cT", [128, 16])
    mso = P.dout("modshare", [128, NSHARE])
    with kb.scope():
        cT = kb.sb("cT", [128, 16], F32)
        cB = kb.buf("cT")
        cond = kb.sb("cond", [128, 16], F32)
        condB = kb.buf("cond")
        kb.dma("sp", cT[:], cin, writes=[cB])
        kb.op("act", lambda e: e.activation(out=cond[:], in_=cT[:], func=AF.Silu), reads=[cB], writes=[condB])
        wr = Ring(kb, "wada", [128, 2048], F32, 4)
        mod, modp, mB = P.mod[0]
        shr = kb.sb("mshare", [128, NSHARE], F32)
        shB = kb.buf("mshare")
        bank = 7
        col0 = 0
        for (wname, bname, ncol, dst, dB, dcol) in (("wada_own", "bada_own", 32, mod, mB, 0), ("wada_shr", "bada_shr", NSHARE, shr, shB, 0)):
            wd = P.din(wname, [ncol * 128, 2048])
            bd = P.din(bname, [128, ncol])
            bt = kb.sb(bname, [128, ncol], F32)
            btB = kb.buf(bname)
            kb.dma("sp", bt[:], bd, writes=[btB])
            for i in range(ncol):
                wt, wB = wr.next()
                kb.dma("sp", wt[:], wd[i * 128:(i + 1) * 128, :], writes=[wB])
                for kc in range(16):
                    kb.op("pe", lambda e: e.matmul(P.bank(bank, 1, col0 + i), wt[:, kc * 128:(kc + 1) * 128],
                                                   cond[:, kc:kc + 1], start=(kc == 0), stop=(kc == 15)),
                          reads=[wB, condB], writes=[P.psB[bank]], inc=(kc == 15))
            kb.op("dve", lambda e: e.tensor_tensor(out=dst[:, 0:ncol], in0=P.bank(bank, ncol, col0), in1=bt[:, 0:ncol], op=ALU.add),
                  reads=[P.psB[bank], btB], writes=[dB])
            col0 += ncol
        kb.op("dve", lambda e: e.tensor_scalar_add(out=modp[:, 0:32], in0=mod[:, 0:32], scalar1=1.0), reads=[mB], writes=[mB])
        kb.dma("sp", mso, shr[:], reads=[shB], owner=shB)


def emit_mod_load(P, l):
    kb = P.kb
    md = P.din(f"modfull{l}", [128, 96])
    mod, modp, mB = P.mod[l]
    kb.dma("sp", mod[:], md, writes=[mB])
    kb.op("dve", lambda e: e.tensor_scalar_add(out=modp[:], in0=mod[:], scalar1=1.0), reads=[mB], writes=[mB])


def mcol(P, l, j, c, plus1=False):
    mod, modp, _ = P.mod[l]
    t = modp if plus1 else mod
    return t[:, j * 16 + c:j * 16 + c + 1]


def emit_rstd(P, psbank, scale, eps, out_t, out_b):
    kb = P.kb
    t, tB = P.tmpR.next()
    kb.op("act", lambda e: e.activation(out=t[:], in_=P.bank(psbank), func=AF.Sqrt, bias=eps, scale=scale),
          reads=[P.psB[psbank]], writes=[tB])
    kb.op("dve", lambda e: e.reciprocal(out=out_t, in_=t[:]), reads=[tB], writes=[out_b])


def emit_A(P, l):
    kb, nc = P.kb, P.nc
    xT, xB = P.xT, P.xB
    mB = P.mod[l][2]
    o = {}
    for name, shp in [("qaT", [512, TPC]), ("kaT", [512, TPC]), ("va", [TPC, 512]),
                      ("qcT", [768, TPC]), ("kcT", [256, TPC]), ("vc", [TPC, 256]),
                      ("qnT", [768, TPC]), ("qpT", [384, TPC]), ("knT", [768, TPC]),
                      ("kpeT", [128, TPC]), ("vb", [TPC, 768])]:
        o[name] = P.dout(f"{name}", shp, BF16)
    win = P.din(f"winfm{l}", [24 * 128, 2048])
    wv = P.din(f"winv{l}", [128, 16 * 768])
    wuqn = P.din(f"wuqn{l}", [128, 4 * 768])
    wuqr = P.din(f"wuqr{l}", [128, 4 * 384])
    wuqs = P.din(f"wuqs{l}", [128, 4 * 384])
    wukk = P.din(f"wukk{l}", [128, 2 * 768])
    wukv = P.din(f"wukv{l}", [128, 2 * 768])
    gq = P.din(f"gq{l}", [128, 4])
    gkv = P.din(f"gkv{l}", [128, 2])
    cosd = P.din("cosT", [128, TPC])
    sind = P.din("ssinT", [128, TPC])
    with kb.scope():
        uT = kb.sb("uT", [128, 16 * 1024], BF16)
        uB = [[kb.buf(f"u{c}_{tb}") for tb in range(2)] for c in range(16)]
        for c in range(16):
            for tb in range(2):
                kb.op("dve", lambda e: e.tensor_scalar(out=uT[:, c1024(c, tb * 512, tb * 512 + 512)],
                                                       in0=xT[:, c1024(c, tb * 512, tb * 512 + 512)],
                                                       scalar1=mcol(P, l, 1, c, True), scalar2=mcol(P, l, 0, c),
                                                       op0=ALU.mult, op1=ALU.add),
                      reads=[xB[c][tb], mB], writes=[uB[c][tb]])
        with kb.scope():
            wvt = kb.sb("wv", [128, 16 * 768], BF16)
            wvB = kb.buf("A_wv")
            kb.dma("pool", wvt[:], wv, writes=[wvB])
            st2R = Ring(kb, "stg2", [128, 768], BF16, 2)
            for tt in range(8):
                tb, off = tt // 4, (tt % 4) * 128 + (tt // 4) * 512
                b1, b2 = gbank(P), gbank(P)
                for (bk, c0, n) in ((b1, 0, 512), (b2, 512, 256)):
                    for kc in range(16):
                        kb.op("pe", lambda e: e.matmul(P.bank(bk, n), uT[:, c1024(kc, off, off + 128)],
                                                       wvt[:, kc * 768 + c0:kc * 768 + c0 + n],
                                                       start=(kc == 0), stop=(kc == 15)),
                              reads=[wvB, uB[kc][tb]], writes=[P.psB[bk]], inc=(kc == 15))
                stg, sB = st2R.next()
                kb.op("act", lambda e: e.activation(out=stg[:, 0:512], in_=P.bank(b1), func=AF.Copy), reads=[P.psB[b1]], writes=[sB])
                kb.op("act", lambda e: e.activation(out=stg[:, 512:768], in_=P.bank(b2, 256), func=AF.Copy), reads=[P.psB[b2]], writes=[sB])
                kb.dma("sp", o["va"][tt * 128:(tt + 1) * 128, :], stg[:, 0:512], reads=[sB], owner=sB)
                kb.dma("sp", o["vc"][tt * 128:(tt + 1) * 128, :], stg[:, 512:768], reads=[sB], owner=sB)
        sm = {}
        for nm, dd, shp in [("gq", gq, [128, 4]), ("gkv", gkv, [128, 2]), ("cos", cosd, [128, TPC]), ("sin", sind, [128, TPC])]:
            t = kb.sb(nm, shp, F32)
            b = kb.buf("A_" + nm)
            kb.dma("sp", t[:], dd, writes=[b])
            sm[nm] = (t, b)
        wsm = {}
        for nm, dd, shp in [("wuqn", wuqn, [128, 4 * 768]), ("wuqr", wuqr, [128, 4 * 384]), ("wuqs", wuqs, [128, 4 * 384]),
                            ("wukk", wukk, [128, 2 * 768]), ("wukv", wukv, [128, 2 * 768])]:
            t = kb.sb(nm, shp, BF16)
            b = kb.buf("A_" + nm)
            kb.dma("pool", t[:], dd, writes=[b])
            wsm[nm] = (t, b)
        cqT = kb.sb("cqT", [128, 4 * 1024], F32)
        ckvT = kb.sb("ckvT", [128, 2 * 1024], F32)
        krT = kb.sb("krT", [128, 2 * 1024], F32)
        cqB = [[kb.buf(f"cq{c}_{tb}") for tb in range(2)] for c in range(4)]
        ckvB = [[kb.buf(f"ckv{c}_{tb}") for tb in range(2)] for c in range(2)]
        krB = [[kb.buf(f"kr{c}_{tb}") for tb in range(2)] for c in range(2)]
        wR = Ring(kb, "win", [128, 2048], BF16, 3)
        stR = Ring(kb, "stg", [128, 1024], BF16, 3)
        dest = ([("qaT", i) for i in range(4)] + [("kaT", i) for i in range(4)] + [("qcT", i) for i in range(6)] +
                [("kcT", i) for i in range(2)] + [("cq", i) for i in range(4)] + [("ckv", i) for i in range(2)] +
                [("kr", 0), ("kr", 1)])
        for oc, (dn, di) in enumerate(dest):
            wt, wB = wR.next()
            kb.dma("pool", wt[:], win[oc * 128:(oc + 1) * 128, :], writes=[wB])
            if dn in o:
                stg, sB = stR.next()
            for tb in range(2):
                bk = gbank(P)
                for kc in range(16):
                    kb.op("pe", lambda e: e.matmul(P.bank(bk), wt[:, kc * 128:(kc + 1) * 128],
                                                   uT[:, c1024(kc, tb * 512, tb * 512 + 512)],
                                                   start=(kc == 0), stop=(kc == 15)),
                          reads=[wB, uB[kc][tb]], writes=[P.psB[bk]], inc=(kc == 15))
                if dn in o:
                    kb.op("act", lambda e: e.activation(out=stg[:, tb * 512:(tb + 1) * 512], in_=P.bank(bk), func=AF.Copy),
                          reads=[P.psB[bk]], writes=[sB])
                else:
                    tt, bb = {"cq": (cqT, cqB), "ckv": (ckvT, ckvB), "kr": (krT, krB)}[dn]
                    kb.op("act", lambda e: e.activation(out=tt[:, c1024(di, tb * 512, tb * 512 + 512)], in_=P.bank(bk), func=AF.Copy),
                          reads=[P.psB[bk]], writes=[bb[di][tb]])
            if dn in o:
                kb.dma("sp", o[dn][di * 128:(di + 1) * 128, :], stg[:], reads=[sB], owner=sB)
        cqn = kb.sb("cqn", [128, 4 * 1024], BF16)
        ckvn = kb.sb("ckvn", [128, 2 * 1024], BF16)
        cqnB = [kb.buf(f"cqn{tb}") for tb in range(2)]
        ckvnB = [kb.buf(f"ckvn{tb}") for tb in range(2)]
        rstd = kb.sb("rstdA", [128, 512], F32)
        rB = kb.buf("rstdA")
        for (src, sB_, nchunk, dst, dB, gname) in ((cqT, cqB, 4, cqn, cqnB, "gq"), (ckvT, ckvB, 2, ckvn, ckvnB, "gkv")):
            gt, gB = sm[gname]
            for tb in range(2):
                bk = gbank(P)
                for c in range(nchunk):
                    t, tB = P.tbfR.next()
                    kb.op("act", lambda e: e.activation(out=t[:], in_=src[:, c1024(c, tb * 512, tb * 512 + 512)], func=AF.Square),
                          reads=[sB_[c][tb]], writes=[tB])
                    kb.op("pe", lambda e: e.matmul(P.bank(bk), P.ones[:], t[:], start=(c == 0), stop=(c == nchunk - 1)),
                          reads=[tB, P.onesB], writes=[P.psB[bk]], inc=True)
                emit_rstd(P, bk, 1.0 / (nchunk * 128), RMS_EPS, rstd[:], rB)
                for c in range(nchunk):
                    kb.op("dve", lambda e: e.scalar_tensor_tensor(out=dst[:, c1024(c, tb * 512, tb * 512 + 512)],
                                                                  in0=src[:, c1024(c, tb * 512, tb * 512 + 512)],
                                                                  scalar=gt[:, c:c + 1], in1=rstd[:],
                                                                  op0=ALU.mult, op1=ALU.mult),
                          reads=[sB_[c][tb], gB, rB], writes=[dB[tb]])
        cost, cosB = sm["cos"]
        sint, sinB = sm["sin"]

        def rope(out_ap, a_ap, s_ap, rows, tb, reads, wbuf):
            t1, t1B = P.tmpR.next()
            t2, t2B = P.tmpR.next()
            kb.op("dve", lambda e: e.tensor_tensor(out=t1[0:rows, :], in0=a_ap, in1=cost[0:rows, tb * 512:(tb + 1) * 512], op=ALU.mult),
                  reads=reads + [cosB], writes=[t1B])
            kb.op("dve", lambda e: e.tensor_tensor(out=t2[0:rows, :], in0=s_ap, in1=sint[0:rows, tb * 512:(tb + 1) * 512], op=ALU.mult),
                  reads=reads + [sinB], writes=[t2B])
            kb.op("dve", lambda e: e.tensor_tensor(out=out_ap, in0=t1[0:rows, :], in1=t2[0:rows, :], op=ALU.add),
                  reads=[t1B, t2B], writes=[wbuf])

        stg, sB = stR.next()
        for tb in range(2):
            rope(stg[:, tb * 512:(tb + 1) * 512], krT[:, c1024(0, tb * 512, tb * 512 + 512)],
                 krT[:, c1024(1, tb * 512, tb * 512 + 512)], 128, tb, [krB[0][tb], krB[1][tb]], sB)
        kb.dma("sp", o["kpeT"], stg[:], reads=[sB], owner=sB)
        for (wname, nk, srcn, srcB, oname) in (("wuqn", 4, cqn, cqnB, "qnT"), ("wukk", 2, ckvn, ckvnB, "knT")):
            wt, wB = wsm[wname]
            for h in range(6):
                stg, sB = stR.next()
                for tb in range(2):
                    bk = gbank(P)
                    for kc in range(nk):
                        kb.op("pe", lambda e: e.matmul(P.bank(bk), wt[:, kc * 768 + h * 128:kc * 768 + (h + 1) * 128],
                                                       srcn[:, c1024(kc, tb * 512, tb * 512 + 512)],
                                                       start=(kc == 0), stop=(kc == nk - 1)),
                              reads=[wB, srcB[tb]], writes=[P.psB[bk]], inc=(kc == nk - 1))
                    kb.op("act", lambda e: e.activation(out=stg[:, tb * 512:(tb + 1) * 512], in_=P.bank(bk), func=AF.Copy),
                          reads=[P.psB[bk]], writes=[sB])
                kb.dma("sp", o[oname][h * 128:(h + 1) * 128, :], stg[:], reads=[sB], owner=sB)
        wr_, wrB = wsm["wuqr"]
        ws_, wsB = wsm["wuqs"]
        for h in range(6):
            stg, sB = stR.next()
            for tb in range(2):
                b1, b2 = gbank(P), gbank(P)
                for (bk, wt, wB) in ((b1, wr_, wrB), (b2, ws_, wsB)):
                    for kc in range(4):
                        kb.op("pe", lambda e: e.matmul(P.bank(bk, 512, 0, 64), wt[:, kc * 384 + h * 64:kc * 384 + (h + 1) * 64],
                                                       cqn[:, c1024(kc, tb * 512, tb * 512 + 512)],
                                                       start=(kc == 0), stop=(kc == 3)),
                              reads=[wB, cqnB[tb]], writes=[P.psB[bk]], inc=(kc == 3))
                t3, t3B = P.tmpR.next()
                kb.op("act", lambda e: e.activation(out=t3[0:64, :], in_=P.bank(b2, 512, 0, 64), func=AF.Copy), reads=[P.psB[b2]], writes=[t3B])
                rope(stg[0:64, tb * 512:(tb + 1) * 512], P.bank(b1, 512, 0, 64), t3[0:64, :], 64, tb, [P.psB[b1], t3B], sB)
            kb.dma("sp", o["qpT"][h * 64:(h + 1) * 64, :], stg[0:64, :], reads=[sB], owner=sB)
        st2R = Ring(kb, "stg3", [128, 768], BF16, 2)
        wt, wB = wsm["wukv"]
        for tt in range(8):
            tb, off = tt // 4, (tt % 4) * 128 + (tt // 4) * 512
            b1, b2 = gbank(P), gbank(P)
            for (bk, c0, n) in ((b1, 0, 512), (b2, 512, 256)):
                for kc in range(2):
                    kb.op("pe", lambda e: e.matmul(P.bank(bk, n), ckvn[:, c1024(kc, off, off + 128)],
                                                   wt[:, kc * 768 + c0:kc * 768 + c0 + n],
                                                   start=(kc == 0), stop=(kc == 1)),
                          reads=[wB, ckvnB[tb]], writes=[P.psB[bk]], inc=(kc == 1))
            stg, sB = st2R.next()
            kb.op("act", lambda e: e.activation(out=stg[:, 0:512], in_=P.bank(b1), func=AF.Copy), reads=[P.psB[b1]], writes=[sB])
            kb.op("act", lambda e: e.activation(out=stg[:, 512:768], in_=P.bank(b2, 256), func=AF.Copy), reads=[P.psB[b2]], writes=[sB])
            kb.dma("sp", o["vb"][tt * 128:(tt + 1) * 128, :], stg[:], reads=[sB], owner=sB)


def emit_local_attn(P, tb, yT, yB, spec):
    kb = P.kb
    nkt = spec["nkt"]
    nwin = 4 + nkt - 1
    W = nkt * 128
    kR = Ring(kb, spec["n"] + "k", [128, nwin * 128], BF16, 2)
    vR = Ring(kb, spec["n"] + "v", [128, nwin * 128], BF16, 2)
    qR = Ring(kb, spec["n"] + "q", [128, 512], BF16, 2)
    bR = Ring(kb, spec["n"] + "b", [128, W], F32, 3)
    sR = Ring(kb, spec["n"] + "s", [128, W], F32, 2)
    pR = Ring(kb, spec["n"] + "p", [128, W], BF16, 3)
    rR = Ring(kb, spec["n"] + "r", [128, 128], F32, 2)
    prev_kv = None
    state = {"kv": None, "n": 0}

    def stage1(h, qt):
        kvh = spec["kvh"](h)
        if kvh != state["kv"]:
            kt_, kB_ = kR.next()
            vt_, vB_ = vR.next()
            kb.dma("sp", kt_[:], spec["kd"][kvh * 128:(kvh + 1) * 128, tb * 512:tb * 512 + nwin * 128], writes=[kB_])
            kb.dma("sp", vt_[:], spec["vd"][kvh * 128:(kvh + 1) * 128, tb * 512:tb * 512 + nwin * 128], writes=[vB_])
            state["kv"] = kvh
            state["kvt"] = (kt_, kB_, vt_, vB_)
        kt_, kB_, vt_, vB_ = state["kvt"]
        if qt == 0:
            qt_, qB_ = qR.next()
            kb.dma("sp", qt_[:], spec["qd"][h * 128:(h + 1) * 128, tb * 512:(tb + 1) * 512], writes=[qB_])
            state["q"] = (qt_, qB_)
        qt_, qB_ = state["q"]
        j = tb * 4 + qt
        bt_, bB_ = bR.next()
        row = (j * spec["nh"] + h) * 128
        kb.dma("sp", bt_[:], spec["bd"][row:row + 128, :], writes=[bB_])
        i = state["n"]
        state["n"] += 1
        base = (i % 2) * 2
        for kt in range(nkt):
            bk = base + (kt // 4)
            kb.op("pe", lambda e: e.matmul(P.bank(bk, 128, (kt % 4) * 128),
                                           kt_[:, (qt + kt) * 128:(qt + kt + 1) * 128],
                                           qt_[:, qt * 128:(qt + 1) * 128], start=True, stop=True),
                  reads=[kB_, qB_], writes=[P.psB[bk]], inc=(kt == nkt - 1 or kt == 3))
        st_, sB_ = sR.next()
        n0 = min(W, 512)
        kb.op("dve", lambda e: e.scalar_tensor_tensor(out=st_[:, 0:n0], in0=P.bank(base, n0), scalar=spec["scale"],
                                                      in1=bt_[:, 0:n0], op0=ALU.mult, op1=ALU.add),
              reads=[P.psB[base], bB_], writes=[sB_])
        if W > 512:
            kb.op("dve", lambda e: e.scalar_tensor_tensor(out=st_[:, 512:W], in0=P.bank(base + 1, W - 512), scalar=spec["scale"],
                                                          in1=bt_[:, 512:W], op0=ALU.mult, op1=ALU.add),
                  reads=[P.psB[base + 1], bB_], writes=[sB_])
        pt_, pB_ = pR.next()
        kb.op("act", lambda e: e.activation(out=pt_[:], in_=st_[:], func=AF.Exp), reads=[sB_], writes=[pB_])
        return (h, qt, i, vt_, vB_, pt_, pB_)

    def stage2(item):
        h, qt, i, vt_, vB_, pt_, pB_ = item
        ob, db = 4 + (i % 2), 6 + (i % 2)
        for kt in range(nkt):
            kb.op("pe", lambda e: e.matmul(P.bank(ob, 128), vt_[:, (qt + kt) * 128:(qt + kt + 1) * 128],
                                           pt_[:, kt * 128:(kt + 1) * 128], start=(kt == 0), stop=(kt == nkt - 1)),
                  reads=[vB_, pB_], writes=[P.psB[ob]], inc=(kt == nkt - 1))
        for kt in range(nkt):
            kb.op("pe", lambda e: e.matmul(P.bank(db, 128), P.ones[:], pt_[:, kt * 128:(kt + 1) * 128],
                                           start=(kt == 0), stop=(kt == nkt - 1)),
                  reads=[P.onesB, pB_], writes=[P.psB[db]], inc=(kt == nkt - 1))
        rt_, rB_ = rR.next()
        if spec["esink"] is not None:
            es, esB = spec["esink"]
            kb.op("dve", lambda e: e.tensor_scalar_add(out=rt_[:], in0=P.bank(db, 128), scalar1=es[:, h:h + 1]),
                  reads=[P.psB[db], esB], writes=[rB_])
            kb.op("dve", lambda e: e.reciprocal(out=rt_[:], in_=rt_[:]), reads=[rB_], writes=[rB_])
        else:
            kb.op("dve", lambda e: e.reciprocal(out=rt_[:], in_=P.bank(db, 128)), reads=[P.psB[db]], writes=[rB_])
        ch = spec["ychunk0"] + h
        kb.op("dve", lambda e: e.tensor_tensor(out=yT[:, ch * 512 + qt * 128:ch * 512 + (qt + 1) * 128],
                                               in0=P.bank(ob, 128), in1=rt_[:], op=ALU.mult),
              reads=[P.psB[ob], rB_], writes=[yB[ch]])

    prev = None
    for h in range(spec["nh"]):
        for qt in range(4):
            cur = stage1(h, qt)
            if prev is not None:
                stage2(prev)
            prev = cur
    stage2(prev)


def emit_mla(P, tb, yT, yB, dd, kpe, kpeB):
    kb = P.kb
    scale = float(192 ** -0.5)
    knR = Ring(kb, "mkn", [128, 4096], BF16, 2)
    vR = Ring(kb, "mv", [128, 4096], BF16, 2)
    qnR = Ring(kb, "mqn", [128, 512], BF16, 2)
    qpR = Ring(kb, "mqp", [64, 512], BF16, 2)
    pR = Ring(kb, "mp", [128, 512], BF16, 4)
    accR = Ring(kb, "macc", [128, 512], F32, 2)
    rt_ = kb.sb("mr", [128, 512], F32)
    rB_ = kb.buf("mr")
    srot = 0
    for h in range(6):
        qn, qnB = qnR.next()
        qp, qpB = qpR.next()
        kb.dma("sp", qn[:], dd["qnT"][h * 128:(h + 1) * 128, tb * 512:(tb + 1) * 512], writes=[qnB])
        kb.dma("sp", qp[:], dd["qpT"][h * 64:(h + 1) * 64, tb * 512:(tb + 1) * 512], writes=[qpB])
        ob, db = 4 + (h % 2), 6 + (h % 2)
        acc, accB = accR.next()
        pending = None

        def pv(item, first, last):
            pt, pB, kt, vt, vB = item
            kb.op("pe", lambda e: e.matmul(P.bank(ob), vt[:, (kt % 32) * 128:(kt % 32 + 1) * 128], pt[:], start=first, stop=last),
                  reads=[vB, pB], writes=[P.psB[ob]], inc=True)
            if first:
                kb.op("dve", lambda e: e.tensor_copy(out=acc[:], in_=pt[:]), reads=[pB], writes=[accB])
            else:
                kb.op("dve", lambda e: e.tensor_tensor(out=acc[:], in0=acc[:], in1=pt[:], op=ALU.add), reads=[pB, accB], writes=[accB])

        for half in range(2):
            kn, knB = knR.next()
            vt, vB = vR.next()
            kb.dma("sp", kn[:], dd["knT_all"][h * 128:(h + 1) * 128, half * 4096:(half + 1) * 4096], writes=[knB])
            kb.dma("sp", vt[:], dd["vb_all"][h * 128:(h + 1) * 128, half * 4096:(half + 1) * 4096], writes=[vB])
            for k32 in range(32):
                kt = half * 32 + k32
                bk = srot % 4
                srot += 1
                kb.op("pe", lambda e: e.matmul(P.bank(bk), kn[:, k32 * 128:(k32 + 1) * 128], qn[:], start=True, stop=False),
                      reads=[knB, qnB], writes=[P.psB[bk]], inc=False)
                kb.op("pe", lambda e: e.matmul(P.bank(bk), kpe[0:64, kt * 128:(kt + 1) * 128], qp[:], start=False, stop=True),
                      reads=[kpeB, qpB], writes=[P.psB[bk]], inc=True)
                pt, pB = pR.next()
                kb.op("act", lambda e: e.activation(out=pt[:], in_=P.bank(bk), func=AF.Exp, scale=scale),
                      reads=[P.psB[bk]], writes=[pB])
                if pending is not None:
                    pv(pending, pending[2] == 0, False)
                pending = (pt, pB, kt, vt, vB)
        pv(pending, False, True)
        kb.op("pe", lambda e: e.matmul(P.bank(db), P.ones32[:], acc[:], start=True, stop=True),
              reads=[P.ones32B, accB], writes=[P.psB[db]], inc=True)
        kb.op("dve", lambda e: e.reciprocal(out=rt_[:], in_=P.bank(db)), reads=[P.psB[db]], writes=[rB_])
        ch = 4 + h
        kb.op("dve", lambda e: e.tensor_tensor(out=yT[:, ch * 512:(ch + 1) * 512], in0=P.bank(ob), in1=rt_[:], op=ALU.mult),
              reads=[P.psB[ob], rB_], writes=[yB[ch]])


def emit_ln(P, l, tb, gt, bt, gbB):
    kb = P.kb
    xT, xB = P.xT, P.xB
    sb_, qb_ = 4, 6
    for c in range(16):
        t1, t1B = P.tbfR.next()
        t2, t2B = P.tbfR.next()
        xs = xT[:, c1024(c, tb * 512, tb * 512 + 512)]
        kb.op("act", lambda e: e.activation(out=t1[:], in_=xs, func=AF.Copy), reads=[xB[c][tb]], writes=[t1B])
        kb.op("act", lambda e: e.activation(out=t2[:], in_=xs, func=AF.Square), reads=[xB[c][tb]], writes=[t2B])
        kb.op("pe", lambda e: e.matmul(P.bank(sb_), P.ones[:], t1[:], start=(c == 0), stop=(c == 15)),
              reads=[P.onesB, t1B], writes=[P.psB[sb_]], inc=True)
        kb.op("pe", lambda e: e.matmul(P.bank(qb_), P.ones[:], t2[:], start=(c == 0), stop=(c == 15)),
              reads=[P.onesB, t2B], writes=[P.psB[qb_]], inc=True)
    mean, mB = P.tmpR.next()
    msq, qB = P.tmpR.next()
    var, vB = P.tmpR.next()
    rstd, rB = P.tmpR.next()
    kb.op("act", lambda e: e.mul(out=mean[:], in_=P.bank(sb_), mul=1.0 / D), reads=[P.psB[sb_]], writes=[mB])
    kb.op("dve", lambda e: e.tensor_tensor(out=msq[:], in0=mean[:], in1=mean[:], op=ALU.mult), reads=[mB], writes=[qB])
    kb.op("dve", lambda e: e.scalar_tensor_tensor(out=var[:], in0=P.bank(qb_), scalar=1.0 / D, in1=msq[:],
                                                  op0=ALU.mult, op1=ALU.subtract),
          reads=[P.psB[qb_], qB], writes=[vB])
    t, tB = P.tmpR.next()
    kb.op("act", lambda e: e.activation(out=t[:], in_=var[:], func=AF.Sqrt, bias=LN_EPS, scale=1.0), reads=[vB], writes=[tB])
    kb.op("dve", lambda e: e.reciprocal(out=rstd[:], in_=t[:]), reads=[tB], writes=[rB])
    for c in range(16):
        xs = xT[:, c1024(c, tb * 512, tb * 512 + 512)]
        kb.op("dve", lambda e: e.tensor_tensor(out=xs, in0=xs, in1=mean[:], op=ALU.subtract), reads=[xB[c][tb], mB], writes=[xB[c][tb]])
        kb.op("dve", lambda e: e.tensor_tensor(out=xs, in0=xs, in1=rstd[:], op=ALU.mult), reads=[xB[c][tb], rB], writes=[xB[c][tb]])
        kb.op("act", lambda e: e.activation(out=xs, in_=xs, func=AF.Identity, scale=gt[:, c:c + 1], bias=bt[:, c:c + 1]),
              reads=[xB[c][tb], gbB], writes=[xB[c][tb]])


def emit_resid(P, l, jg, oc, tb, bk):
    kb = P.kb
    xs = P.xT[:, c1024(oc, tb * 512, tb * 512 + 512)]
    t, tB = P.tmpR.next()
    kb.op("act", lambda e: e.mul(out=t[:], in_=xs, mul=ALPHA), reads=[P.xB[oc][tb]], writes=[tB])
    kb.op("dve", lambda e: e.scalar_tensor_tensor(out=xs, in0=P.bank(bk), scalar=mcol(P, l, jg, oc, True), in1=t[:],
                                                  op0=ALU.mult, op1=ALU.add),
          reads=[P.psB[bk], tB, P.mod[l][2]], writes=[P.xB[oc][tb]])


def emit_B(P, l):
    kb, nc = P.kb, P.nc
    xT, xB = P.xT, P.xB
    mB = P.mod[l][2]
    dd = {}
    for name, shp in [("qaT", [512, TPC]), ("qcT", [768, TPC]), ("qnT", [768, TPC]), ("qpT", [384, TPC]),
                      ("kaT_win", [512, 14 * 128]), ("va_win", [512, 14 * 128]),
                      ("kcT_win", [256, 10 * 128]), ("vc_win", [256, 10 * 128]),
                      ("knT_all", [768, S]), ("kpeT_all", [64, S]), ("vb_all", [768, S])]:
        dd[name] = P.din("B_" + name, shp, BF16)
    dd["nab"] = P.din("B_nab", [NQT * 4 * 128, NA_KT * 128])
    dd["swb"] = P.din("B_swb", [NQT * 6 * 128, SW_KT * 128])
    sinkd = P.din("B_sink", [128, 6])
    vecs = {}
    with kb.scope():
        for nm in ["gn", "ln1g", "ln1b", "ln2g", "ln2b"]:
            d_ = P.din(f"B_{nm}", [128, 16])
            t = kb.sb(nm, [128, 16], F32)
            b = kb.buf("B_" + nm)
            kb.dma("sp", t[:], d_, writes=[b])
            vecs[nm] = (t, b)
        sk = kb.sb("sink", [128, 6], F32)
        skB = kb.buf("B_sink")
        kb.dma("sp", sk[:], sinkd, writes=[skB])
        es = kb.sb("esink", [128, 6], F32)
        esB = kb.buf("B_esink")
        kb.op("act", lambda e: e.activation(out=es[:], in_=sk[:], func=AF.Exp), reads=[skB], writes=[esB])
        wo = P.din(f"wo{l}", [16 * 128, 2048])
        wgu = P.din(f"wgu{l}", [NFC * 128, 2 * 2048])
        wdn = P.din(f"wdn{l}", [2 * 16 * 128, (NFC // 2) * 128])
        ynB = [kb.buf(f"yn{c}") for c in range(16)]
        for tb in range(2):
          with kb.scope():
            ynT = kb.sb("ynT", [128, 16 * 512], BF16)
            with kb.scope():
                yT = kb.sb("yT", [128, 16 * 512], F32)
                yB = [kb.buf(f"y{c}") for c in range(16)]
                with kb.scope(), nc.named_scope('local_attn'):
                    emit_local_attn(P, tb, yT, yB, dict(n="na", nh=4, kvh=lambda h: h, nkt=NA_KT, qd=dd["qaT"], kd=dd["kaT_win"],
                                                        vd=dd["va_win"], bd=dd["nab"], scale=float(128 ** -0.5), ychunk0=0, esink=None))
                    emit_local_attn(P, tb, yT, yB, dict(n="sw", nh=6, kvh=lambda h: h // 3, nkt=SW_KT, qd=dd["qcT"], kd=dd["kcT_win"],
                                                        vd=dd["vc_win"], bd=dd["swb"], scale=float(128 ** -0.5), ychunk0=10, esink=(es, esB)))
                with kb.scope(), nc.named_scope('mla'):
                    kpe = kb.sb("kpe", [64, S], BF16)
                    kpeB = kb.buf("kpe")
                    kb.dma("sp", kpe[:], dd["kpeT_all"], writes=[kpeB])
                    emit_mla(P, tb, yT, yB, dd, kpe, kpeB)
                gt, gB = vecs["gn"]
                rstd = kb.sb("rstdB", [128, 512], F32)
                rB = kb.buf("rstdB")
                for (c0, c1) in ((0, 4), (4, 10), (10, 16)):
                    bk = gbank(P)
                    for c in range(c0, c1):
                        t, tB = P.tbfR.next()
                        kb.op("act", lambda e: e.activation(out=t[:], in_=yT[:, c * 512:(c + 1) * 512], func=AF.Square),
                              reads=[yB[c]], writes=[tB])
                        kb.op("pe", lambda e: e.matmul(P.bank(bk), P.ones[:], t[:], start=(c == c0), stop=(c == c1 - 1)),
                              reads=[P.onesB, tB], writes=[P.psB[bk]], inc=True)
                    emit_rstd(P, bk, 1.0 / ((c1 - c0) * 128), RMS_EPS, rstd[:], rB)
                    for c in range(c0, c1):
                        kb.op("dve", lambda e: e.scalar_tensor_tensor(out=ynT[:, c * 512:(c + 1) * 512], in0=yT[:, c * 512:(c + 1) * 512],
                                                                      scalar=gt[:, c:c + 1], in1=rstd[:], op0=ALU.mult, op1=ALU.mult),
                              reads=[yB[c], gB, rB], writes=[ynB[c]])
            with kb.scope(), nc.named_scope('oproj_ln1'):
                wR = Ring(kb, "wo", [128, 2048], BF16, 3)
                for oc in range(16):
                    wt, wB = wR.next()
                    kb.dma("pool", wt[:], wo[oc * 128:(oc + 1) * 128, :], writes=[wB])
                    bk = gbank(P)
                    for kc in range(16):
                        kb.op("pe", lambda e: e.matmul(P.bank(bk), wt[:, kc * 128:(kc + 1) * 128], ynT[:, kc * 512:(kc + 1) * 512],
                                                       start=(kc == 0), stop=(kc == 15)),
                              reads=[wB, ynB[kc]], writes=[P.psB[bk]], inc=(kc == 15))
                    emit_resid(P, l, 2, oc, tb, bk)
                emit_ln(P, l, tb, vecs["ln1g"][0], vecs["ln1b"][0], vecs["ln1g"][1])
        with kb.scope(), nc.named_scope('ffn'):
            NH = NFC // 2
            u2 = kb.sb("u2T", [128, 16 * 1024], BF16)
            u2B = [[kb.buf(f"u2_{c}_{tb}") for tb in range(2)] for c in range(16)]
            hT = kb.sb("hT", [128, NH * 1024], BF16)
            hB = [[kb.buf(f"h{c}_{tb}") for tb in range(2)] for c in range(NH)]
            for c in range(16):
                for tb in range(2):
                    kb.op("dve", lambda e: e.tensor_scalar(out=u2[:, c1024(c, tb * 512, tb * 512 + 512)],
                                                           in0=xT[:, c1024(c, tb * 512, tb * 512 + 512)],
                                                           scalar1=mcol(P, l, 4, c, True), scalar2=mcol(P, l, 3, c),
                                                           op0=ALU.mult, op1=ALU.add),
                          reads=[xB[c][tb], mB], writes=[u2B[c][tb]])
            wR = Ring(kb, "wgu", [128, 4096], BF16, 3)
            wR2 = Ring(kb, "wdn", [128, NH * 128], BF16, 3)
            for half in range(2):
                for f in range(NH):
                    fc = half * NH + f
                    wt, wB = wR.next()
                    kb.dma("pool", wt[:], wgu[fc * 128:(fc + 1) * 128, :], writes=[wB])
                    for tb in range(2):
                        bg, bu = gbank(P), gbank(P)
                        for (bk, off) in ((bg, 0), (bu, 2048)):
                            for kc in range(16):
                                kb.op("pe", lambda e: e.matmul(P.bank(bk), wt[:, off + kc * 128:off + (kc + 1) * 128],
                                                               u2[:, c1024(kc, tb * 512, tb * 512 + 512)],
                                                               start=(kc == 0), stop=(kc == 15)),
                                      reads=[wB, u2B[kc][tb]], writes=[P.psB[bk]], inc=(kc == 15))
                        t, tB = P.tmpR.next()
                        kb.op("act", lambda e: e.activation(out=t[:], in_=P.bank(bg), func=AF.Silu), reads=[P.psB[bg]], writes=[tB])
                        kb.op("dve", lambda e: e.tensor_tensor(out=hT[:, c1024(f, tb * 512, tb * 512 + 512)], in0=P.bank(bu), in1=t[:], op=ALU.mult),
                              reads=[P.psB[bu], tB], writes=[hB[f][tb]])
                for oc in range(16):
                    wt, wB = wR2.next()
                    row = (half * 16 + oc) * 128
                    kb.dma("pool", wt[:], wdn[row:row + 128, :], writes=[wB])
                    for tb in range(2):
                        bk = gbank(P)
                        for f in range(NH):
                            kb.op("pe", lambda e: e.matmul(P.bank(bk), wt[:, f * 128:(f + 1) * 128], hT[:, c1024(f, tb * 512, tb * 512 + 512)],
                                                           start=(f == 0), stop=(f == NH - 1)),
                                  reads=[wB, hB[f][tb]], writes=[P.psB[bk]], inc=(f == NH - 1))
                        if half == 0:
                            emit_resid(P, l, 5, oc, tb, bk)
                        else:
                            xs = xT[:, c1024(oc, tb * 512, tb * 512 + 512)]
                            kb.op("dve", lambda e: e.scalar_tensor_tensor(out=xs, in0=P.bank(bk), scalar=mcol(P, l, 5, oc, True), in1=xs,
                                                                          op0=ALU.mult, op1=ALU.add),
                                  reads=[P.psB[bk], xB[oc][tb], mB], writes=[xB[oc][tb]])
            for tb in range(2):
                emit_ln(P, l, tb, vecs["ln2g"][0], vecs["ln2b"][0], vecs["ln2g"][1])


def fm_vec(v):
    return np.ascontiguousarray(v.reshape(-1, 128).T)


def lhs_layout(w, cols=None):
    K, N = w.shape
    nk, no = K // 128, N // 128
    return np.ascontiguousarray(w.reshape(nk, 128, no, 128).transpose(2, 1, 0, 3).reshape(no * 128, nk * 128))


def rhs_layout(w):
    K, N = w.shape
    nk = K // 128
    return np.ascontiguousarray(w.reshape(nk, 128, N).transpose(1, 0, 2).reshape(128, nk * N))


def prep_layer(inp, l):
    w = {}
    w_in = inp["w_in"][l]
    swap64 = np.concatenate([np.arange(32, 64), np.arange(0, 32)])
    kr_cols = np.arange(2304, 2368)
    cols = np.concatenate([np.arange(0, 512), np.arange(512, 1024), np.arange(2368, 3136), np.arange(3136, 3392),
                           np.arange(1536, 2048), np.arange(2048, 2304), kr_cols, kr_cols, kr_cols[swap64], kr_cols[swap64]])
    w[f"winfm{l}"] = lhs_layout(w_in[:, cols])
    vcols = np.concatenate([np.arange(1024, 1536), np.arange(3392, 3648)])
    w[f"winv{l}"] = rhs_layout(w_in[:, vcols])
    uq = inp["mla_w_uq"][l].reshape(512, 6, 192)
    w[f"wuqn{l}"] = rhs_layout(np.ascontiguousarray(uq[:, :, :128]).reshape(512, 768))
    w[f"wuqr{l}"] = rhs_layout(np.ascontiguousarray(uq[:, :, 128:]).reshape(512, 384))
    w[f"wuqs{l}"] = rhs_layout(np.ascontiguousarray(uq[:, :, 128:][:, :, swap64]).reshape(512, 384))
    ukv = inp["mla_w_ukv"][l].reshape(256, 6, 256)
    w[f"wukk{l}"] = rhs_layout(np.ascontiguousarray(ukv[:, :, :128]).reshape(256, 768))
    w[f"wukv{l}"] = rhs_layout(np.ascontiguousarray(ukv[:, :, 128:]).reshape(256, 768))
    w[f"gq{l}"] = fm_vec(inp["mla_q_norm"][l])
    w[f"gkv{l}"] = fm_vec(inp["mla_kv_norm"][l])
    return w


def prep_layer_B(inp, l):
    w = {}
    w[f"wo{l}"] = lhs_layout(inp["w_o"][l])
    wg = lhs_layout(inp["w_gu"][l][:, :DFF]).reshape(NFC, 128, 2048)
    wu = lhs_layout(inp["w_gu"][l][:, DFF:]).reshape(NFC, 128, 2048)
    w[f"wgu{l}"] = np.ascontiguousarray(np.concatenate([wg, wu], axis=2).reshape(NFC * 128, 4096))
    hd = DFF // 2
    w[f"wdn{l}"] = np.ascontiguousarray(np.concatenate([lhs_layout(inp["w_down"][l][0:hd]), lhs_layout(inp["w_down"][l][hd:])], axis=0))
    for nm, key in [("gn", "out_norm_g"), ("ln1g", "ln1_g"), ("ln1b", "ln1_b"), ("ln2g", "ln2_g"), ("ln2b", "ln2_b")]:
        w["B_" + nm] = fm_vec(inp[key][l])
    w["B_sink"] = np.ascontiguousarray(np.broadcast_to(inp["swa_sink"][l][None, :], (128, 6)))
    return w


def ada_chunks(inp, chunks):
    w = np.concatenate([inp["w_ada"][l][:, c * 128:(c + 1) * 128] for l, c in chunks], axis=1)
    b = np.stack([inp["b_ada"][l][c * 128:(c + 1) * 128] for l, c in chunks], axis=1)
    return lhs_layout(w), np.ascontiguousarray(b)


ADA_REST = [(0, c) for c in range(32, 96)] + [(1, c) for c in range(96)]


def gather_mod(r1):
    allc = np.concatenate([r["modshare"] for r in r1], axis=1)
    m0 = np.zeros((128, 96), np.float32)
    m0[:, 32:96] = allc[:, 0:64]
    m1 = np.ascontiguousarray(allc[:, 64:160])
    return m0, m1


def rope_tables(core):
    half = 32
    inv = (10000.0 ** (-np.arange(half, dtype=np.float32) / half)).astype(np.float32)
    pos = np.arange(core * TPC, (core + 1) * TPC, dtype=np.float32)
    ang = (pos[:, None] * inv[None, :]).astype(np.float32)
    cos = np.cos(ang).astype(np.float32).T
    sin = np.sin(ang).astype(np.float32).T
    c64 = np.concatenate([cos, cos], 0)
    s64 = np.concatenate([-sin, sin], 0)
    return (np.ascontiguousarray(np.concatenate([c64, c64], 0)), np.ascontiguousarray(np.concatenate([s64, s64], 0)))


def na_bias_tables(rpb, core):
    out = np.full((NQT, 4, 128, NA_KT, 128), NEG, dtype=np.float32)
    qi = np.arange(128)
    for j in range(NQT):
        r_even = core * 16 + 2 * j
        qrow = r_even + qi // 64
        qcol = qi % 64
        r0 = np.clip(qrow - 4, 0, 120)
        c0 = np.clip(qcol - 8, 0, 48)
        for kt in range(NA_KT):
            ktile = core * 8 + j - NA_HALO + kt
            if ktile < 0 or ktile >= 64:
                continue
            ki = np.arange(128)
            krow = 2 * ktile + ki // 64
            kcol = ki % 64
            rin = (krow[:, None] >= r0[None, :]) & (krow[:, None] < r0[None, :] + 8)
            cin = (kcol[:, None] >= c0[None, :]) & (kcol[:, None] < c0[None, :] + 16)
            roff = np.clip(krow[:, None] - qrow[None, :] + 7, 0, 14)
            coff = np.clip(kcol[:, None] - qcol[None, :], -15, 15) + 15
            m = rin & cin
            for h in range(4):
                vals = rpb[h][roff, coff]
                out[j, h, :, kt, :] = np.where(m, vals, np.float32(NEG))
    return out.reshape(NQT * 4 * 128, NA_KT * 128)


def sw_bias_tables(core):
    out = np.full((NQT, 6, 128, SW_KT, 128), NEG, dtype=np.float32)
    slopes = np.array([2.0 ** (-8.0 * (i + 1) / 6) for i in range(6)], dtype=np.float32)
    qi = np.arange(128)
    for j in range(NQT):
        qpos = (core * 8 + j) * 128 + qi
        for kt in range(SW_KT):
            ktile = core * 8 + j - 1 + kt
            if ktile < 0 or ktile >= 64:
                continue
            kpos = ktile * 128 + np.arange(128)
            dist = np.abs(qpos[None, :] - kpos[:, None])
            m = dist <= 128
            for h in range(6):
                out[j, h, :, kt, :] = np.where(m, -slopes[h] * dist.astype(np.float32), np.float32(NEG))
    return out.reshape(NQT * 6 * 128, SW_KT * 128)


def v_layout(v_tok, nh):
    T = v_tok.shape[0]
    return np.ascontiguousarray(v_tok.reshape(T // 128, 128, nh, 128).transpose(2, 1, 0, 3).reshape(nh * 128, T))


def exchange(outs):
    cat = lambda k, ax: np.concatenate([o[k] for o in outs], axis=ax)
    kaT = cat("kaT", 1)
    kcT = cat("kcT", 1)
    va = v_layout(cat("va", 0), 4)
    vc = v_layout(cat("vc", 0), 2)
    knT = cat("knT", 1)
    kpeT = np.ascontiguousarray(cat("kpeT", 1)[0:64])
    vb = v_layout(cat("vb", 0), 6)

    def win(a, core, halo, n):
        t0 = core * 8 - halo
        res = np.zeros((a.shape[0], n * 128), dtype=a.dtype)
        lo, hi = max(t0, 0), min(t0 + n, 64)
        res[:, (lo - t0) * 128:(hi - t0) * 128] = a[:, lo * 128:hi * 128]
        return res

    res = []
    for i in range(NCORE):
        d = {"B_qaT": outs[i]["qaT"], "B_qcT": outs[i]["qcT"], "B_qnT": outs[i]["qnT"], "B_qpT": outs[i]["qpT"],
             "B_kaT_win": win(kaT, i, NA_HALO, 14), "B_va_win": win(va, i, NA_HALO, 14),
             "B_kcT_win": win(kcT, i, SW_HALO, 10), "B_vc_win": win(vc, i, SW_HALO, 10),
             "B_knT_all": knT, "B_kpeT_all": kpeT, "B_vb_all": vb}
        res.append(d)
    return res


_PROGS = {}


def get_prog(kind):
    if kind not in _PROGS:
        _PROGS[kind] = build(kind)
    return _PROGS[kind]


TRACE = False
LAST = {}


def run(kind, in_maps):
    P = get_prog(kind)
    maps = [{k: np.ascontiguousarray(m[k]) for k in P.dr if k in m} for m in in_maps]
    if TRACE:
        res = run_bass_kernel_spmd(P.nc, maps, core_ids=list(range(NCORE)), trace=True)
        LAST[kind] = res
    else:
        res = run_bass_kernel_spmd(P.nc, maps, core_ids=list(range(NCORE)))
    return res.results


def kernel(**inp):
    inp = {k: np.asarray(v) for k, v in inp.items()}
    x = inp["x"][0]
    xT = np.ascontiguousarray(x.T)
    cT = fm_vec(inp["c"][0])
    ropes = [rope_tables(i) for i in range(NCORE)]
    swb = [sw_bias_tables(i) for i in range(NCORE)]
    com = {"cT": cT}
    com["wada_own"], com["bada_own"] = ada_chunks(inp, [(0, c) for c in range(32)])
    com.update(prep_layer(inp, 0))
    maps = []
    for i in range(NCORE):
        m = dict(com)
        m["xT_in"] = xT[:, i * TPC:(i + 1) * TPC]
        m["cosT"], m["ssinT"] = ropes[i]
        m["wada_shr"], m["bada_shr"] = ada_chunks(inp, ADA_REST[i * NSHARE:(i + 1) * NSHARE])
        maps.append(m)
    r1 = run("A0", maps)
    modfull0, modfull1 = gather_mod(r1)
    ex = exchange(r1)
    com = {"modfull0": modfull0, "modfull1": modfull1}
    com.update(prep_layer_B(inp, 0))
    com.update(prep_layer(inp, 1))
    maps = []
    for i in range(NCORE):
        m = dict(com)
        m.update(ex[i])
        m["xT_in"] = xT[:, i * TPC:(i + 1) * TPC]
        m["cosT"], m["ssinT"] = ropes[i]
        m["B_nab"] = na_bias_tables(inp["na_rpb"][0], i)
        m["B_swb"] = swb[i]
        maps.append(m)
    r2 = run("B0A1", maps)
    ex = exchange(r2)
    com = {"modfull1": modfull1}
    com.update(prep_layer_B(inp, 1))
    maps = []
    for i in range(NCORE):
        m = dict(com)
        m.update(ex[i])
        m["xT_in"] = r2[i]["xT_out"]
        m["B_nab"] = na_bias_tables(inp["na_rpb"][1], i)
        m["B_swb"] = swb[i]
        maps.append(m)
    r3 = run("B1", maps)
    outT = np.concatenate([r["xT_out"] for r in r3], axis=1)
    return np.ascontiguousarray(outT.T)[None].astype(np.float32)
```

```python
import numpy as np
from contextlib import ExitStack, contextmanager
import ml_dtypes
import concourse.bass as bass
import concourse.mybir as mybir
from concourse.bass_utils import run_bass_kernel_spmd

F32 = mybir.dt.float32
BF16 = mybir.dt.bfloat16
AF = mybir.ActivationFunctionType
ALU = mybir.AluOpType
NPBF = ml_dtypes.bfloat16

NCORE = 8
S = 8192
D = 2048
TPC = S // NCORE
NQT = TPC // 128
DFF = 5632
NFC = DFF // 128
ALPHA = float(4 ** 0.25)
LN_EPS = 1e-5
RMS_EPS = 1e-6
NEG = -30000.0
NA_KT = 7
NA_HALO = 3
SW_KT = 3
SW_HALO = 1
MLA_LAG = 3


class Buf:
    __slots__ = ("name", "w", "r", "dsem", "dcnt")

    def __init__(self, name):
        self.name = name
        self.w = None
        self.r = {}
        self.dsem = None
        self.dcnt = 0


class Eng:
    def __init__(self, name, e, sem, is_pe=False):
        self.name = name
        self.e = e
        self.sem = sem
        self.cnt = 0
        self.waited = {}
        self.is_pe = is_pe
        self.pend_r = []
        self.pend_w = []
        self.nwait = 0
        self.nins = 0


class KB:
    def __init__(self, nc, stack):
        self.nc = nc
        self.root = stack
        self.stack = stack
        mk = lambda n: stack.enter_context(nc.semaphore(n))
        self.E = {
            "pe": Eng("pe", nc.tensor, mk("s_pe"), True),
            "act": Eng("act", nc.scalar, mk("s_act")),
            "dve": Eng("dve", nc.vector, mk("s_dve")),
            "pool": Eng("pool", nc.gpsimd, mk("s_pool")),
            "sp": Eng("sp", nc.sync, mk("s_sp")),
        }
        self.nsem = 5
        self.dsems = {}
        self.bufs = {}
        self.uid = 0

    def buf(self, name):
        b = self.bufs.get(name)
        if b is None:
            b = Buf(name)
            self.bufs[name] = b
        return b

    def sem(self, name):
        self.nsem += 1
        return self.root.enter_context(self.nc.semaphore(name))

    def sb(self, name, shape, dt):
        self.uid += 1
        return self.stack.enter_context(self.nc.sbuf_tensor(f"{name}_{self.uid}", list(shape), dt))

    def ps(self, name, shape, dt=F32):
        return self.root.enter_context(self.nc.psum_tensor(name, list(shape), dt))

    @contextmanager
    def scope(self):
        old = self.stack
        with ExitStack() as sub:
            self.stack = sub
            try:
                yield
            finally:
                self.stack = old
            self.barrier()

    def _wait(self, E, deps):
        for sem, val in deps:
            if E.is_pe and sem is E.sem:
                continue
            if E.waited.get(sem.num, 0) >= val:
                continue
            E.e.wait_ge(sem, val)
            E.waited[sem.num] = val
            E.nwait += 1

    @staticmethod
    def _deps(reads, writes):
        deps = []
        for b in reads:
            if b.w is not None:
                deps.append(b.w)
        for b in writes:
            deps.extend(b.r.values())
            if b.w is not None:
                deps.append(b.w)
        return deps

    def op(self, en, fn, reads=(), writes=(), inc=True):
        E = self.E[en]
        self._wait(E, self._deps(reads, writes))
        ins = fn(E.e)
        E.nins += 1
        if not inc:
            assert E.is_pe
            E.pend_r.extend(reads)
            E.pend_w.extend(writes)
            return ins
        E.cnt += 1
        ins.then_inc(E.sem, 1)
        comp = (E.sem, E.cnt)
        rr = list(reads) + E.pend_r
        ww = list(writes) + E.pend_w
        E.pend_r = []
        E.pend_w = []
        for b in rr:
            b.r[E.sem.num] = comp
        for b in ww:
            b.w = comp
            b.r = {}
        return ins

    def dma(self, qn, out, in_, reads=(), writes=(), owner=None):
        E = self.E[qn]
        self._wait(E, self._deps(reads, writes))
        if owner is None:
            owner = (list(writes) + list(reads))[0]
        if owner.dsem is None:
            owner.dsem = self.sem("d_" + owner.name)
        ins = E.e.dma_start(out=out, in_=in_)
        owner.dcnt += 16
        ins.then_inc(owner.dsem, 16)
        comp = (owner.dsem, owner.dcnt)
        self.dsems[owner.dsem.num] = comp
        for b in reads:
            b.r[owner.dsem.num] = comp
        for b in writes:
            b.w = comp
            b.r = {}
        E.nins += 1
        return ins

    def barrier(self):
        assert not self.E["pe"].pend_r and not self.E["pe"].pend_w
        deps = [(X.sem, X.cnt) for X in self.E.values() if X.cnt > 0]
        deps += list(self.dsems.values())
        for X in self.E.values():
            self._wait(X, [d for d in deps if d[0] is not X.sem])

    def finish(self):
        self.barrier()

    def stats(self):
        return {n: (X.nins, X.nwait) for n, X in self.E.items()}


class Ring:
    def __init__(self, kb, name, shape, dt, n):
        self.t = [kb.sb(f"{name}{i}", shape, dt) for i in range(n)]
        self.b = [kb.buf(f"{name}{i}") for i in range(n)]
        self.i = 0

    def next(self):
        i = self.i
        self.i = (i + 1) % len(self.t)
        return self.t[i], self.b[i]


class Prog:
    def __init__(self, kind):
        self.kind = kind
        self.nc = bass.Bass("TRN2", target_bir_lowering=False)
        self.dr = {}

    def din(self, name, shape, dt=F32):
        ap = self.nc.dram_tensor(name, list(shape), dt, kind="ExternalInput").ap()
        self.dr[name] = ap
        return ap

    def dout(self, name, shape, dt=F32):
        ap = self.nc.dram_tensor(name, list(shape), dt, kind="ExternalOutput").ap()
        self.dr[name] = ap
        return ap


def c1024(c, a, b):
    return slice(c * 1024 + a, c * 1024 + b)


def build(kind):
    P = Prog(kind)
    nc = P.nc
    doA = {"A0": 0, "B0A1": 1, "B1": None}[kind]
    doB = {"A0": None, "B0A1": 0, "B1": 1}[kind]
    with ExitStack() as st:
        st.enter_context(nc.allow_low_precision("bf16 matmul operands by design"))
        kb = KB(nc, st)
        P.kb = kb
        xT = kb.sb("xT", [128, 16 * 1024], F32)
        xB = [[kb.buf(f"x{c}_{tb}") for tb in range(2)] for c in range(16)]
        PS = kb.ps("PS", [128, 8 * 512])
        psB = [kb.buf(f"ps{i}") for i in range(8)]
        P.bank = lambda b, n=512, o=0, p=128: PS[0:p, b * 512 + o:b * 512 + o + n]
        ones = kb.sb("ones", [128, 128], BF16)
        onesB = kb.buf("ones")
        kb.op("dve", lambda e: e.memset(ones[:], 1.0), writes=[onesB])
        ones32 = kb.sb("ones32", [128, 128], F32)
        P.ones32, P.ones32B = ones32, kb.buf("ones32")
        kb.op("dve", lambda e: e.memset(ones32[:], 1.0), writes=[P.ones32B])
        tmpR = Ring(kb, "tmp", [128, 512], F32, 6)
        tbfR = Ring(kb, "tbf", [128, 512], BF16, 4)
        P.xT, P.xB, P.psB, P.ones, P.onesB, P.tmpR, P.tbfR = xT, xB, psB, ones, onesB, tmpR, tbfR
        P.grot = 0

        xin = P.din("xT_in", [D, TPC])
        xl = kb.buf("xload")
        for c in range(16):
            kb.dma("sp", xT[:, c * 1024:(c + 1) * 1024], xin[c * 128:(c + 1) * 128, :],
                   writes=[xB[c][0], xB[c][1]], owner=xl)
        for c in range(16):
            for tb in range(2):
                xB[c][tb].w = (xl.dsem, xl.dcnt)

        P.mod = {}
        for l in sorted({x for x in (doA, doB) if x is not None}):
            mod = kb.sb(f"mod{l}", [128, 96], F32)
            modp = kb.sb(f"modp{l}", [128, 96], F32)
            P.mod[l] = (mod, modp, kb.buf(f"mod{l}"))
        with nc.named_scope('ada'):
            if kind == "A0":
                emit_ada(P)
            else:
                for l in P.mod:
                    emit_mod_load(P, l)

        if doB is not None:
            emit_B(P, doB)
        if doA is not None:
            with nc.named_scope('phaseA'):
                emit_A(P, doA)
        if kind == "B1" or kind == "B0A1":
            xo = P.dout("xT_out", [D, TPC])
            xs = kb.buf("xstore")
            for c in range(16):
                kb.dma("sp", xo[c * 128:(c + 1) * 128, :], xT[:, c * 1024:(c + 1) * 1024],
                       reads=[xB[c][0], xB[c][1]], owner=xs)
        kb.finish()
        P.stats = kb.stats()
        P.nsem = kb.nsem
    return P


def gbank(P):
    b = P.grot
    P.grot = (b + 1) % 4
    return b


NSHARE = 20


def emit_ada(P):
    kb, nc = P.kb, P.nc
    cin = P.din("cT", [128, 16])
    mso = P.dout("modshare", [128, NSHARE])
    with kb.scope():
        cT = kb.sb("cT", [128, 16], F32)
        cB = kb.buf("cT")
        cond = kb.sb("cond", [128, 16], F32)
        condB = kb.buf("cond")
        kb.dma("sp", cT[:], cin, writes=[cB])
        kb.op("act", lambda e: e.activation(out=cond[:], in_=cT[:], func=AF.Silu), reads=[cB], writes=[condB])
        wr = Ring(kb, "wada", [128, 2048], F32, 4)
        mod, modp, mB = P.mod[0]
        shr = kb.sb("mshare", [128, NSHARE], F32)
        shB = kb.buf("mshare")
        bank = 7
        col0 = 0
        for (wname, bname, ncol, dst, dB, dcol) in (("wada_own", "bada_own", 32, mod, mB, 0), ("wada_shr", "bada_shr", NSHARE, shr, shB, 0)):
            wd = P.din(wname, [ncol * 128, 2048])
            bd = P.din(bname, [128, ncol])
            bt = kb.sb(bname, [128, ncol], F32)
            btB = kb.buf(bname)
            kb.dma("sp", bt[:], bd, writes=[btB])
            for i in range(ncol):
                wt, wB = wr.next()
                kb.dma("sp", wt[:], wd[i * 128:(i + 1) * 128, :], writes=[wB])
                for kc in range(16):
                    kb.op("pe", lambda e: e.matmul(P.bank(bank, 1, col0 + i), wt[:, kc * 128:(kc + 1) * 128],
                                                   cond[:, kc:kc + 1], start=(kc == 0), stop=(kc == 15)),
                          reads=[wB, condB], writes=[P.psB[bank]], inc=(kc == 15))
            kb.op("dve", lambda e: e.tensor_tensor(out=dst[:, 0:ncol], in0=P.bank(bank, ncol, col0), in1=bt[:, 0:ncol], op=ALU.add),
                  reads=[P.psB[bank], btB], writes=[dB])
            col0 += ncol
        kb.op("dve", lambda e: e.tensor_scalar_add(out=modp[:, 0:32], in0=mod[:, 0:32], scalar1=1.0), reads=[mB], writes=[mB])
        kb.dma("sp", mso, shr[:], reads=[shB], owner=shB)


def emit_mod_load(P, l):
    kb = P.kb
    md = P.din(f"modfull{l}", [128, 96])
    mod, modp, mB = P.mod[l]
    kb.dma("sp", mod[:], md, writes=[mB])
    kb.op("dve", lambda e: e.tensor_scalar_add(out=modp[:], in0=mod[:], scalar1=1.0), reads=[mB], writes=[mB])


def mcol(P, l, j, c, plus1=False):
    mod, modp, _ = P.mod[l]
    t = modp if plus1 else mod
    return t[:, j * 16 + c:j * 16 + c + 1]


def emit_rstd(P, psbank, scale, eps, out_t, out_b):
    kb = P.kb
    t, tB = P.tmpR.next()
    kb.op("act", lambda e: e.activation(out=t[:], in_=P.bank(psbank), func=AF.Sqrt, bias=eps, scale=scale),
          reads=[P.psB[psbank]], writes=[tB])
    kb.op("dve", lambda e: e.reciprocal(out=out_t, in_=t[:]), reads=[tB], writes=[out_b])


def emit_A(P, l):
    kb, nc = P.kb, P.nc
    xT, xB = P.xT, P.xB
    mB = P.mod[l][2]
    o = {}
    for name, shp in [("qaT", [512, TPC]), ("kaT", [512, TPC]), ("va", [TPC, 512]),
                      ("qcT", [768, TPC]), ("kcT", [256, TPC]), ("vc", [TPC, 256]),
                      ("qnT", [768, TPC]), ("qpT", [384, TPC]), ("knT", [768, TPC]),
                      ("kpeT", [128, TPC]), ("vb", [TPC, 768])]:
        o[name] = P.dout(f"{name}", shp, BF16)
    win = P.din(f"winfm{l}", [24 * 128, 2048])
    wv = P.din(f"winv{l}", [128, 16 * 768])
    wuqn = P.din(f"wuqn{l}", [128, 4 * 768])
    wuqr = P.din(f"wuqr{l}", [128, 4 * 384])
    wuqs = P.din(f"wuqs{l}", [128, 4 * 384])
    wukk = P.din(f"wukk{l}", [128, 2 * 768])
    wukv = P.din(f"wukv{l}", [128, 2 * 768])
    gq = P.din(f"gq{l}", [128, 4])
    gkv = P.din(f"gkv{l}", [128, 2])
    cosd = P.din("cosT", [128, TPC])
    sind = P.din("ssinT", [128, TPC])
    with kb.scope():
        uT = kb.sb("uT", [128, 16 * 1024], BF16)
        uB = [[kb.buf(f"u{c}_{tb}") for tb in range(2)] for c in range(16)]
        for c in range(16):
            for tb in range(2):
                kb.op("dve", lambda e: e.tensor_scalar(out=uT[:, c1024(c, tb * 512, tb * 512 + 512)],
                                                       in0=xT[:, c1024(c, tb * 512, tb * 512 + 512)],
                                                       scalar1=mcol(P, l, 1, c, True), scalar2=mcol(P, l, 0, c),
                                                       op0=ALU.mult, op1=ALU.add),
                      reads=[xB[c][tb], mB], writes=[uB[c][tb]])
        with kb.scope():
            wvt = kb.sb("wv", [128, 16 * 768], BF16)
            wvB = kb.buf("A_wv")
            kb.dma("pool", wvt[:], wv, writes=[wvB])
            st2R = Ring(kb, "stg2", [128, 768], BF16, 2)
            for tt in range(8):
                tb, off = tt // 4, (tt % 4) * 128 + (tt // 4) * 512
                b1, b2 = gbank(P), gbank(P)
                for (bk, c0, n) in ((b1, 0, 512), (b2, 512, 256)):
                    for kc in range(16):
                        kb.op("pe", lambda e: e.matmul(P.bank(bk, n), uT[:, c1024(kc, off, off + 128)],
                                                       wvt[:, kc * 768 + c0:kc * 768 + c0 + n],
                                                       start=(kc == 0), stop=(kc == 15)),
                              reads=[wvB, uB[kc][tb]], writes=[P.psB[bk]], inc=(kc == 15))
                stg, sB = st2R.next()
                kb.op("act", lambda e: e.activation(out=stg[:, 0:512], in_=P.bank(b1), func=AF.Copy), reads=[P.psB[b1]], writes=[sB])
                kb.op("act", lambda e: e.activation(out=stg[:, 512:768], in_=P.bank(b2, 256), func=AF.Copy), reads=[P.psB[b2]], writes=[sB])
                kb.dma("sp", o["va"][tt * 128:(tt + 1) * 128, :], stg[:, 0:512], reads=[sB], owner=sB)
                kb.dma("sp", o["vc"][tt * 128:(tt + 1) * 128, :], stg[:, 512:768], reads=[sB], owner=sB)
        sm = {}
        for nm, dd, shp in [("gq", gq, [128, 4]), ("gkv", gkv, [128, 2]), ("cos", cosd, [128, TPC]), ("sin", sind, [128, TPC])]:
            t = kb.sb(nm, shp, F32)
            b = kb.buf("A_" + nm)
            kb.dma("sp", t[:], dd, writes=[b])
            sm[nm] = (t, b)
        wsm = {}
        for nm, dd, shp in [("wuqn", wuqn, [128, 4 * 768]), ("wuqr", wuqr, [128, 4 * 384]), ("wuqs", wuqs, [128, 4 * 384]),
                            ("wukk", wukk, [128, 2 * 768]), ("wukv", wukv, [128, 2 * 768])]:
            t = kb.sb(nm, shp, BF16)
            b = kb.buf("A_" + nm)
            kb.dma("pool", t[:], dd, writes=[b])
            wsm[nm] = (t, b)
        cqT = kb.sb("cqT", [128, 4 * 1024], F32)
        ckvT = kb.sb("ckvT", [128, 2 * 1024], F32)
        krT = kb.sb("krT", [128, 2 * 1024], F32)
        cqB = [[kb.buf(f"cq{c}_{tb}") for tb in range(2)] for c in range(4)]
        ckvB = [[kb.buf(f"ckv{c}_{tb}") for tb in range(2)] for c in range(2)]
        krB = [[kb.buf(f"kr{c}_{tb}") for tb in range(2)] for c in range(2)]
        wR = Ring(kb, "win", [128, 2048], BF16, 3)
        stR = Ring(kb, "stg", [128, 1024], BF16, 3)
        dest = ([("qaT", i) for i in range(4)] + [("kaT", i) for i in range(4)] + [("qcT", i) for i in range(6)] +
                [("kcT", i) for i in range(2)] + [("cq", i) for i in range(4)] + [("ckv", i) for i in range(2)] +
                [("kr", 0), ("kr", 1)])
        for oc, (dn, di) in enumerate(dest):
            wt, wB = wR.next()
            kb.dma("pool", wt[:], win[oc * 128:(oc + 1) * 128, :], writes=[wB])
            if dn in o:
                stg, sB = stR.next()
            for tb in range(2):
                bk = gbank(P)
                for kc in range(16):
                    kb.op("pe", lambda e: e.matmul(P.bank(bk), wt[:, kc * 128:(kc + 1) * 128],
                                                   uT[:, c1024(kc, tb * 512, tb * 512 + 512)],
                                                   start=(kc == 0), stop=(kc == 15)),
                          reads=[wB, uB[kc][tb]], writes=[P.psB[bk]], inc=(kc == 15))
                if dn in o:
                    kb.op("act", lambda e: e.activation(out=stg[:, tb * 512:(tb + 1) * 512], in_=P.bank(bk), func=AF.Copy),
                          reads=[P.psB[bk]], writes=[sB])
                else:
                    tt, bb = {"cq": (cqT, cqB), "ckv": (ckvT, ckvB), "kr": (krT, krB)}[dn]
                    kb.op("act", lambda e: e.activation(out=tt[:, c1024(di, tb * 512, tb * 512 + 512)], in_=P.bank(bk), func=AF.Copy),
                          reads=[P.psB[bk]], writes=[bb[di][tb]])
            if dn in o:
                kb.dma("sp", o[dn][di * 128:(di + 1) * 128, :], stg[:], reads=[sB], owner=sB)
        cqn = kb.sb("cqn", [128, 4 * 1024], BF16)
        ckvn = kb.sb("ckvn", [128, 2 * 1024], BF16)
        cqnB = [kb.buf(f"cqn{tb}") for tb in range(2)]
        ckvnB = [kb.buf(f"ckvn{tb}") for tb in range(2)]
        rstd = kb.sb("rstdA", [128, 512], F32)
        rB = kb.buf("rstdA")
        for (src, sB_, nchunk, dst, dB, gname) in ((cqT, cqB, 4, cqn, cqnB, "gq"), (ckvT, ckvB, 2, ckvn, ckvnB, "gkv")):
            gt, gB = sm[gname]
            for tb in range(2):
                bk = gbank(P)
                for c in range(nchunk):
                    t, tB = P.tbfR.next()
                    kb.op("act", lambda e: e.activation(out=t[:], in_=src[:, c1024(c, tb * 512, tb * 512 + 512)], func=AF.Square),
                          reads=[sB_[c][tb]], writes=[tB])
                    kb.op("pe", lambda e: e.matmul(P.bank(bk), P.ones[:], t[:], start=(c == 0), stop=(c == nchunk - 1)),
                          reads=[tB, P.onesB], writes=[P.psB[bk]], inc=True)
                emit_rstd(P, bk, 1.0 / (nchunk * 128), RMS_EPS, rstd[:], rB)
                for c in range(nchunk):
                    kb.op("dve", lambda e: e.scalar_tensor_tensor(out=dst[:, c1024(c, tb * 512, tb * 512 + 512)],
                                                                  in0=src[:, c1024(c, tb * 512, tb * 512 + 512)],
                                                                  scalar=gt[:, c:c + 1], in1=rstd[:],
                                                                  op0=ALU.mult, op1=ALU.mult),
                          reads=[sB_[c][tb], gB, rB], writes=[dB[tb]])
        cost, cosB = sm["cos"]
        sint, sinB = sm["sin"]

        def rope(out_ap, a_ap, s_ap, rows, tb, reads, wbuf):
            t1, t1B = P.tmpR.next()
            t2, t2B = P.tmpR.next()
            kb.op("dve", lambda e: e.tensor_tensor(out=t1[0:rows, :], in0=a_ap, in1=cost[0:rows, tb * 512:(tb + 1) * 512], op=ALU.mult),
                  reads=reads + [cosB], writes=[t1B])
            kb.op("dve", lambda e: e.tensor_tensor(out=t2[0:rows, :], in0=s_ap, in1=sint[0:rows, tb * 512:(tb + 1) * 512], op=ALU.mult),
                  reads=reads + [sinB], writes=[t2B])
            kb.op("dve", lambda e: e.tensor_tensor(out=out_ap, in0=t1[0:rows, :], in1=t2[0:rows, :], op=ALU.add),
                  reads=[t1B, t2B], writes=[wbuf])

        stg, sB = stR.next()
        for tb in range(2):
            rope(stg[:, tb * 512:(tb + 1) * 512], krT[:, c1024(0, tb * 512, tb * 512 + 512)],
                 krT[:, c1024(1, tb * 512, tb * 512 + 512)], 128, tb, [krB[0][tb], krB[1][tb]], sB)
        kb.dma("sp", o["kpeT"], stg[:], reads=[sB], owner=sB)
        for (wname, nk, srcn, srcB, oname) in (("wuqn", 4, cqn, cqnB, "qnT"), ("wukk", 2, ckvn, ckvnB, "knT")):
            wt, wB = wsm[wname]
            for h in range(6):
                stg, sB = stR.next()
                for tb in range(2):
                    bk = gbank(P)
                    for kc in range(nk):
                        kb.op("pe", lambda e: e.matmul(P.bank(bk), wt[:, kc * 768 + h * 128:kc * 768 + (h + 1) * 128],
                                                       srcn[:, c1024(kc, tb * 512, tb * 512 + 512)],
                                                       start=(kc == 0), stop=(kc == nk - 1)),
                              reads=[wB, srcB[tb]], writes=[P.psB[bk]], inc=(kc == nk - 1))
                    kb.op("act", lambda e: e.activation(out=stg[:, tb * 512:(tb + 1) * 512], in_=P.bank(bk), func=AF.Copy),
                          reads=[P.psB[bk]], writes=[sB])
                kb.dma("sp", o[oname][h * 128:(h + 1) * 128, :], stg[:], reads=[sB], owner=sB)
        wr_, wrB = wsm["wuqr"]
        ws_, wsB = wsm["wuqs"]
        for h in range(6):
            stg, sB = stR.next()
            for tb in range(2):
                b1, b2 = gbank(P), gbank(P)
                for (bk, wt, wB) in ((b1, wr_, wrB), (b2, ws_, wsB)):
                    for kc in range(4):
                        kb.op("pe", lambda e: e.matmul(P.bank(bk, 512, 0, 64), wt[:, kc * 384 + h * 64:kc * 384 + (h + 1) * 64],
                                                       cqn[:, c1024(kc, tb * 512, tb * 512 + 512)],
                                                       start=(kc == 0), stop=(kc == 3)),
                              reads=[wB, cqnB[tb]], writes=[P.psB[bk]], inc=(kc == 3))
                t3, t3B = P.tmpR.next()
                kb.op("act", lambda e: e.activation(out=t3[0:64, :], in_=P.bank(b2, 512, 0, 64), func=AF.Copy), reads=[P.psB[b2]], writes=[t3B])
                rope(stg[0:64, tb * 512:(tb + 1) * 512], P.bank(b1, 512, 0, 64), t3[0:64, :], 64, tb, [P.psB[b1], t3B], sB)
            kb.dma("sp", o["qpT"][h * 64:(h + 1) * 64, :], stg[0:64, :], reads=[sB], owner=sB)
        st2R = Ring(kb, "stg3", [128, 768], BF16, 2)
        wt, wB = wsm["wukv"]
        for tt in range(8):
            tb, off = tt // 4, (tt % 4) * 128 + (tt // 4) * 512
            b1, b2 = gbank(P), gbank(P)
            for (bk, c0, n) in ((b1, 0, 512), (b2, 512, 256)):
                for kc in range(2):
                    kb.op("pe", lambda e: e.matmul(P.bank(bk, n), ckvn[:, c1024(kc, off, off + 128)],
                                                   wt[:, kc * 768 + c0:kc * 768 + c0 + n],
                                                   start=(kc == 0), stop=(kc == 1)),
                          reads=[wB, ckvnB[tb]], writes=[P.psB[bk]], inc=(kc == 1))
            stg, sB = st2R.next()
            kb.op("act", lambda e: e.activation(out=stg[:, 0:512], in_=P.bank(b1), func=AF.Copy), reads=[P.psB[b1]], writes=[sB])
            kb.op("act", lambda e: e.activation(out=stg[:, 512:768], in_=P.bank(b2, 256), func=AF.Copy), reads=[P.psB[b2]], writes=[sB])
            kb.dma("sp", o["vb"][tt * 128:(tt + 1) * 128, :], stg[:], reads=[sB], owner=sB)


def emit_local_attn(P, tb, yT, yB, spec):
    kb = P.kb
    nkt = spec["nkt"]
    nwin = 4 + nkt - 1
    W = nkt * 128
    kR = Ring(kb, spec["n"] + "k", [128, nwin * 128], BF16, 2)
    vR = Ring(kb, spec["n"] + "v", [128, nwin * 128], BF16, 2)
    qR = Ring(kb, spec["n"] + "q", [128, 512], BF16, 2)
    bR = Ring(kb, spec["n"] + "b", [128, W], F32, 3)
    sR = Ring(kb, spec["n"] + "s", [128, W], F32, 2)
    pR = Ring(kb, spec["n"] + "p", [128, W], BF16, 3)
    rR = Ring(kb, spec["n"] + "r", [128, 128], F32, 2)
    prev_kv = None
    state = {"kv": None, "n": 0}

    def stage1(h, qt):
        kvh = spec["kvh"](h)
        if kvh != state["kv"]:
            kt_, kB_ = kR.next()
            vt_, vB_ = vR.next()
            kb.dma("sp", kt_[:], spec["kd"][kvh * 128:(kvh + 1) * 128, tb * 512:tb * 512 + nwin * 128], writes=[kB_])
            kb.dma("sp", vt_[:], spec["vd"][kvh * 128:(kvh + 1) * 128, tb * 512:tb * 512 + nwin * 128], writes=[vB_])
            state["kv"] = kvh
            state["kvt"] = (kt_, kB_, vt_, vB_)
        kt_, kB_, vt_, vB_ = state["kvt"]
        if qt == 0:
            qt_, qB_ = qR.next()
            kb.dma("sp", qt_[:], spec["qd"][h * 128:(h + 1) * 128, tb * 512:(tb + 1) * 512], writes=[qB_])
            state["q"] = (qt_, qB_)
        qt_, qB_ = state["q"]
        j = tb * 4 + qt
        bt_, bB_ = bR.next()
        row = (j * spec["nh"] + h) * 128
        kb.dma("sp", bt_[:], spec["bd"][row:row + 128, :], writes=[bB_])
        i = state["n"]
        state["n"] += 1
        base = (i % 2) * 2
        for kt in range(nkt):
            bk = base + (kt // 4)
            kb.op("pe", lambda e: e.matmul(P.bank(bk, 128, (kt % 4) * 128),
                                           kt_[:, (qt + kt) * 128:(qt + kt + 1) * 128],
                                           qt_[:, qt * 128:(qt + 1) * 128], start=True, stop=True),
                  reads=[kB_, qB_], writes=[P.psB[bk]], inc=(kt == nkt - 1 or kt == 3))
        st_, sB_ = sR.next()
        n0 = min(W, 512)
        kb.op("dve", lambda e: e.scalar_tensor_tensor(out=st_[:, 0:n0], in0=P.bank(base, n0), scalar=spec["scale"],
                                                      in1=bt_[:, 0:n0], op0=ALU.mult, op1=ALU.add),
              reads=[P.psB[base], bB_], writes=[sB_])
        if W > 512:
            kb.op("dve", lambda e: e.scalar_tensor_tensor(out=st_[:, 512:W], in0=P.bank(base + 1, W - 512), scalar=spec["scale"],
                                                          in1=bt_[:, 512:W], op0=ALU.mult, op1=ALU.add),
                  reads=[P.psB[base + 1], bB_], writes=[sB_])
        pt_, pB_ = pR.next()
        kb.op("act", lambda e: e.activation(out=pt_[:], in_=st_[:], func=AF.Exp), reads=[sB_], writes=[pB_])
        return (h, qt, i, vt_, vB_, pt_, pB_)

    def stage2(item):
        h, qt, i, vt_, vB_, pt_, pB_ = item
        ob, db = 4 + (i % 2), 6 + (i % 2)
        for kt in range(nkt):
            kb.op("pe", lambda e: e.matmul(P.bank(ob, 128), vt_[:, (qt + kt) * 128:(qt + kt + 1) * 128],
                                           pt_[:, kt * 128:(kt + 1) * 128], start=(kt == 0), stop=(kt == nkt - 1)),
                  reads=[vB_, pB_], writes=[P.psB[ob]], inc=(kt == nkt - 1))
        for kt in range(nkt):
            kb.op("pe", lambda e: e.matmul(P.bank(db, 128), P.ones[:], pt_[:, kt * 128:(kt + 1) * 128],
                                           start=(kt == 0), stop=(kt == nkt - 1)),
                  reads=[P.onesB, pB_], writes=[P.psB[db]], inc=(kt == nkt - 1))
        rt_, rB_ = rR.next()
        if spec["esink"] is not None:
            es, esB = spec["esink"]
            kb.op("dve", lambda e: e.tensor_scalar_add(out=rt_[:], in0=P.bank(db, 128), scalar1=es[:, h:h + 1]),
                  reads=[P.psB[db], esB], writes=[rB_])
            kb.op("dve", lambda e: e.reciprocal(out=rt_[:], in_=rt_[:]), reads=[rB_], writes=[rB_])
        else:
            kb.op("dve", lambda e: e.reciprocal(out=rt_[:], in_=P.bank(db, 128)), reads=[P.psB[db]], writes=[rB_])
        ch = spec["ychunk0"] + h
        kb.op("dve", lambda e: e.tensor_tensor(out=yT[:, ch * 512 + qt * 128:ch * 512 + (qt + 1) * 128],
                                               in0=P.bank(ob, 128), in1=rt_[:], op=ALU.mult),
              reads=[P.psB[ob], rB_], writes=[yB[ch]])

    prev = None
    for h in range(spec["nh"]):
        for qt in range(4):
            cur = stage1(h, qt)
            if prev is not None:
                stage2(prev)
            prev = cur
    stage2(prev)


def emit_mla(P, tb, yT, yB, dd, kpe, kpeB):
    kb = P.kb
    scale = float(192 ** -0.5)
    knR = Ring(kb, "mkn", [128, 4096], BF16, 2)
    vR = Ring(kb, "mv", [128, 4096], BF16, 2)
    qnR = Ring(kb, "mqn", [128, 512], BF16, 2)
    qpR = Ring(kb, "mqp", [128, 512], BF16, 2)
    for t_, b_ in zip(qpR.t, qpR.b):
        kb.op("pool", lambda e: e.memset(t_[64:128, :], 0.0), writes=[kb.buf(b_.name + "_z")])
    pR = Ring(kb, "mp", [128, 512], BF16, 8)
    accR = Ring(kb, "macc", [128, 512], F32, 2)
    prR = Ring(kb, "mpr", [128, 512], BF16, 2)
    rt_ = kb.sb("mr", [128, 512], F32)
    rB_ = kb.buf("mr")
    srot = 0
    for h in range(6):
        qn, qnB = qnR.next()
        qp, qpB = qpR.next()
        kb.dma("sp", qn[:], dd["qnT"][h * 128:(h + 1) * 128, tb * 512:(tb + 1) * 512], writes=[qnB])
        kb.dma("sp", qp[0:64, :], dd["qpT"][h * 64:(h + 1) * 64, tb * 512:(tb + 1) * 512], writes=[qpB])
        qpZ = kb.buf(qpB.name + "_z")
        ob, db = 4 + (h % 2), 6 + (h % 2)
        acc, accB = accR.next()
        stash = []
        pendq = []

        def pv(item, first, last):
            pt, pB, kt, vt, vB = item
            kb.op("pe", lambda e: e.matmul(P.bank(ob), vt[:, (kt % 32) * 128:(kt % 32 + 1) * 128], pt[:], start=first, stop=last),
                  reads=[vB, pB], writes=[P.psB[ob]], inc=True)
            if kt % 2 == 0:
                stash.append((pt, pB))
            else:
                p0, p0B = stash.pop()
                t2, t2B = prR.next()
                kb.op("dve", lambda e: e.tensor_tensor(out=t2[:], in0=p0[:], in1=pt[:], op=ALU.add), reads=[p0B, pB], writes=[t2B])
                if kt == 1:
                    kb.op("dve", lambda e: e.tensor_copy(out=acc[:], in_=t2[:]), reads=[t2B], writes=[accB])
                else:
                    kb.op("dve", lambda e: e.tensor_tensor(out=acc[:], in0=acc[:], in1=t2[:], op=ALU.add), reads=[t2B, accB], writes=[accB])

        for half in range(2):
            kn, knB = knR.next()
            vt, vB = vR.next()
            kb.dma("sp", kn[:], dd["knT_all"][h * 128:(h + 1) * 128, half * 4096:(half + 1) * 4096], writes=[knB])
            kb.dma("sp", vt[:], dd["vb_all"][h * 128:(h + 1) * 128, half * 4096:(half + 1) * 4096], writes=[vB])
            for k32 in range(32):
                kt = half * 32 + k32
                bk = srot % 4
                srot += 1
                kb.op("pe", lambda e: e.matmul(P.bank(bk), kn[:, k32 * 128:(k32 + 1) * 128], qn[:], start=True, stop=False),
                      reads=[knB, qnB], writes=[P.psB[bk]], inc=False)
                kb.op("pe", lambda e: e.matmul(P.bank(bk), kpe[:, kt * 128:(kt + 1) * 128], qp[:], start=False, stop=True),
                      reads=kpeB + [qpB, qpZ], writes=[P.psB[bk]], inc=True)
                pt, pB = pR.next()
                kb.op("act", lambda e: e.activation(out=pt[:], in_=P.bank(bk), func=AF.Exp, scale=scale),
                      reads=[P.psB[bk]], writes=[pB])
                pendq.append((pt, pB, kt, vt, vB))
                if len(pendq) > MLA_LAG:
                    it = pendq.pop(0)
                    pv(it, it[2] == 0, False)
        while pendq:
            it = pendq.pop(0)
            pv(it, it[2] == 0, it[2] == 63)
        kb.op("pe", lambda e: e.matmul(P.bank(db), P.ones32[:], acc[:], start=True, stop=True),
              reads=[P.ones32B, accB], writes=[P.psB[db]], inc=True)
        kb.op("dve", lambda e: e.reciprocal(out=rt_[:], in_=P.bank(db)), reads=[P.psB[db]], writes=[rB_])
        ch = 4 + h
        kb.op("dve", lambda e: e.tensor_tensor(out=yT[:, ch * 512:(ch + 1) * 512], in0=P.bank(ob), in1=rt_[:], op=ALU.mult),
              reads=[P.psB[ob], rB_], writes=[yB[ch]])


def emit_ln(P, l, tb, gt, bt, gbB):
    kb = P.kb
    xT, xB = P.xT, P.xB
    sb_, qb_ = 4, 6
    for c in range(16):
        t1, t1B = P.tbfR.next()
        t2, t2B = P.tbfR.next()
        xs = xT[:, c1024(c, tb * 512, tb * 512 + 512)]
        kb.op("act", lambda e: e.activation(out=t1[:], in_=xs, func=AF.Copy), reads=[xB[c][tb]], writes=[t1B])
        kb.op("act", lambda e: e.activation(out=t2[:], in_=xs, func=AF.Square), reads=[xB[c][tb]], writes=[t2B])
        kb.op("pe", lambda e: e.matmul(P.bank(sb_), P.ones[:], t1[:], start=(c == 0), stop=(c == 15)),
              reads=[P.onesB, t1B], writes=[P.psB[sb_]], inc=True)
        kb.op("pe", lambda e: e.matmul(P.bank(qb_), P.ones[:], t2[:], start=(c == 0), stop=(c == 15)),
              reads=[P.onesB, t2B], writes=[P.psB[qb_]], inc=True)
    mean, mB = P.tmpR.next()
    msq, qB = P.tmpR.next()
    var, vB = P.tmpR.next()
    rstd, rB = P.tmpR.next()
    kb.op("act", lambda e: e.mul(out=mean[:], in_=P.bank(sb_), mul=1.0 / D), reads=[P.psB[sb_]], writes=[mB])
    kb.op("dve", lambda e: e.tensor_tensor(out=msq[:], in0=mean[:], in1=mean[:], op=ALU.mult), reads=[mB], writes=[qB])
    kb.op("dve", lambda e: e.scalar_tensor_tensor(out=var[:], in0=P.bank(qb_), scalar=1.0 / D, in1=msq[:],
                                                  op0=ALU.mult, op1=ALU.subtract),
          reads=[P.psB[qb_], qB], writes=[vB])
    t, tB = P.tmpR.next()
    kb.op("act", lambda e: e.activation(out=t[:], in_=var[:], func=AF.Sqrt, bias=LN_EPS, scale=1.0), reads=[vB], writes=[tB])
    kb.op("dve", lambda e: e.reciprocal(out=rstd[:], in_=t[:]), reads=[tB], writes=[rB])
    for c in range(16):
        xs = xT[:, c1024(c, tb * 512, tb * 512 + 512)]
        kb.op("dve", lambda e: e.tensor_tensor(out=xs, in0=xs, in1=mean[:], op=ALU.subtract), reads=[xB[c][tb], mB], writes=[xB[c][tb]])
        kb.op("dve", lambda e: e.tensor_tensor(out=xs, in0=xs, in1=rstd[:], op=ALU.mult), reads=[xB[c][tb], rB], writes=[xB[c][tb]])
        kb.op("act", lambda e: e.activation(out=xs, in_=xs, func=AF.Identity, scale=gt[:, c:c + 1], bias=bt[:, c:c + 1]),
              reads=[xB[c][tb], gbB], writes=[xB[c][tb]])


def emit_resid(P, l, jg, oc, tb, bk):
    kb = P.kb
    xs = P.xT[:, c1024(oc, tb * 512, tb * 512 + 512)]
    t, tB = P.tmpR.next()
    kb.op("act", lambda e: e.mul(out=t[:], in_=xs, mul=ALPHA), reads=[P.xB[oc][tb]], writes=[tB])
    kb.op("dve", lambda e: e.scalar_tensor_tensor(out=xs, in0=P.bank(bk), scalar=mcol(P, l, jg, oc, True), in1=t[:],
                                                  op0=ALU.mult, op1=ALU.add),
          reads=[P.psB[bk], tB, P.mod[l][2]], writes=[P.xB[oc][tb]])


def emit_B(P, l):
    kb, nc = P.kb, P.nc
    xT, xB = P.xT, P.xB
    mB = P.mod[l][2]
    dd = {}
    for name, shp in [("qaT", [512, TPC]), ("qcT", [768, TPC]), ("qnT", [768, TPC]), ("qpT", [384, TPC]),
                      ("kaT_win", [512, 14 * 128]), ("va_win", [512, 14 * 128]),
                      ("kcT_win", [256, 10 * 128]), ("vc_win", [256, 10 * 128]),
                      ("knT_all", [768, S]), ("kpeT_all", [64, S]), ("vb_all", [768, S])]:
        dd[name] = P.din("B_" + name, shp, BF16)
    dd["nab"] = P.din("B_nab", [NQT * 4 * 128, NA_KT * 128])
    dd["swb"] = P.din("B_swb", [NQT * 6 * 128, SW_KT * 128])
    sinkd = P.din("B_sink", [128, 6])
    vecs = {}
    with kb.scope():
        for nm in ["gn", "ln1g", "ln1b", "ln2g", "ln2b"]:
            d_ = P.din(f"B_{nm}", [128, 16])
            t = kb.sb(nm, [128, 16], F32)
            b = kb.buf("B_" + nm)
            kb.dma("sp", t[:], d_, writes=[b])
            vecs[nm] = (t, b)
        sk = kb.sb("sink", [128, 6], F32)
        skB = kb.buf("B_sink")
        kb.dma("sp", sk[:], sinkd, writes=[skB])
        es = kb.sb("esink", [128, 6], F32)
        esB = kb.buf("B_esink")
        kb.op("act", lambda e: e.activation(out=es[:], in_=sk[:], func=AF.Exp), reads=[skB], writes=[esB])
        wo = P.din(f"wo{l}", [16 * 128, 2048])
        wgu = P.din(f"wgu{l}", [NFC * 128, 2 * 2048])
        wdn = P.din(f"wdn{l}", [2 * 16 * 128, (NFC // 2) * 128])
        ynB = [kb.buf(f"yn{c}") for c in range(16)]
        for tb in range(2):
          with kb.scope():
            ynT = kb.sb("ynT", [128, 16 * 512], BF16)
            with kb.scope():
                yT = kb.sb("yT", [128, 16 * 512], F32)
                yB = [kb.buf(f"y{c}") for c in range(16)]
                with kb.scope(), nc.named_scope('local_attn'):
                    emit_local_attn(P, tb, yT, yB, dict(n="na", nh=4, kvh=lambda h: h, nkt=NA_KT, qd=dd["qaT"], kd=dd["kaT_win"],
                                                        vd=dd["va_win"], bd=dd["nab"], scale=float(128 ** -0.5), ychunk0=0, esink=None))
                    emit_local_attn(P, tb, yT, yB, dict(n="sw", nh=6, kvh=lambda h: h // 3, nkt=SW_KT, qd=dd["qcT"], kd=dd["kcT_win"],
                                                        vd=dd["vc_win"], bd=dd["swb"], scale=float(128 ** -0.5), ychunk0=10, esink=(es, esB)))
                with kb.scope(), nc.named_scope('mla'):
                    kpe = kb.sb("kpe", [128, S], BF16)
                    kpeB = kb.buf("kpe")
                    kpeB2 = kb.buf("kpe_z")
                    kb.op("pool", lambda e: e.memset(kpe[64:128, :], 0.0), writes=[kpeB2])
                    kb.dma("sp", kpe[0:64, :], dd["kpeT_all"], writes=[kpeB])
                    emit_mla(P, tb, yT, yB, dd, kpe, [kpeB, kpeB2])
                gt, gB = vecs["gn"]
                rstd = kb.sb("rstdB", [128, 512], F32)
                rB = kb.buf("rstdB")
                for (c0, c1) in ((0, 4), (4, 10), (10, 16)):
                    bk = gbank(P)
                    for c in range(c0, c1):
                        t, tB = P.tbfR.next()
                        kb.op("act", lambda e: e.activation(out=t[:], in_=yT[:, c * 512:(c + 1) * 512], func=AF.Square),
                              reads=[yB[c]], writes=[tB])
                        kb.op("pe", lambda e: e.matmul(P.bank(bk), P.ones[:], t[:], start=(c == c0), stop=(c == c1 - 1)),
                              reads=[P.onesB, tB], writes=[P.psB[bk]], inc=True)
                    emit_rstd(P, bk, 1.0 / ((c1 - c0) * 128), RMS_EPS, rstd[:], rB)
                    for c in range(c0, c1):
                        kb.op("dve", lambda e: e.scalar_tensor_tensor(out=ynT[:, c * 512:(c + 1) * 512], in0=yT[:, c * 512:(c + 1) * 512],
                                                                      scalar=gt[:, c:c + 1], in1=rstd[:], op0=ALU.mult, op1=ALU.mult),
                              reads=[yB[c], gB, rB], writes=[ynB[c]])
            with kb.scope(), nc.named_scope('oproj_ln1'):
                wR = Ring(kb, "wo", [128, 2048], BF16, 3)
                for oc in range(16):
                    wt, wB = wR.next()
                    kb.dma("pool", wt[:], wo[oc * 128:(oc + 1) * 128, :], writes=[wB])
                    bk = gbank(P)
                    for kc in range(16):
                        kb.op("pe", lambda e: e.matmul(P.bank(bk), wt[:, kc * 128:(kc + 1) * 128], ynT[:, kc * 512:(kc + 1) * 512],
                                                       start=(kc == 0), stop=(kc == 15)),
                              reads=[wB, ynB[kc]], writes=[P.psB[bk]], inc=(kc == 15))
                    emit_resid(P, l, 2, oc, tb, bk)
                emit_ln(P, l, tb, vecs["ln1g"][0], vecs["ln1b"][0], vecs["ln1g"][1])
        with kb.scope(), nc.named_scope('ffn'):
            NH = NFC // 2
            u2 = kb.sb("u2T", [128, 16 * 1024], BF16)
            u2B = [[kb.buf(f"u2_{c}_{tb}") for tb in range(2)] for c in range(16)]
            hT = kb.sb("hT", [128, NH * 1024], BF16)
            hB = [[kb.buf(f"h{c}_{tb}") for tb in range(2)] for c in range(NH)]
            for c in range(16):
                for tb in range(2):
                    kb.op("dve", lambda e: e.tensor_scalar(out=u2[:, c1024(c, tb * 512, tb * 512 + 512)],
                                                           in0=xT[:, c1024(c, tb * 512, tb * 512 + 512)],
                                                           scalar1=mcol(P, l, 4, c, True), scalar2=mcol(P, l, 3, c),
                                                           op0=ALU.mult, op1=ALU.add),
                          reads=[xB[c][tb], mB], writes=[u2B[c][tb]])
            wR = Ring(kb, "wgu", [128, 4096], BF16, 3)
            wR2 = Ring(kb, "wdn", [128, NH * 128], BF16, 3)
            for half in range(2):
                for f in range(NH):
                    fc = half * NH + f
                    wt, wB = wR.next()
                    kb.dma("pool", wt[:], wgu[fc * 128:(fc + 1) * 128, :], writes=[wB])
                    for tb in range(2):
                        bg, bu = gbank(P), gbank(P)
                        for (bk, off) in ((bg, 0), (bu, 2048)):
                            for kc in range(16):
                                kb.op("pe", lambda e: e.matmul(P.bank(bk), wt[:, off + kc * 128:off + (kc + 1) * 128],
                                                               u2[:, c1024(kc, tb * 512, tb * 512 + 512)],
                                                               start=(kc == 0), stop=(kc == 15)),
                                      reads=[wB, u2B[kc][tb]], writes=[P.psB[bk]], inc=(kc == 15))
                        t, tB = P.tmpR.next()
                        kb.op("act", lambda e: e.activation(out=t[:], in_=P.bank(bg), func=AF.Silu), reads=[P.psB[bg]], writes=[tB])
                        kb.op("dve", lambda e: e.tensor_tensor(out=hT[:, c1024(f, tb * 512, tb * 512 + 512)], in0=P.bank(bu), in1=t[:], op=ALU.mult),
                              reads=[P.psB[bu], tB], writes=[hB[f][tb]])
                for oc in range(16):
                    wt, wB = wR2.next()
                    row = (half * 16 + oc) * 128
                    kb.dma("pool", wt[:], wdn[row:row + 128, :], writes=[wB])
                    for tb in range(2):
                        bk = gbank(P)
                        for f in range(NH):
                            kb.op("pe", lambda e: e.matmul(P.bank(bk), wt[:, f * 128:(f + 1) * 128], hT[:, c1024(f, tb * 512, tb * 512 + 512)],
                                                           start=(f == 0), stop=(f == NH - 1)),
                                  reads=[wB, hB[f][tb]], writes=[P.psB[bk]], inc=(f == NH - 1))
                        if half == 0:
                            emit_resid(P, l, 5, oc, tb, bk)
                        else:
                            xs = xT[:, c1024(oc, tb * 512, tb * 512 + 512)]
                            kb.op("dve", lambda e: e.scalar_tensor_tensor(out=xs, in0=P.bank(bk), scalar=mcol(P, l, 5, oc, True), in1=xs,
                                                                          op0=ALU.mult, op1=ALU.add),
                                  reads=[P.psB[bk], xB[oc][tb], mB], writes=[xB[oc][tb]])
            for tb in range(2):
                emit_ln(P, l, tb, vecs["ln2g"][0], vecs["ln2b"][0], vecs["ln2g"][1])


def fm_vec(v):
    return np.ascontiguousarray(v.reshape(-1, 128).T)


def lhs_layout(w, cols=None):
    K, N = w.shape
    nk, no = K // 128, N // 128
    return np.ascontiguousarray(w.reshape(nk, 128, no, 128).transpose(2, 1, 0, 3).reshape(no * 128, nk * 128))


def rhs_layout(w):
    K, N = w.shape
    nk = K // 128
    return np.ascontiguousarray(w.reshape(nk, 128, N).transpose(1, 0, 2).reshape(128, nk * N))


def prep_layer(inp, l):
    w = {}
    w_in = inp["w_in"][l]
    swap64 = np.concatenate([np.arange(32, 64), np.arange(0, 32)])
    kr_cols = np.arange(2304, 2368)
    cols = np.concatenate([np.arange(0, 512), np.arange(512, 1024), np.arange(2368, 3136), np.arange(3136, 3392),
                           np.arange(1536, 2048), np.arange(2048, 2304), kr_cols, kr_cols, kr_cols[swap64], kr_cols[swap64]])
    w[f"winfm{l}"] = lhs_layout(w_in[:, cols])
    vcols = np.concatenate([np.arange(1024, 1536), np.arange(3392, 3648)])
    w[f"winv{l}"] = rhs_layout(w_in[:, vcols])
    uq = inp["mla_w_uq"][l].reshape(512, 6, 192)
    w[f"wuqn{l}"] = rhs_layout(np.ascontiguousarray(uq[:, :, :128]).reshape(512, 768))
    w[f"wuqr{l}"] = rhs_layout(np.ascontiguousarray(uq[:, :, 128:]).reshape(512, 384))
    w[f"wuqs{l}"] = rhs_layout(np.ascontiguousarray(uq[:, :, 128:][:, :, swap64]).reshape(512, 384))
    ukv = inp["mla_w_ukv"][l].reshape(256, 6, 256)
    w[f"wukk{l}"] = rhs_layout(np.ascontiguousarray(ukv[:, :, :128]).reshape(256, 768))
    w[f"wukv{l}"] = rhs_layout(np.ascontiguousarray(ukv[:, :, 128:]).reshape(256, 768))
    w[f"gq{l}"] = fm_vec(inp["mla_q_norm"][l])
    w[f"gkv{l}"] = fm_vec(inp["mla_kv_norm"][l])
    return w


def prep_layer_B(inp, l):
    w = {}
    w[f"wo{l}"] = lhs_layout(inp["w_o"][l])
    wg = lhs_layout(inp["w_gu"][l][:, :DFF]).reshape(NFC, 128, 2048)
    wu = lhs_layout(inp["w_gu"][l][:, DFF:]).reshape(NFC, 128, 2048)
    w[f"wgu{l}"] = np.ascontiguousarray(np.concatenate([wg, wu], axis=2).reshape(NFC * 128, 4096))
    hd = DFF // 2
    w[f"wdn{l}"] = np.ascontiguousarray(np.concatenate([lhs_layout(inp["w_down"][l][0:hd]), lhs_layout(inp["w_down"][l][hd:])], axis=0))
    for nm, key in [("gn", "out_norm_g"), ("ln1g", "ln1_g"), ("ln1b", "ln1_b"), ("ln2g", "ln2_g"), ("ln2b", "ln2_b")]:
        w["B_" + nm] = fm_vec(inp[key][l])
    w["B_sink"] = np.ascontiguousarray(np.broadcast_to(inp["swa_sink"][l][None, :], (128, 6)))
    return w


def ada_chunks(inp, chunks):
    w = np.concatenate([inp["w_ada"][l][:, c * 128:(c + 1) * 128] for l, c in chunks], axis=1)
    b = np.stack([inp["b_ada"][l][c * 128:(c + 1) * 128] for l, c in chunks], axis=1)
    return lhs_layout(w), np.ascontiguousarray(b)


ADA_REST = [(0, c) for c in range(32, 96)] + [(1, c) for c in range(96)]


def gather_mod(r1):
    allc = np.concatenate([r["modshare"] for r in r1], axis=1)
    m0 = np.zeros((128, 96), np.float32)
    m0[:, 32:96] = allc[:, 0:64]
    m1 = np.ascontiguousarray(allc[:, 64:160])
    return m0, m1


def rope_tables(core):
    half = 32
    inv = (10000.0 ** (-np.arange(half, dtype=np.float32) / half)).astype(np.float32)
    pos = np.arange(core * TPC, (core + 1) * TPC, dtype=np.float32)
    ang = (pos[:, None] * inv[None, :]).astype(np.float32)
    cos = np.cos(ang).astype(np.float32).T
    sin = np.sin(ang).astype(np.float32).T
    c64 = np.concatenate([cos, cos], 0)
    s64 = np.concatenate([-sin, sin], 0)
    return (np.ascontiguousarray(np.concatenate([c64, c64], 0)), np.ascontiguousarray(np.concatenate([s64, s64], 0)))


def na_bias_tables(rpb, core):
    out = np.full((NQT, 4, 128, NA_KT, 128), NEG, dtype=np.float32)
    qi = np.arange(128)
    for j in range(NQT):
        r_even = core * 16 + 2 * j
        qrow = r_even + qi // 64
        qcol = qi % 64
        r0 = np.clip(qrow - 4, 0, 120)
        c0 = np.clip(qcol - 8, 0, 48)
        for kt in range(NA_KT):
            ktile = core * 8 + j - NA_HALO + kt
            if ktile < 0 or ktile >= 64:
                continue
            ki = np.arange(128)
            krow = 2 * ktile + ki // 64
            kcol = ki % 64
            rin = (krow[:, None] >= r0[None, :]) & (krow[:, None] < r0[None, :] + 8)
            cin = (kcol[:, None] >= c0[None, :]) & (kcol[:, None] < c0[None, :] + 16)
            roff = np.clip(krow[:, None] - qrow[None, :] + 7, 0, 14)
            coff = np.clip(kcol[:, None] - qcol[None, :], -15, 15) + 15
            m = rin & cin
            for h in range(4):
                vals = rpb[h][roff, coff]
                out[j, h, :, kt, :] = np.where(m, vals, np.float32(NEG))
    return out.reshape(NQT * 4 * 128, NA_KT * 128)


def sw_bias_tables(core):
    out = np.full((NQT, 6, 128, SW_KT, 128), NEG, dtype=np.float32)
    slopes = np.array([2.0 ** (-8.0 * (i + 1) / 6) for i in range(6)], dtype=np.float32)
    qi = np.arange(128)
    for j in range(NQT):
        qpos = (core * 8 + j) * 128 + qi
        for kt in range(SW_KT):
            ktile = core * 8 + j - 1 + kt
            if ktile < 0 or ktile >= 64:
                continue
            kpos = ktile * 128 + np.arange(128)
            dist = np.abs(qpos[None, :] - kpos[:, None])
            m = dist <= 128
            for h in range(6):
                out[j, h, :, kt, :] = np.where(m, -slopes[h] * dist.astype(np.float32), np.float32(NEG))
    return out.reshape(NQT * 6 * 128, SW_KT * 128)


def v_layout(v_tok, nh):
    T = v_tok.shape[0]
    return np.ascontiguousarray(v_tok.reshape(T // 128, 128, nh, 128).transpose(2, 1, 0, 3).reshape(nh * 128, T))


def exchange(outs):
    cat = lambda k, ax: np.concatenate([o[k] for o in outs], axis=ax)
    kaT = cat("kaT", 1)
    kcT = cat("kcT", 1)
    va = v_layout(cat("va", 0), 4)
    vc = v_layout(cat("vc", 0), 2)
    knT = cat("knT", 1)
    kpeT = np.ascontiguousarray(cat("kpeT", 1)[0:64])
    vb = v_layout(cat("vb", 0), 6)

    def win(a, core, halo, n):
        t0 = core * 8 - halo
        res = np.zeros((a.shape[0], n * 128), dtype=a.dtype)
        lo, hi = max(t0, 0), min(t0 + n, 64)
        res[:, (lo - t0) * 128:(hi - t0) * 128] = a[:, lo * 128:hi * 128]
        return res

    res = []
    for i in range(NCORE):
        d = {"B_qaT": outs[i]["qaT"], "B_qcT": outs[i]["qcT"], "B_qnT": outs[i]["qnT"], "B_qpT": outs[i]["qpT"],
             "B_kaT_win": win(kaT, i, NA_HALO, 14), "B_va_win": win(va, i, NA_HALO, 14),
             "B_kcT_win": win(kcT, i, SW_HALO, 10), "B_vc_win": win(vc, i, SW_HALO, 10),
             "B_knT_all": knT, "B_kpeT_all": kpeT, "B_vb_all": vb}
        res.append(d)
    return res


_PROGS = {}


def get_prog(kind):
    if kind not in _PROGS:
        _PROGS[kind] = build(kind)
    return _PROGS[kind]


TRACE = False
LAST = {}


def run(kind, in_maps):
    P = get_prog(kind)
    maps = [{k: np.ascontiguousarray(m[k]) for k in P.dr if k in m} for m in in_maps]
    if TRACE:
        res = run_bass_kernel_spmd(P.nc, maps, core_ids=list(range(NCORE)), trace=True)
        LAST[kind] = res
    else:
        res = run_bass_kernel_spmd(P.nc, maps, core_ids=list(range(NCORE)))
    return res.results


def kernel(**inp):
    inp = {k: np.asarray(v) for k, v in inp.items()}
    x = inp["x"][0]
    xT = np.ascontiguousarray(x.T)
    cT = fm_vec(inp["c"][0])
    ropes = [rope_tables(i) for i in range(NCORE)]
    swb = [sw_bias_tables(i) for i in range(NCORE)]
    com = {"cT": cT}
    com["wada_own"], com["bada_own"] = ada_chunks(inp, [(0, c) for c in range(32)])
    com.update(prep_layer(inp, 0))
    maps = []
    for i in range(NCORE):
        m = dict(com)
        m["xT_in"] = xT[:, i * TPC:(i + 1) * TPC]
        m["cosT"], m["ssinT"] = ropes[i]
        m["wada_shr"], m["bada_shr"] = ada_chunks(inp, ADA_REST[i * NSHARE:(i + 1) * NSHARE])
        maps.append(m)
    r1 = run("A0", maps)
    modfull0, modfull1 = gather_mod(r1)
    ex = exchange(r1)
    com = {"modfull0": modfull0, "modfull1": modfull1}
    com.update(prep_layer_B(inp, 0))
    com.update(prep_layer(inp, 1))
    maps = []
    for i in range(NCORE):
        m = dict(com)
        m.update(ex[i])
        m["xT_in"] = xT[:, i * TPC:(i + 1) * TPC]
        m["cosT"], m["ssinT"] = ropes[i]
        m["B_nab"] = na_bias_tables(inp["na_rpb"][0], i)
        m["B_swb"] = swb[i]
        maps.append(m)
    r2 = run("B0A1", maps)
    ex = exchange(r2)
    com = {"modfull1": modfull1}
    com.update(prep_layer_B(inp, 1))
    maps = []
    for i in range(NCORE):
        m = dict(com)
        m.update(ex[i])
        m["xT_in"] = r2[i]["xT_out"]
        m["B_nab"] = na_bias_tables(inp["na_rpb"][1], i)
        m["B_swb"] = swb[i]
        maps.append(m)
    r3 = run("B1", maps)
    outT = np.concatenate([r["xT_out"] for r in r3], axis=1)
    return np.ascontiguousarray(outT.T)[None].astype(np.float32)
```

```python
import numpy as np
from contextlib import ExitStack, contextmanager
import ml_dtypes
import concourse.bass as bass
import concourse.mybir as mybir
from concourse.bass_utils import run_bass_kernel_spmd

F32 = mybir.dt.float32
BF16 = mybir.dt.bfloat16
AF = mybir.ActivationFunctionType
ALU = mybir.AluOpType
NPBF = ml_dtypes.bfloat16

NCORE = 8
S = 8192
D = 2048
TPC = S // NCORE
NQT = TPC // 128
DFF = 5632
NFC = DFF // 128
ALPHA = float(4 ** 0.25)
LN_EPS = 1e-5
RMS_EPS = 1e-6
NEG = -30000.0
NA_KT = 7
NA_HALO = 3
SW_KT = 3
SW_HALO = 1
MLA_LAG = 3


class Buf:
    __slots__ = ("name", "w", "r", "dsem", "dcnt")

    def __init__(self, name):
        self.name = name
        self.w = None
        self.r = {}
        self.dsem = None
        self.dcnt = 0


class Eng:
    def __init__(self, name, e, sem, is_pe=False):
        self.name = name
        self.e = e
        self.sem = sem
        self.cnt = 0
        self.waited = {}
        self.is_pe = is_pe
        self.pend_r = []
        self.pend_w = []
        self.nwait = 0
        self.nins = 0


class KB:
    def __init__(self, nc, stack):
        self.nc = nc
        self.root = stack
        self.stack = stack
        mk = lambda n: stack.enter_context(nc.semaphore(n))
        self.E = {
            "pe": Eng("pe", nc.tensor, mk("s_pe"), True),
            "act": Eng("act", nc.scalar, mk("s_act")),
            "dve": Eng("dve", nc.vector, mk("s_dve")),
            "pool": Eng("pool", nc.gpsimd, mk("s_pool")),
            "sp": Eng("sp", nc.sync, mk("s_sp")),
        }
        self.nsem = 5
        self.dsems = {}
        self.bufs = {}
        self.uid = 0

    def buf(self, name):
        b = self.bufs.get(name)
        if b is None:
            b = Buf(name)
            self.bufs[name] = b
        return b

    def sem(self, name):
        self.nsem += 1
        return self.root.enter_context(self.nc.semaphore(name))

    def sb(self, name, shape, dt):
        self.uid += 1
        return self.stack.enter_context(self.nc.sbuf_tensor(f"{name}_{self.uid}", list(shape), dt))

    def ps(self, name, shape, dt=F32):
        return self.root.enter_context(self.nc.psum_tensor(name, list(shape), dt))

    @contextmanager
    def scope(self):
        old = self.stack
        with ExitStack() as sub:
            self.stack = sub
            try:
                yield
            finally:
                self.stack = old
            self.barrier()

    def _wait(self, E, deps):
        for sem, val in deps:
            if E.is_pe and sem is E.sem:
                continue
            if E.waited.get(sem.num, 0) >= val:
                continue
            E.e.wait_ge(sem, val)
            E.waited[sem.num] = val
            E.nwait += 1

    @staticmethod
    def _deps(reads, writes):
        deps = []
        for b in reads:
            if b.w is not None:
                deps.append(b.w)
        for b in writes:
            deps.extend(b.r.values())
            if b.w is not None:
                deps.append(b.w)
        return deps

    def op(self, en, fn, reads=(), writes=(), inc=True):
        E = self.E[en]
        self._wait(E, self._deps(reads, writes))
        ins = fn(E.e)
        E.nins += 1
        if not inc:
            assert E.is_pe
            E.pend_r.extend(reads)
            E.pend_w.extend(writes)
            return ins
        E.cnt += 1
        ins.then_inc(E.sem, 1)
        comp = (E.sem, E.cnt)
        rr = list(reads) + E.pend_r
        ww = list(writes) + E.pend_w
        E.pend_r = []
        E.pend_w = []
        for b in rr:
            b.r[E.sem.num] = comp
        for b in ww:
            b.w = comp
            b.r = {}
        return ins

    def dma(self, qn, out, in_, reads=(), writes=(), owner=None):
        E = self.E[qn]
        self._wait(E, self._deps(reads, writes))
        if owner is None:
            owner = (list(writes) + list(reads))[0]
        if owner.dsem is None:
            owner.dsem = self.sem("d_" + owner.name)
        ins = E.e.dma_start(out=out, in_=in_)
        owner.dcnt += 16
        ins.then_inc(owner.dsem, 16)
        comp = (owner.dsem, owner.dcnt)
        self.dsems[owner.dsem.num] = comp
        for b in reads:
            b.r[owner.dsem.num] = comp
        for b in writes:
            b.w = comp
            b.r = {}
        E.nins += 1
        return ins

    def barrier(self):
        assert not self.E["pe"].pend_r and not self.E["pe"].pend_w
        deps = [(X.sem, X.cnt) for X in self.E.values() if X.cnt > 0]
        deps += list(self.dsems.values())
        for X in self.E.values():
            self._wait(X, [d for d in deps if d[0] is not X.sem])

    def finish(self):
        self.barrier()

    def stats(self):
        return {n: (X.nins, X.nwait) for n, X in self.E.items()}


class Ring:
    def __init__(self, kb, name, shape, dt, n):
        self.t = [kb.sb(f"{name}{i}", shape, dt) for i in range(n)]
        self.b = [kb.buf(f"{name}{i}") for i in range(n)]
        self.i = 0

    def next(self):
        i = self.i
        self.i = (i + 1) % len(self.t)
        return self.t[i], self.b[i]


class Prog:
    def __init__(self, kind):
        self.kind = kind
        self.nc = bass.Bass("TRN2", target_bir_lowering=False)
        self.dr = {}

    def din(self, name, shape, dt=F32):
        ap = self.nc.dram_tensor(name, list(shape), dt, kind="ExternalInput").ap()
        self.dr[name] = ap
        return ap

    def dout(self, name, shape, dt=F32):
        ap = self.nc.dram_tensor(name, list(shape), dt, kind="ExternalOutput").ap()
        self.dr[name] = ap
        return ap


def c1024(c, a, b):
    return slice(c * 1024 + a, c * 1024 + b)


def build(kind):
    P = Prog(kind)
    nc = P.nc
    doA = {"A0": 0, "B0A1": 1, "B1": None}[kind]
    doB = {"A0": None, "B0A1": 0, "B1": 1}[kind]
    with ExitStack() as st:
        st.enter_context(nc.allow_low_precision("bf16 matmul operands by design"))
        kb = KB(nc, st)
        P.kb = kb
        xT = kb.sb("xT", [128, 16 * 1024], F32)
        xB = [[kb.buf(f"x{c}_{tb}") for tb in range(2)] for c in range(16)]
        PS = kb.ps("PS", [128, 8 * 512])
        psB = [kb.buf(f"ps{i}") for i in range(8)]
        P.bank = lambda b, n=512, o=0, p=128: PS[0:p, b * 512 + o:b * 512 + o + n]
        ones = kb.sb("ones", [128, 128], BF16)
        onesB = kb.buf("ones")
        kb.op("dve", lambda e: e.memset(ones[:], 1.0), writes=[onesB])
        ones32 = kb.sb("ones32", [128, 128], F32)
        P.ones32, P.ones32B = ones32, kb.buf("ones32")
        kb.op("dve", lambda e: e.memset(ones32[:], 1.0), writes=[P.ones32B])
        tmpR = Ring(kb, "tmp", [128, 512], F32, 6)
        tbfR = Ring(kb, "tbf", [128, 512], BF16, 4)
        P.xT, P.xB, P.psB, P.ones, P.onesB, P.tmpR, P.tbfR = xT, xB, psB, ones, onesB, tmpR, tbfR
        P.grot = 0

        xin = P.din("xT_in", [D, TPC])
        xl = kb.buf("xload")
        for c in range(16):
            kb.dma("sp", xT[:, c * 1024:(c + 1) * 1024], xin[c * 128:(c + 1) * 128, :],
                   writes=[xB[c][0], xB[c][1]], owner=xl)
        for c in range(16):
            for tb in range(2):
                xB[c][tb].w = (xl.dsem, xl.dcnt)

        P.mod = {}
        for l in sorted({x for x in (doA, doB) if x is not None}):
            mod = kb.sb(f"mod{l}", [128, 96], F32)
            modp = kb.sb(f"modp{l}", [128, 96], F32)
            P.mod[l] = (mod, modp, kb.buf(f"mod{l}"))
        with nc.named_scope('ada'):
            if kind == "A0":
                emit_ada(P)
            else:
                for l in P.mod:
                    emit_mod_load(P, l)

        if doB is not None:
            emit_B(P, doB)
        if doA is not None:
            with nc.named_scope('phaseA'):
                emit_A(P, doA)
        if kind == "B1" or kind == "B0A1":
            xo = P.dout("xT_out", [D, TPC])
            xs = kb.buf("xstore")
            for c in range(16):
                kb.dma("sp", xo[c * 128:(c + 1) * 128, :], xT[:, c * 1024:(c + 1) * 1024],
                       reads=[xB[c][0], xB[c][1]], owner=xs)
        kb.finish()
        P.stats = kb.stats()
        P.nsem = kb.nsem
    return P


def gbank(P):
    b = P.grot
    P.grot = (b + 1) % 4
    return b


NSHARE = 20


def emit_ada(P):
    kb, nc = P.kb, P.nc
    cin = P.din("cT", [128, 16])
    mso = P.dout("modshare", [128, NSHARE])
    with kb.scope():
        cT = kb.sb("cT", [128, 16], F32)
        cB = kb.buf("cT")
        cond = kb.sb("cond", [128, 16], F32)
        condB = kb.buf("cond")
        kb.dma("sp", cT[:], cin, writes=[cB])
        kb.op("act", lambda e: e.activation(out=cond[:], in_=cT[:], func=AF.Silu), reads=[cB], writes=[condB])
        wr = Ring(kb, "wada", [128, 2048], F32, 4)
        mod, modp, mB = P.mod[0]
        shr = kb.sb("mshare", [128, NSHARE], F32)
        shB = kb.buf("mshare")
        bank = 7
        col0 = 0
        for (wname, bname, ncol, dst, dB, dcol) in (("wada_own", "bada_own", 32, mod, mB, 0), ("wada_shr", "bada_shr", NSHARE, shr, shB, 0)):
            wd = P.din(wname, [ncol * 128, 2048])
            bd = P.din(bname, [128, ncol])
            bt = kb.sb(bname, [128, ncol], F32)
            btB = kb.buf(bname)
            kb.dma("sp", bt[:], bd, writes=[btB])
            for i in range(ncol):
                wt, wB = wr.next()
                kb.dma("sp", wt[:], wd[i * 128:(i + 1) * 128, :], writes=[wB])
                for kc in range(16):
                    kb.op("pe", lambda e: e.matmul(P.bank(bank, 1, col0 + i), wt[:, kc * 128:(kc + 1) * 128],
                                                   cond[:, kc:kc + 1], start=(kc == 0), stop=(kc == 15)),
                          reads=[wB, condB], writes=[P.psB[bank]], inc=(kc == 15))
            kb.op("dve", lambda e: e.tensor_tensor(out=dst[:, 0:ncol], in0=P.bank(bank, ncol, col0), in1=bt[:, 0:ncol], op=ALU.add),
                  reads=[P.psB[bank], btB], writes=[dB])
            col0 += ncol
        kb.op("dve", lambda e: e.tensor_scalar_add(out=modp[:, 0:32], in0=mod[:, 0:32], scalar1=1.0), reads=[mB], writes=[mB])
        kb.dma("sp", mso, shr[:], reads=[shB], owner=shB)


def emit_mod_load(P, l):
    kb = P.kb
    md = P.din(f"modfull{l}", [128, 96])
    mod, modp, mB = P.mod[l]
    kb.dma("sp", mod[:], md, writes=[mB])
    kb.op("dve", lambda e: e.tensor_scalar_add(out=modp[:], in0=mod[:], scalar1=1.0), reads=[mB], writes=[mB])


def mcol(P, l, j, c, plus1=False):
    mod, modp, _ = P.mod[l]
    t = modp if plus1 else mod
    return t[:, j * 16 + c:j * 16 + c + 1]


def emit_rstd(P, psbank, scale, eps, out_t, out_b):
    kb = P.kb
    t, tB = P.tmpR.next()
    kb.op("act", lambda e: e.activation(out=t[:], in_=P.bank(psbank), func=AF.Ln, bias=eps, scale=scale),
          reads=[P.psB[psbank]], writes=[tB])
    kb.op("act", lambda e: e.activation(out=out_t, in_=t[:], func=AF.Exp, scale=-0.5), reads=[tB], writes=[out_b])


def emit_A(P, l):
    kb, nc = P.kb, P.nc
    xT, xB = P.xT, P.xB
    mB = P.mod[l][2]
    o = {}
    for name, shp in [("qaT", [512, TPC]), ("kaT", [512, TPC]), ("va", [TPC, 512]),
                      ("qcT", [768, TPC]), ("kcT", [256, TPC]), ("vc", [TPC, 256]),
                      ("qnT", [768, TPC]), ("qpT", [384, TPC]), ("knT", [768, TPC]),
                      ("kpeT", [128, TPC]), ("vb", [TPC, 768])]:
        o[name] = P.dout(f"{name}", shp, BF16)
    win = P.din(f"winfm{l}", [24 * 128, 2048])
    wv = P.din(f"winv{l}", [128, 16 * 768])
    wuqn = P.din(f"wuqn{l}", [128, 4 * 768])
    wuqr = P.din(f"wuqr{l}", [128, 4 * 384])
    wuqs = P.din(f"wuqs{l}", [128, 4 * 384])
    wukk = P.din(f"wukk{l}", [128, 2 * 768])
    wukv = P.din(f"wukv{l}", [128, 2 * 768])
    gq = P.din(f"gq{l}", [128, 4])
    gkv = P.din(f"gkv{l}", [128, 2])
    cosd = P.din("cosT", [128, TPC])
    sind = P.din("ssinT", [128, TPC])
    with kb.scope():
        uT = kb.sb("uT", [128, 16 * 1024], BF16)
        uB = [[kb.buf(f"u{c}_{tb}") for tb in range(2)] for c in range(16)]
        for c in range(16):
            for tb in range(2):
                kb.op("dve", lambda e: e.tensor_scalar(out=uT[:, c1024(c, tb * 512, tb * 512 + 512)],
                                                       in0=xT[:, c1024(c, tb * 512, tb * 512 + 512)],
                                                       scalar1=mcol(P, l, 1, c, True), scalar2=mcol(P, l, 0, c),
                                                       op0=ALU.mult, op1=ALU.add),
                      reads=[xB[c][tb], mB], writes=[uB[c][tb]])
        with kb.scope():
            wvt = kb.sb("wv", [128, 16 * 768], BF16)
            wvB = kb.buf("A_wv")
            kb.dma("pool", wvt[:], wv, writes=[wvB])
            st2R = Ring(kb, "stg2", [128, 768], BF16, 2)
            for tt in range(8):
                tb, off = tt // 4, (tt % 4) * 128 + (tt // 4) * 512
                b1, b2 = gbank(P), gbank(P)
                for (bk, c0, n) in ((b1, 0, 512), (b2, 512, 256)):
                    for kc in range(16):
                        kb.op("pe", lambda e: e.matmul(P.bank(bk, n), uT[:, c1024(kc, off, off + 128)],
                                                       wvt[:, kc * 768 + c0:kc * 768 + c0 + n],
                                                       start=(kc == 0), stop=(kc == 15)),
                              reads=[wvB, uB[kc][tb]], writes=[P.psB[bk]], inc=(kc == 15))
                stg, sB = st2R.next()
                kb.op("act", lambda e: e.activation(out=stg[:, 0:512], in_=P.bank(b1), func=AF.Copy), reads=[P.psB[b1]], writes=[sB])
                kb.op("act", lambda e: e.activation(out=stg[:, 512:768], in_=P.bank(b2, 256), func=AF.Copy), reads=[P.psB[b2]], writes=[sB])
                kb.dma("sp", o["va"][tt * 128:(tt + 1) * 128, :], stg[:, 0:512], reads=[sB], owner=sB)
                kb.dma("sp", o["vc"][tt * 128:(tt + 1) * 128, :], stg[:, 512:768], reads=[sB], owner=sB)
        sm = {}
        for nm, dd, shp in [("gq", gq, [128, 4]), ("gkv", gkv, [128, 2]), ("cos", cosd, [128, TPC]), ("sin", sind, [128, TPC])]:
            t = kb.sb(nm, shp, F32)
            b = kb.buf("A_" + nm)
            kb.dma("sp", t[:], dd, writes=[b])
            sm[nm] = (t, b)
        wsm = {}
        for nm, dd, shp in [("wuqn", wuqn, [128, 4 * 768]), ("wuqr", wuqr, [128, 4 * 384]), ("wuqs", wuqs, [128, 4 * 384]),
                            ("wukk", wukk, [128, 2 * 768]), ("wukv", wukv, [128, 2 * 768])]:
            t = kb.sb(nm, shp, BF16)
            b = kb.buf("A_" + nm)
            kb.dma("pool", t[:], dd, writes=[b])
            wsm[nm] = (t, b)
        cqT = kb.sb("cqT", [128, 4 * 1024], F32)
        ckvT = kb.sb("ckvT", [128, 2 * 1024], F32)
        krT = kb.sb("krT", [128, 2 * 1024], F32)
        cqB = [[kb.buf(f"cq{c}_{tb}") for tb in range(2)] for c in range(4)]
        ckvB = [[kb.buf(f"ckv{c}_{tb}") for tb in range(2)] for c in range(2)]
        krB = [[kb.buf(f"kr{c}_{tb}") for tb in range(2)] for c in range(2)]
        wR = Ring(kb, "win", [128, 2048], BF16, 3)
        stR = Ring(kb, "stg", [128, 1024], BF16, 3)
        dest = ([("qaT", i) for i in range(4)] + [("kaT", i) for i in range(4)] + [("qcT", i) for i in range(6)] +
                [("kcT", i) for i in range(2)] + [("cq", i) for i in range(4)] + [("ckv", i) for i in range(2)] +
                [("kr", 0), ("kr", 1)])
        for oc, (dn, di) in enumerate(dest):
            wt, wB = wR.next()
            kb.dma("pool", wt[:], win[oc * 128:(oc + 1) * 128, :], writes=[wB])
            if dn in o:
                stg, sB = stR.next()
            for tb in range(2):
                bk = gbank(P)
                for kc in range(16):
                    kb.op("pe", lambda e: e.matmul(P.bank(bk), wt[:, kc * 128:(kc + 1) * 128],
                                                   uT[:, c1024(kc, tb * 512, tb * 512 + 512)],
                                                   start=(kc == 0), stop=(kc == 15)),
                          reads=[wB, uB[kc][tb]], writes=[P.psB[bk]], inc=(kc == 15))
                if dn in o:
                    kb.op("act", lambda e: e.activation(out=stg[:, tb * 512:(tb + 1) * 512], in_=P.bank(bk), func=AF.Copy),
                          reads=[P.psB[bk]], writes=[sB])
                else:
                    tt, bb = {"cq": (cqT, cqB), "ckv": (ckvT, ckvB), "kr": (krT, krB)}[dn]
                    kb.op("act", lambda e: e.activation(out=tt[:, c1024(di, tb * 512, tb * 512 + 512)], in_=P.bank(bk), func=AF.Copy),
                          reads=[P.psB[bk]], writes=[bb[di][tb]])
            if dn in o:
                kb.dma("sp", o[dn][di * 128:(di + 1) * 128, :], stg[:], reads=[sB], owner=sB)
        cqn = kb.sb("cqn", [128, 4 * 1024], BF16)
        ckvn = kb.sb("ckvn", [128, 2 * 1024], BF16)
        cqnB = [kb.buf(f"cqn{tb}") for tb in range(2)]
        ckvnB = [kb.buf(f"ckvn{tb}") for tb in range(2)]
        rstd = kb.sb("rstdA", [128, 512], F32)
        rB = kb.buf("rstdA")
        for (src, sB_, nchunk, dst, dB, gname) in ((cqT, cqB, 4, cqn, cqnB, "gq"), (ckvT, ckvB, 2, ckvn, ckvnB, "gkv")):
            gt, gB = sm[gname]
            for tb in range(2):
                bk = gbank(P)
                for c in range(nchunk):
                    t, tB = P.tbfR.next()
                    kb.op("act", lambda e: e.activation(out=t[:], in_=src[:, c1024(c, tb * 512, tb * 512 + 512)], func=AF.Square),
                          reads=[sB_[c][tb]], writes=[tB])
                    kb.op("pe", lambda e: e.matmul(P.bank(bk), P.ones[:], t[:], start=(c == 0), stop=(c == nchunk - 1)),
                          reads=[tB, P.onesB], writes=[P.psB[bk]], inc=True)
                emit_rstd(P, bk, 1.0 / (nchunk * 128), RMS_EPS, rstd[:], rB)
                for c in range(nchunk):
                    kb.op("dve", lambda e: e.scalar_tensor_tensor(out=dst[:, c1024(c, tb * 512, tb * 512 + 512)],
                                                                  in0=src[:, c1024(c, tb * 512, tb * 512 + 512)],
                                                                  scalar=gt[:, c:c + 1], in1=rstd[:],
                                                                  op0=ALU.mult, op1=ALU.mult),
                          reads=[sB_[c][tb], gB, rB], writes=[dB[tb]])
        cost, cosB = sm["cos"]
        sint, sinB = sm["sin"]

        def rope(out_ap, a_ap, s_ap, rows, tb, reads, wbuf):
            t1, t1B = P.tmpR.next()
            t2, t2B = P.tmpR.next()
            kb.op("dve", lambda e: e.tensor_tensor(out=t1[0:rows, :], in0=a_ap, in1=cost[0:rows, tb * 512:(tb + 1) * 512], op=ALU.mult),
                  reads=reads + [cosB], writes=[t1B])
            kb.op("dve", lambda e: e.tensor_tensor(out=t2[0:rows, :], in0=s_ap, in1=sint[0:rows, tb * 512:(tb + 1) * 512], op=ALU.mult),
                  reads=reads + [sinB], writes=[t2B])
            kb.op("dve", lambda e: e.tensor_tensor(out=out_ap, in0=t1[0:rows, :], in1=t2[0:rows, :], op=ALU.add),
                  reads=[t1B, t2B], writes=[wbuf])

        stg, sB = stR.next()
        for tb in range(2):
            rope(stg[:, tb * 512:(tb + 1) * 512], krT[:, c1024(0, tb * 512, tb * 512 + 512)],
                 krT[:, c1024(1, tb * 512, tb * 512 + 512)], 128, tb, [krB[0][tb], krB[1][tb]], sB)
        kb.dma("sp", o["kpeT"], stg[:], reads=[sB], owner=sB)
        for (wname, nk, srcn, srcB, oname) in (("wuqn", 4, cqn, cqnB, "qnT"), ("wukk", 2, ckvn, ckvnB, "knT")):
            wt, wB = wsm[wname]
            for h in range(6):
                stg, sB = stR.next()
                for tb in range(2):
                    bk = gbank(P)
                    for kc in range(nk):
                        kb.op("pe", lambda e: e.matmul(P.bank(bk), wt[:, kc * 768 + h * 128:kc * 768 + (h + 1) * 128],
                                                       srcn[:, c1024(kc, tb * 512, tb * 512 + 512)],
                                                       start=(kc == 0), stop=(kc == nk - 1)),
                              reads=[wB, srcB[tb]], writes=[P.psB[bk]], inc=(kc == nk - 1))
                    kb.op("act", lambda e: e.activation(out=stg[:, tb * 512:(tb + 1) * 512], in_=P.bank(bk), func=AF.Copy),
                          reads=[P.psB[bk]], writes=[sB])
                kb.dma("sp", o[oname][h * 128:(h + 1) * 128, :], stg[:], reads=[sB], owner=sB)
        wr_, wrB = wsm["wuqr"]
        ws_, wsB = wsm["wuqs"]
        for h in range(6):
            stg, sB = stR.next()
            for tb in range(2):
                b1, b2 = gbank(P), gbank(P)
                for (bk, wt, wB) in ((b1, wr_, wrB), (b2, ws_, wsB)):
                    for kc in range(4):
                        kb.op("pe", lambda e: e.matmul(P.bank(bk, 512, 0, 64), wt[:, kc * 384 + h * 64:kc * 384 + (h + 1) * 64],
                                                       cqn[:, c1024(kc, tb * 512, tb * 512 + 512)],
                                                       start=(kc == 0), stop=(kc == 3)),
                              reads=[wB, cqnB[tb]], writes=[P.psB[bk]], inc=(kc == 3))
                t3, t3B = P.tmpR.next()
                kb.op("act", lambda e: e.activation(out=t3[0:64, :], in_=P.bank(b2, 512, 0, 64), func=AF.Copy), reads=[P.psB[b2]], writes=[t3B])
                rope(stg[0:64, tb * 512:(tb + 1) * 512], P.bank(b1, 512, 0, 64), t3[0:64, :], 64, tb, [P.psB[b1], t3B], sB)
            kb.dma("sp", o["qpT"][h * 64:(h + 1) * 64, :], stg[0:64, :], reads=[sB], owner=sB)
        st2R = Ring(kb, "stg3", [128, 768], BF16, 2)
        wt, wB = wsm["wukv"]
        for tt in range(8):
            tb, off = tt // 4, (tt % 4) * 128 + (tt // 4) * 512
            b1, b2 = gbank(P), gbank(P)
            for (bk, c0, n) in ((b1, 0, 512), (b2, 512, 256)):
                for kc in range(2):
                    kb.op("pe", lambda e: e.matmul(P.bank(bk, n), ckvn[:, c1024(kc, off, off + 128)],
                                                   wt[:, kc * 768 + c0:kc * 768 + c0 + n],
                                                   start=(kc == 0), stop=(kc == 1)),
                          reads=[wB, ckvnB[tb]], writes=[P.psB[bk]], inc=(kc == 1))
            stg, sB = st2R.next()
            kb.op("act", lambda e: e.activation(out=stg[:, 0:512], in_=P.bank(b1), func=AF.Copy), reads=[P.psB[b1]], writes=[sB])
            kb.op("act", lambda e: e.activation(out=stg[:, 512:768], in_=P.bank(b2, 256), func=AF.Copy), reads=[P.psB[b2]], writes=[sB])
            kb.dma("sp", o["vb"][tt * 128:(tt + 1) * 128, :], stg[:], reads=[sB], owner=sB)


def emit_local_attn(P, tb, yT, yB, spec):
    kb = P.kb
    nkt = spec["nkt"]
    nwin = 4 + nkt - 1
    W = nkt * 128
    kR = Ring(kb, spec["n"] + "k", [128, nwin * 128], BF16, 2)
    vR = Ring(kb, spec["n"] + "v", [128, nwin * 128], BF16, 2)
    qR = Ring(kb, spec["n"] + "q", [128, 512], BF16, 2)
    bR = Ring(kb, spec["n"] + "b", [128, W], F32, 3)
    sR = Ring(kb, spec["n"] + "s", [128, W], F32, 3)
    pR = Ring(kb, spec["n"] + "p", [128, W], BF16, 4)
    rR = Ring(kb, spec["n"] + "r", [128, 128], F32, 3)
    class _Rot:
        def __init__(self, lst):
            self.l = lst

        def __getitem__(self, k):
            return self.l[k % len(self.l)]
    NS = 6
    oB = _Rot([P.psB[4], P.psB[5], P.psB[6]])
    dB = _Rot([P.psB[7]])
    oAP = lambda i: P.bank(4 + (i % 3), 128)
    dAP = lambda i: P.bank(7, 128)
    state = {"kv": None}
    its = []

    def stA(it):
        h, qt, i = it["h"], it["qt"], it["i"]
        kvh = spec["kvh"](h)
        if kvh != state["kv"]:
            kt_, kB_ = kR.next()
            vt_, vB_ = vR.next()
            kb.dma("sp", kt_[:], spec["kd"][kvh * 128:(kvh + 1) * 128, tb * 512:tb * 512 + nwin * 128], writes=[kB_])
            kb.dma("sp", vt_[:], spec["vd"][kvh * 128:(kvh + 1) * 128, tb * 512:tb * 512 + nwin * 128], writes=[vB_])
            state["kv"] = kvh
            state["kvt"] = (kt_, kB_, vt_, vB_)
        kt_, kB_, vt_, vB_ = state["kvt"]
        if qt == 0:
            qt_, qB_ = qR.next()
            kb.dma("sp", qt_[:], spec["qd"][h * 128:(h + 1) * 128, tb * 512:(tb + 1) * 512], writes=[qB_])
            state["q"] = (qt_, qB_)
        qt_, qB_ = state["q"]
        j = tb * 4 + qt
        bt_, bB_ = bR.next()
        row = (j * spec["nh"] + h) * 128
        kb.dma("sp", bt_[:], spec["bd"][row:row + 128, :], writes=[bB_])
        base = (i % 2) * 2
        for kt in range(nkt):
            bk = base + (kt // 4)
            kb.op("pe", lambda e: e.matmul(P.bank(bk, 128, (kt % 4) * 128),
                                           kt_[:, (qt + kt) * 128:(qt + kt + 1) * 128],
                                           qt_[:, qt * 128:(qt + 1) * 128], start=True, stop=True),
                  reads=[kB_, qB_], writes=[P.psB[bk]], inc=(kt == nkt - 1 or kt == 3))
        st_, sB_ = sR.next()
        n0 = min(W, 512)
        kb.op("dve", lambda e: e.scalar_tensor_tensor(out=st_[:, 0:n0], in0=P.bank(base, n0), scalar=spec["scale"],
                                                      in1=bt_[:, 0:n0], op0=ALU.mult, op1=ALU.add),
              reads=[P.psB[base], bB_], writes=[sB_])
        if W > 512:
            kb.op("dve", lambda e: e.scalar_tensor_tensor(out=st_[:, 512:W], in0=P.bank(base + 1, W - 512), scalar=spec["scale"],
                                                          in1=bt_[:, 512:W], op0=ALU.mult, op1=ALU.add),
                  reads=[P.psB[base + 1], bB_], writes=[sB_])
        it.update(st=st_, sB=sB_, vt=vt_, vB=vB_)

    def stB(it):
        pt_, pB_ = pR.next()
        st_, sB_ = it["st"], it["sB"]
        kb.op("act", lambda e: e.activation(out=pt_[:], in_=st_[:], func=AF.Exp), reads=[sB_], writes=[pB_])
        it.update(pt=pt_, pB=pB_)

    def stC(it):
        i, qt, vt_, vB_, pt_, pB_ = it["i"], it["qt"], it["vt"], it["vB"], it["pt"], it["pB"]
        for kt in range(nkt):
            kb.op("pe", lambda e: e.matmul(oAP(i), vt_[:, (qt + kt) * 128:(qt + kt + 1) * 128],
                                           pt_[:, kt * 128:(kt + 1) * 128], start=(kt == 0), stop=(kt == nkt - 1)),
                  reads=[vB_, pB_], writes=[oB[i % NS]], inc=(kt == nkt - 1))
        for kt in range(nkt):
            kb.op("pe", lambda e: e.matmul(dAP(i), P.ones[:], pt_[:, kt * 128:(kt + 1) * 128],
                                           start=(kt == 0), stop=(kt == nkt - 1)),
                  reads=[P.onesB, pB_], writes=[dB[i % NS]], inc=(kt == nkt - 1))

    def stD(it):
        i, h = it["i"], it["h"]
        rt_, rB_ = rR.next()
        if spec["esink"] is not None:
            es, esB = spec["esink"]
            kb.op("act", lambda e: e.activation(out=rt_[:], in_=dAP(i), func=AF.Ln, bias=es[:, h:h + 1], scale=1.0),
                  reads=[dB[i % NS], esB], writes=[rB_])
        else:
            kb.op("act", lambda e: e.activation(out=rt_[:], in_=dAP(i), func=AF.Ln), reads=[dB[i % NS]], writes=[rB_])
        kb.op("act", lambda e: e.activation(out=rt_[:], in_=rt_[:], func=AF.Exp, scale=-1.0), reads=[rB_], writes=[rB_])
        it.update(rt=rt_, rB=rB_)

    def stE(it):
        i, h, qt, rt_, rB_ = it["i"], it["h"], it["qt"], it["rt"], it["rB"]
        ch = spec["ychunk0"] + h
        kb.op("dve", lambda e: e.tensor_tensor(out=yT[:, ch * 512 + qt * 128:ch * 512 + (qt + 1) * 128],
                                               in0=oAP(i), in1=rt_[:], op=ALU.mult),
              reads=[oB[i % NS], rB_], writes=[yB[ch]])

    for h in range(spec["nh"]):
        for qt in range(4):
            its.append(dict(h=h, qt=qt, i=len(its)))
    n = len(its)
    stages = [stA, stB, stC, stD, stE]
    for t in range(n + len(stages) - 1):
        for si in reversed(range(len(stages))):
            k = t - si
            if 0 <= k < n:
                stages[si](its[k])


def emit_mla(P, tb, yT, yB, dd, kpe, kpeB):
    kb = P.kb
    scale = float(192 ** -0.5)
    knR = Ring(kb, "mkn", [128, 4096], BF16, 2)
    vR = Ring(kb, "mv", [128, 4096], BF16, 2)
    qnR = Ring(kb, "mqn", [128, 512], BF16, 2)
    qpR = Ring(kb, "mqp", [128, 512], BF16, 2)
    for t_, b_ in zip(qpR.t, qpR.b):
        kb.op("pool", lambda e: e.memset(t_[64:128, :], 0.0), writes=[kb.buf(b_.name + "_z")])
    pR = Ring(kb, "mp", [128, 512], BF16, 8)
    accR = Ring(kb, "macc", [128, 512], F32, 2)
    prR = Ring(kb, "mpr", [128, 512], BF16, 2)
    rt_ = kb.sb("mr", [128, 512], F32)
    rB_ = kb.buf("mr")
    srot = 0
    for h in range(6):
        qn, qnB = qnR.next()
        qp, qpB = qpR.next()
        kb.dma("sp", qn[:], dd["qnT"][h * 128:(h + 1) * 128, tb * 512:(tb + 1) * 512], writes=[qnB])
        kb.dma("sp", qp[0:64, :], dd["qpT"][h * 64:(h + 1) * 64, tb * 512:(tb + 1) * 512], writes=[qpB])
        qpZ = kb.buf(qpB.name + "_z")
        ob, db = 4 + (h % 2), 6 + (h % 2)
        acc, accB = accR.next()
        stash = []
        pendq = []

        def pv(item, first, last):
            pt, pB, kt, vt, vB = item
            kb.op("pe", lambda e: e.matmul(P.bank(ob), vt[:, (kt % 32) * 128:(kt % 32 + 1) * 128], pt[:], start=first, stop=last),
                  reads=[vB, pB], writes=[P.psB[ob]], inc=True)
            if kt % 2 == 0:
                stash.append((pt, pB))
            else:
                p0, p0B = stash.pop()
                t2, t2B = prR.next()
                kb.op("dve", lambda e: e.tensor_tensor(out=t2[:], in0=p0[:], in1=pt[:], op=ALU.add), reads=[p0B, pB], writes=[t2B])
                if kt == 1:
                    kb.op("dve", lambda e: e.tensor_copy(out=acc[:], in_=t2[:]), reads=[t2B], writes=[accB])
                else:
                    kb.op("dve", lambda e: e.tensor_tensor(out=acc[:], in0=acc[:], in1=t2[:], op=ALU.add), reads=[t2B, accB], writes=[accB])

        for half in range(2):
            kn, knB = knR.next()
            vt, vB = vR.next()
            kb.dma("sp", kn[:], dd["knT_all"][h * 128:(h + 1) * 128, half * 4096:(half + 1) * 4096], writes=[knB])
            kb.dma("sp", vt[:], dd["vb_all"][h * 128:(h + 1) * 128, half * 4096:(half + 1) * 4096], writes=[vB])
            for k32 in range(32):
                kt = half * 32 + k32
                bk = srot % 4
                srot += 1
                kb.op("pe", lambda e: e.matmul(P.bank(bk), kn[:, k32 * 128:(k32 + 1) * 128], qn[:], start=True, stop=False),
                      reads=[knB, qnB], writes=[P.psB[bk]], inc=False)
                kb.op("pe", lambda e: e.matmul(P.bank(bk), kpe[:, kt * 128:(kt + 1) * 128], qp[:], start=False, stop=True),
                      reads=kpeB + [qpB, qpZ], writes=[P.psB[bk]], inc=True)
                pt, pB = pR.next()
                kb.op("act", lambda e: e.activation(out=pt[:], in_=P.bank(bk), func=AF.Exp, scale=scale),
                      reads=[P.psB[bk]], writes=[pB])
                pendq.append((pt, pB, kt, vt, vB))
                if len(pendq) > MLA_LAG:
                    it = pendq.pop(0)
                    pv(it, it[2] == 0, False)
        while pendq:
            it = pendq.pop(0)
            pv(it, it[2] == 0, it[2] == 63)
        kb.op("pe", lambda e: e.matmul(P.bank(db), P.ones32[:], acc[:], start=True, stop=True),
              reads=[P.ones32B, accB], writes=[P.psB[db]], inc=True)
        kb.op("act", lambda e: e.activation(out=rt_[:], in_=P.bank(db), func=AF.Ln), reads=[P.psB[db]], writes=[rB_])
        kb.op("act", lambda e: e.activation(out=rt_[:], in_=rt_[:], func=AF.Exp, scale=-1.0), reads=[rB_], writes=[rB_])
        ch = 4 + h
        kb.op("dve", lambda e: e.tensor_tensor(out=yT[:, ch * 512:(ch + 1) * 512], in0=P.bank(ob), in1=rt_[:], op=ALU.mult),
              reads=[P.psB[ob], rB_], writes=[yB[ch]])


def emit_ln(P, l, tb, gt, bt, gbB):
    kb = P.kb
    xT, xB = P.xT, P.xB
    sb_, qb_ = 4, 6
    for c in range(16):
        t1, t1B = P.tbfR.next()
        t2, t2B = P.tbfR.next()
        xs = xT[:, c1024(c, tb * 512, tb * 512 + 512)]
        kb.op("act", lambda e: e.activation(out=t1[:], in_=xs, func=AF.Copy), reads=[xB[c][tb]], writes=[t1B])
        kb.op("act", lambda e: e.activation(out=t2[:], in_=xs, func=AF.Square), reads=[xB[c][tb]], writes=[t2B])
        kb.op("pe", lambda e: e.matmul(P.bank(sb_), P.ones[:], t1[:], start=(c == 0), stop=(c == 15)),
              reads=[P.onesB, t1B], writes=[P.psB[sb_]], inc=True)
        kb.op("pe", lambda e: e.matmul(P.bank(qb_), P.ones[:], t2[:], start=(c == 0), stop=(c == 15)),
              reads=[P.onesB, t2B], writes=[P.psB[qb_]], inc=True)
    mean, mB = P.tmpR.next()
    msq, qB = P.tmpR.next()
    var, vB = P.tmpR.next()
    rstd, rB = P.tmpR.next()
    kb.op("act", lambda e: e.mul(out=mean[:], in_=P.bank(sb_), mul=1.0 / D), reads=[P.psB[sb_]], writes=[mB])
    kb.op("dve", lambda e: e.tensor_tensor(out=msq[:], in0=mean[:], in1=mean[:], op=ALU.mult), reads=[mB], writes=[qB])
    kb.op("dve", lambda e: e.scalar_tensor_tensor(out=var[:], in0=P.bank(qb_), scalar=1.0 / D, in1=msq[:],
                                                  op0=ALU.mult, op1=ALU.subtract),
          reads=[P.psB[qb_], qB], writes=[vB])
    t, tB = P.tmpR.next()
    kb.op("act", lambda e: e.activation(out=t[:], in_=var[:], func=AF.Ln, bias=LN_EPS, scale=1.0), reads=[vB], writes=[tB])
    kb.op("act", lambda e: e.activation(out=rstd[:], in_=t[:], func=AF.Exp, scale=-0.5), reads=[tB], writes=[rB])
    for c in range(16):
        xs = xT[:, c1024(c, tb * 512, tb * 512 + 512)]
        kb.op("dve", lambda e: e.tensor_tensor(out=xs, in0=xs, in1=mean[:], op=ALU.subtract), reads=[xB[c][tb], mB], writes=[xB[c][tb]])
        kb.op("dve", lambda e: e.tensor_tensor(out=xs, in0=xs, in1=rstd[:], op=ALU.mult), reads=[xB[c][tb], rB], writes=[xB[c][tb]])
        kb.op("act", lambda e: e.activation(out=xs, in_=xs, func=AF.Identity, scale=gt[:, c:c + 1], bias=bt[:, c:c + 1]),
              reads=[xB[c][tb], gbB], writes=[xB[c][tb]])


def emit_resid(P, l, jg, oc, tb, bk):
    kb = P.kb
    xs = P.xT[:, c1024(oc, tb * 512, tb * 512 + 512)]
    t, tB = P.tmpR.next()
    kb.op("act", lambda e: e.mul(out=t[:], in_=xs, mul=ALPHA), reads=[P.xB[oc][tb]], writes=[tB])
    kb.op("dve", lambda e: e.scalar_tensor_tensor(out=xs, in0=P.bank(bk), scalar=mcol(P, l, jg, oc, True), in1=t[:],
                                                  op0=ALU.mult, op1=ALU.add),
          reads=[P.psB[bk], tB, P.mod[l][2]], writes=[P.xB[oc][tb]])


def emit_B(P, l):
    kb, nc = P.kb, P.nc
    xT, xB = P.xT, P.xB
    mB = P.mod[l][2]
    dd = {}
    for name, shp in [("qaT", [512, TPC]), ("qcT", [768, TPC]), ("qnT", [768, TPC]), ("qpT", [384, TPC]),
                      ("kaT_win", [512, 14 * 128]), ("va_win", [512, 14 * 128]),
                      ("kcT_win", [256, 10 * 128]), ("vc_win", [256, 10 * 128]),
                      ("knT_all", [768, S]), ("kpeT_all", [64, S]), ("vb_all", [768, S])]:
        dd[name] = P.din("B_" + name, shp, BF16)
    dd["nab"] = P.din("B_nab", [NQT * 4 * 128, NA_KT * 128])
    dd["swb"] = P.din("B_swb", [NQT * 6 * 128, SW_KT * 128])
    sinkd = P.din("B_sink", [128, 6])
    vecs = {}
    with kb.scope():
        for nm in ["gn", "ln1g", "ln1b", "ln2g", "ln2b"]:
            d_ = P.din(f"B_{nm}", [128, 16])
            t = kb.sb(nm, [128, 16], F32)
            b = kb.buf("B_" + nm)
            kb.dma("sp", t[:], d_, writes=[b])
            vecs[nm] = (t, b)
        sk = kb.sb("sink", [128, 6], F32)
        skB = kb.buf("B_sink")
        kb.dma("sp", sk[:], sinkd, writes=[skB])
        es = kb.sb("esink", [128, 6], F32)
        esB = kb.buf("B_esink")
        kb.op("act", lambda e: e.activation(out=es[:], in_=sk[:], func=AF.Exp), reads=[skB], writes=[esB])
        wo = P.din(f"wo{l}", [16 * 128, 2048])
        wgu = P.din(f"wgu{l}", [NFC * 128, 2 * 2048])
        wdn = P.din(f"wdn{l}", [2 * 16 * 128, (NFC // 2) * 128])
        ynB = [kb.buf(f"yn{c}") for c in range(16)]
        for tb in range(2):
          with kb.scope():
            ynT = kb.sb("ynT", [128, 16 * 512], BF16)
            with kb.scope():
                yT = kb.sb("yT", [128, 16 * 512], F32)
                yB = [kb.buf(f"y{c}") for c in range(16)]
                with kb.scope(), nc.named_scope('local_attn'):
                    emit_local_attn(P, tb, yT, yB, dict(n="na", nh=4, kvh=lambda h: h, nkt=NA_KT, qd=dd["qaT"], kd=dd["kaT_win"],
                                                        vd=dd["va_win"], bd=dd["nab"], scale=float(128 ** -0.5), ychunk0=0, esink=None))
                    emit_local_attn(P, tb, yT, yB, dict(n="sw", nh=6, kvh=lambda h: h // 3, nkt=SW_KT, qd=dd["qcT"], kd=dd["kcT_win"],
                                                        vd=dd["vc_win"], bd=dd["swb"], scale=float(128 ** -0.5), ychunk0=10, esink=(es, esB)))
                with kb.scope(), nc.named_scope('mla'):
                    kpe = kb.sb("kpe", [128, S], BF16)
                    kpeB = kb.buf("kpe")
                    kpeB2 = kb.buf("kpe_z")
                    kb.op("pool", lambda e: e.memset(kpe[64:128, :], 0.0), writes=[kpeB2])
                    kb.dma("sp", kpe[0:64, :], dd["kpeT_all"], writes=[kpeB])
                    emit_mla(P, tb, yT, yB, dd, kpe, [kpeB, kpeB2])
                gt, gB = vecs["gn"]
                rstd = kb.sb("rstdB", [128, 512], F32)
                rB = kb.buf("rstdB")
                for (c0, c1) in ((0, 4), (4, 10), (10, 16)):
                    bk = gbank(P)
                    for c in range(c0, c1):
                        t, tB = P.tbfR.next()
                        kb.op("act", lambda e: e.activation(out=t[:], in_=yT[:, c * 512:(c + 1) * 512], func=AF.Square),
                              reads=[yB[c]], writes=[tB])
                        kb.op("pe", lambda e: e.matmul(P.bank(bk), P.ones[:], t[:], start=(c == c0), stop=(c == c1 - 1)),
                              reads=[P.onesB, tB], writes=[P.psB[bk]], inc=True)
                    emit_rstd(P, bk, 1.0 / ((c1 - c0) * 128), RMS_EPS, rstd[:], rB)
                    for c in range(c0, c1):
                        kb.op("dve", lambda e: e.scalar_tensor_tensor(out=ynT[:, c * 512:(c + 1) * 512], in0=yT[:, c * 512:(c + 1) * 512],
                                                                      scalar=gt[:, c:c + 1], in1=rstd[:], op0=ALU.mult, op1=ALU.mult),
                              reads=[yB[c], gB, rB], writes=[ynB[c]])
            with kb.scope(), nc.named_scope('oproj_ln1'):
                wR = Ring(kb, "wo", [128, 2048], BF16, 3)
                for oc in range(16):
                    wt, wB = wR.next()
                    kb.dma("pool", wt[:], wo[oc * 128:(oc + 1) * 128, :], writes=[wB])
                    bk = gbank(P)
                    for kc in range(16):
                        kb.op("pe", lambda e: e.matmul(P.bank(bk), wt[:, kc * 128:(kc + 1) * 128], ynT[:, kc * 512:(kc + 1) * 512],
                                                       start=(kc == 0), stop=(kc == 15)),
                              reads=[wB, ynB[kc]], writes=[P.psB[bk]], inc=(kc == 15))
                    emit_resid(P, l, 2, oc, tb, bk)
                emit_ln(P, l, tb, vecs["ln1g"][0], vecs["ln1b"][0], vecs["ln1g"][1])
        with kb.scope(), nc.named_scope('ffn'):
            NH = NFC // 2
            u2 = kb.sb("u2T", [128, 16 * 1024], BF16)
            u2B = [[kb.buf(f"u2_{c}_{tb}") for tb in range(2)] for c in range(16)]
            hT = kb.sb("hT", [128, NH * 1024], BF16)
            hB = [[kb.buf(f"h{c}_{tb}") for tb in range(2)] for c in range(NH)]
            for c in range(16):
                for tb in range(2):
                    kb.op("dve", lambda e: e.tensor_scalar(out=u2[:, c1024(c, tb * 512, tb * 512 + 512)],
                                                           in0=xT[:, c1024(c, tb * 512, tb * 512 + 512)],
                                                           scalar1=mcol(P, l, 4, c, True), scalar2=mcol(P, l, 3, c),
                                                           op0=ALU.mult, op1=ALU.add),
                          reads=[xB[c][tb], mB], writes=[u2B[c][tb]])
            wR = Ring(kb, "wgu", [128, 4096], BF16, 3)
            wR2 = Ring(kb, "wdn", [128, NH * 128], BF16, 3)
            for half in range(2):
                for f in range(NH):
                    fc = half * NH + f
                    wt, wB = wR.next()
                    kb.dma("pool", wt[:], wgu[fc * 128:(fc + 1) * 128, :], writes=[wB])
                    for tb in range(2):
                        bg, bu = gbank(P), gbank(P)
                        for (bk, off) in ((bg, 0), (bu, 2048)):
                            for kc in range(16):
                                kb.op("pe", lambda e: e.matmul(P.bank(bk), wt[:, off + kc * 128:off + (kc + 1) * 128],
                                                               u2[:, c1024(kc, tb * 512, tb * 512 + 512)],
                                                               start=(kc == 0), stop=(kc == 15)),
                                      reads=[wB, u2B[kc][tb]], writes=[P.psB[bk]], inc=(kc == 15))
                        t, tB = P.tmpR.next()
                        kb.op("act", lambda e: e.activation(out=t[:], in_=P.bank(bg), func=AF.Silu), reads=[P.psB[bg]], writes=[tB])
                        kb.op("dve", lambda e: e.tensor_tensor(out=hT[:, c1024(f, tb * 512, tb * 512 + 512)], in0=P.bank(bu), in1=t[:], op=ALU.mult),
                              reads=[P.psB[bu], tB], writes=[hB[f][tb]])
                for oc in range(16):
                    wt, wB = wR2.next()
                    row = (half * 16 + oc) * 128
                    kb.dma("pool", wt[:], wdn[row:row + 128, :], writes=[wB])
                    for tb in range(2):
                        bk = gbank(P)
                        for f in range(NH):
                            kb.op("pe", lambda e: e.matmul(P.bank(bk), wt[:, f * 128:(f + 1) * 128], hT[:, c1024(f, tb * 512, tb * 512 + 512)],
                                                           start=(f == 0), stop=(f == NH - 1)),
                                  reads=[wB, hB[f][tb]], writes=[P.psB[bk]], inc=(f == NH - 1))
                        if half == 0:
                            emit_resid(P, l, 5, oc, tb, bk)
                        else:
                            xs = xT[:, c1024(oc, tb * 512, tb * 512 + 512)]
                            kb.op("dve", lambda e: e.scalar_tensor_tensor(out=xs, in0=P.bank(bk), scalar=mcol(P, l, 5, oc, True), in1=xs,
                                                                          op0=ALU.mult, op1=ALU.add),
                                  reads=[P.psB[bk], xB[oc][tb], mB], writes=[xB[oc][tb]])
            for tb in range(2):
                emit_ln(P, l, tb, vecs["ln2g"][0], vecs["ln2b"][0], vecs["ln2g"][1])


def fm_vec(v):
    return np.ascontiguousarray(v.reshape(-1, 128).T)


def lhs_layout(w, cols=None):
    K, N = w.shape
    nk, no = K // 128, N // 128
    return np.ascontiguousarray(w.reshape(nk, 128, no, 128).transpose(2, 1, 0, 3).reshape(no * 128, nk * 128))


def rhs_layout(w):
    K, N = w.shape
    nk = K // 128
    return np.ascontiguousarray(w.reshape(nk, 128, N).transpose(1, 0, 2).reshape(128, nk * N))


def prep_layer(inp, l):
    w = {}
    w_in = inp["w_in"][l]
    swap64 = np.concatenate([np.arange(32, 64), np.arange(0, 32)])
    kr_cols = np.arange(2304, 2368)
    cols = np.concatenate([np.arange(0, 512), np.arange(512, 1024), np.arange(2368, 3136), np.arange(3136, 3392),
                           np.arange(1536, 2048), np.arange(2048, 2304), kr_cols, kr_cols, kr_cols[swap64], kr_cols[swap64]])
    w[f"winfm{l}"] = lhs_layout(w_in[:, cols])
    vcols = np.concatenate([np.arange(1024, 1536), np.arange(3392, 3648)])
    w[f"winv{l}"] = rhs_layout(w_in[:, vcols])
    uq = inp["mla_w_uq"][l].reshape(512, 6, 192)
    w[f"wuqn{l}"] = rhs_layout(np.ascontiguousarray(uq[:, :, :128]).reshape(512, 768))
    w[f"wuqr{l}"] = rhs_layout(np.ascontiguousarray(uq[:, :, 128:]).reshape(512, 384))
    w[f"wuqs{l}"] = rhs_layout(np.ascontiguousarray(uq[:, :, 128:][:, :, swap64]).reshape(512, 384))
    ukv = inp["mla_w_ukv"][l].reshape(256, 6, 256)
    w[f"wukk{l}"] = rhs_layout(np.ascontiguousarray(ukv[:, :, :128]).reshape(256, 768))
    w[f"wukv{l}"] = rhs_layout(np.ascontiguousarray(ukv[:, :, 128:]).reshape(256, 768))
    w[f"gq{l}"] = fm_vec(inp["mla_q_norm"][l])
    w[f"gkv{l}"] = fm_vec(inp["mla_kv_norm"][l])
    return w


def prep_layer_B(inp, l):
    w = {}
    w[f"wo{l}"] = lhs_layout(inp["w_o"][l])
    wg = lhs_layout(inp["w_gu"][l][:, :DFF]).reshape(NFC, 128, 2048)
    wu = lhs_layout(inp["w_gu"][l][:, DFF:]).reshape(NFC, 128, 2048)
    w[f"wgu{l}"] = np.ascontiguousarray(np.concatenate([wg, wu], axis=2).reshape(NFC * 128, 4096))
    hd = DFF // 2
    w[f"wdn{l}"] = np.ascontiguousarray(np.concatenate([lhs_layout(inp["w_down"][l][0:hd]), lhs_layout(inp["w_down"][l][hd:])], axis=0))
    for nm, key in [("gn", "out_norm_g"), ("ln1g", "ln1_g"), ("ln1b", "ln1_b"), ("ln2g", "ln2_g"), ("ln2b", "ln2_b")]:
        w["B_" + nm] = fm_vec(inp[key][l])
    w["B_sink"] = np.ascontiguousarray(np.broadcast_to(inp["swa_sink"][l][None, :], (128, 6)))
    return w


def ada_chunks(inp, chunks):
    w = np.concatenate([inp["w_ada"][l][:, c * 128:(c + 1) * 128] for l, c in chunks], axis=1)
    b = np.stack([inp["b_ada"][l][c * 128:(c + 1) * 128] for l, c in chunks], axis=1)
    return lhs_layout(w), np.ascontiguousarray(b)


ADA_REST = [(0, c) for c in range(32, 96)] + [(1, c) for c in range(96)]


def gather_mod(r1):
    allc = np.concatenate([r["modshare"] for r in r1], axis=1)
    m0 = np.zeros((128, 96), np.float32)
    m0[:, 32:96] = allc[:, 0:64]
    m1 = np.ascontiguousarray(allc[:, 64:160])
    return m0, m1


def rope_tables(core):
    half = 32
    inv = (10000.0 ** (-np.arange(half, dtype=np.float32) / half)).astype(np.float32)
    pos = np.arange(core * TPC, (core + 1) * TPC, dtype=np.float32)
    ang = (pos[:, None] * inv[None, :]).astype(np.float32)
    cos = np.cos(ang).astype(np.float32).T
    sin = np.sin(ang).astype(np.float32).T
    c64 = np.concatenate([cos, cos], 0)
    s64 = np.concatenate([-sin, sin], 0)
    return (np.ascontiguousarray(np.concatenate([c64, c64], 0)), np.ascontiguousarray(np.concatenate([s64, s64], 0)))


def na_bias_tables(rpb, core):
    out = np.full((NQT, 4, 128, NA_KT, 128), NEG, dtype=np.float32)
    qi = np.arange(128)
    for j in range(NQT):
        r_even = core * 16 + 2 * j
        qrow = r_even + qi // 64
        qcol = qi % 64
        r0 = np.clip(qrow - 4, 0, 120)
        c0 = np.clip(qcol - 8, 0, 48)
        for kt in range(NA_KT):
            ktile = core * 8 + j - NA_HALO + kt
            if ktile < 0 or ktile >= 64:
                continue
            ki = np.arange(128)
            krow = 2 * ktile + ki // 64
            kcol = ki % 64
            rin = (krow[:, None] >= r0[None, :]) & (krow[:, None] < r0[None, :] + 8)
            cin = (kcol[:, None] >= c0[None, :]) & (kcol[:, None] < c0[None, :] + 16)
            roff = np.clip(krow[:, None] - qrow[None, :] + 7, 0, 14)
            coff = np.clip(kcol[:, None] - qcol[None, :], -15, 15) + 15
            m = rin & cin
            for h in range(4):
                vals = rpb[h][roff, coff]
                out[j, h, :, kt, :] = np.where(m, vals, np.float32(NEG))
    return out.reshape(NQT * 4 * 128, NA_KT * 128)


def sw_bias_tables(core):
    out = np.full((NQT, 6, 128, SW_KT, 128), NEG, dtype=np.float32)
    slopes = np.array([2.0 ** (-8.0 * (i + 1) / 6) for i in range(6)], dtype=np.float32)
    qi = np.arange(128)
    for j in range(NQT):
        qpos = (core * 8 + j) * 128 + qi
        for kt in range(SW_KT):
            ktile = core * 8 + j - 1 + kt
            if ktile < 0 or ktile >= 64:
                continue
            kpos = ktile * 128 + np.arange(128)
            dist = np.abs(qpos[None, :] - kpos[:, None])
            m = dist <= 128
            for h in range(6):
                out[j, h, :, kt, :] = np.where(m, -slopes[h] * dist.astype(np.float32), np.float32(NEG))
    return out.reshape(NQT * 6 * 128, SW_KT * 128)


def v_layout(v_tok, nh):
    T = v_tok.shape[0]
    return np.ascontiguousarray(v_tok.reshape(T // 128, 128, nh, 128).transpose(2, 1, 0, 3).reshape(nh * 128, T))


def exchange(outs):
    cat = lambda k, ax: np.concatenate([o[k] for o in outs], axis=ax)
    kaT = cat("kaT", 1)
    kcT = cat("kcT", 1)
    va = v_layout(cat("va", 0), 4)
    vc = v_layout(cat("vc", 0), 2)
    knT = cat("knT", 1)
    kpeT = np.ascontiguousarray(cat("kpeT", 1)[0:64])
    vb = v_layout(cat("vb", 0), 6)

    def win(a, core, halo, n):
        t0 = core * 8 - halo
        res = np.zeros((a.shape[0], n * 128), dtype=a.dtype)
        lo, hi = max(t0, 0), min(t0 + n, 64)
        res[:, (lo - t0) * 128:(hi - t0) * 128] = a[:, lo * 128:hi * 128]
        return res

    res = []
    for i in range(NCORE):
        d = {"B_qaT": outs[i]["qaT"], "B_qcT": outs[i]["qcT"], "B_qnT": outs[i]["qnT"], "B_qpT": outs[i]["qpT"],
             "B_kaT_win": win(kaT, i, NA_HALO, 14), "B_va_win": win(va, i, NA_HALO, 14),
             "B_kcT_win": win(kcT, i, SW_HALO, 10), "B_vc_win": win(vc, i, SW_HALO, 10),
             "B_knT_all": knT, "B_kpeT_all": kpeT, "B_vb_all": vb}
        res.append(d)
    return res


_PROGS = {}


def get_prog(kind):
    if kind not in _PROGS:
        _PROGS[kind] = build(kind)
    return _PROGS[kind]


TRACE = False
LAST = {}


def run(kind, in_maps):
    P = get_prog(kind)
    maps = [{k: np.ascontiguousarray(m[k]) for k in P.dr if k in m} for m in in_maps]
    if TRACE:
        res = run_bass_kernel_spmd(P.nc, maps, core_ids=list(range(NCORE)), trace=True)
        LAST[kind] = res
    else:
        res = run_bass_kernel_spmd(P.nc, maps, core_ids=list(range(NCORE)))
    return res.results


def kernel(**inp):
    inp = {k: np.asarray(v) for k, v in inp.items()}
    x = inp["x"][0]
    xT = np.ascontiguousarray(x.T)
    cT = fm_vec(inp["c"][0])
    ropes = [rope_tables(i) for i in range(NCORE)]
    swb = [sw_bias_tables(i) for i in range(NCORE)]
    com = {"cT": cT}
    com["wada_own"], com["bada_own"] = ada_chunks(inp, [(0, c) for c in range(32)])
    com.update(prep_layer(inp, 0))
    maps = []
    for i in range(NCORE):
        m = dict(com)
        m["xT_in"] = xT[:, i * TPC:(i + 1) * TPC]
        m["cosT"], m["ssinT"] = ropes[i]
        m["wada_shr"], m["bada_shr"] = ada_chunks(inp, ADA_REST[i * NSHARE:(i + 1) * NSHARE])
        maps.append(m)
    r1 = run("A0", maps)
    modfull0, modfull1 = gather_mod(r1)
    ex = exchange(r1)
    com = {"modfull0": modfull0, "modfull1": modfull1}
    com.update(prep_layer_B(inp, 0))
    com.update(prep_layer(inp, 1))
    maps = []
    for i in range(NCORE):
        m = dict(com)
        m.update(ex[i])
        m["xT_in"] = xT[:, i * TPC:(i + 1) * TPC]
        m["cosT"], m["ssinT"] = ropes[i]
        m["B_nab"] = na_bias_tables(inp["na_rpb"][0], i)
        m["B_swb"] = swb[i]
        maps.append(m)
    r2 = run("B0A1", maps)
    ex = exchange(r2)
    com = {"modfull1": modfull1}
    com.update(prep_layer_B(inp, 1))
    maps = []
    for i in range(NCORE):
        m = dict(com)
        m.update(ex[i])
        m["xT_in"] = r2[i]["xT_out"]
        m["B_nab"] = na_bias_tables(inp["na_rpb"][1], i)
        m["B_swb"] = swb[i]
        maps.append(m)
    r3 = run("B1", maps)
    outT = np.concatenate([r["xT_out"] for r in r3], axis=1)
    return np.ascontiguousarray(outT.T)[None].astype(np.float32)
```

```python
import numpy as np
from contextlib import ExitStack, contextmanager
import ml_dtypes
import concourse.bass as bass
import concourse.mybir as mybir
from concourse.bass_utils import run_bass_kernel_spmd

F32 = mybir.dt.float32
BF16 = mybir.dt.bfloat16
AF = mybir.ActivationFunctionType
ALU = mybir.AluOpType
NPBF = ml_dtypes.bfloat16

NCORE = 8
S = 8192
D = 2048
TPC = S // NCORE
NQT = TPC // 128
DFF = 5632
NFC = DFF // 128
ALPHA = float(4 ** 0.25)
LN_EPS = 1e-5
RMS_EPS = 1e-6
NEG = -30000.0
NA_KT = 7
NA_HALO = 3
SW_KT = 3
SW_HALO = 1
MLA_LAG = 3


class Buf:
    __slots__ = ("name", "w", "r", "dsem", "dcnt")

    def __init__(self, name):
        self.name = name
        self.w = None
        self.r = {}
        self.dsem = None
        self.dcnt = 0


class Eng:
    def __init__(self, name, e, sem, is_pe=False):
        self.name = name
        self.e = e
        self.sem = sem
        self.cnt = 0
        self.waited = {}
        self.is_pe = is_pe
        self.pend_r = []
        self.pend_w = []
        self.nwait = 0
        self.nins = 0


class KB:
    def __init__(self, nc, stack):
        self.nc = nc
        self.root = stack
        self.stack = stack
        mk = lambda n: stack.enter_context(nc.semaphore(n))
        self.E = {
            "pe": Eng("pe", nc.tensor, mk("s_pe"), True),
            "act": Eng("act", nc.scalar, mk("s_act")),
            "dve": Eng("dve", nc.vector, mk("s_dve")),
            "pool": Eng("pool", nc.gpsimd, mk("s_pool")),
            "sp": Eng("sp", nc.sync, mk("s_sp")),
        }
        self.nsem = 5
        self.dsems = {}
        self.bufs = {}
        self.uid = 0

    def buf(self, name):
        b = self.bufs.get(name)
        if b is None:
            b = Buf(name)
            self.bufs[name] = b
        return b

    def sem(self, name):
        self.nsem += 1
        return self.root.enter_context(self.nc.semaphore(name))

    def sb(self, name, shape, dt):
        self.uid += 1
        return self.stack.enter_context(self.nc.sbuf_tensor(f"{name}_{self.uid}", list(shape), dt))

    def ps(self, name, shape, dt=F32):
        return self.root.enter_context(self.nc.psum_tensor(name, list(shape), dt))

    @contextmanager
    def scope(self):
        old = self.stack
        with ExitStack() as sub:
            self.stack = sub
            try:
                yield
            finally:
                self.stack = old
            self.barrier()

    def _wait(self, E, deps):
        for sem, val in deps:
            if E.is_pe and sem is E.sem:
                continue
            if E.waited.get(sem.num, 0) >= val:
                continue
            E.e.wait_ge(sem, val)
            E.waited[sem.num] = val
            E.nwait += 1

    @staticmethod
    def _deps(reads, writes):
        deps = []
        for b in reads:
            if b.w is not None:
                deps.append(b.w)
        for b in writes:
            deps.extend(b.r.values())
            if b.w is not None:
                deps.append(b.w)
        return deps

    def op(self, en, fn, reads=(), writes=(), inc=True):
        E = self.E[en]
        self._wait(E, self._deps(reads, writes))
        ins = fn(E.e)
        E.nins += 1
        if not inc:
            assert E.is_pe
            E.pend_r.extend(reads)
            E.pend_w.extend(writes)
            return ins
        E.cnt += 1
        ins.then_inc(E.sem, 1)
        comp = (E.sem, E.cnt)
        rr = list(reads) + E.pend_r
        ww = list(writes) + E.pend_w
        E.pend_r = []
        E.pend_w = []
        for b in rr:
            b.r[E.sem.num] = comp
        for b in ww:
            b.w = comp
            b.r = {}
        return ins

    def dma(self, qn, out, in_, reads=(), writes=(), owner=None):
        E = self.E[qn]
        self._wait(E, self._deps(reads, writes))
        if owner is None:
            owner = (list(writes) + list(reads))[0]
        if owner.dsem is None:
            owner.dsem = self.sem("d_" + owner.name)
        ins = E.e.dma_start(out=out, in_=in_)
        owner.dcnt += 16
        ins.then_inc(owner.dsem, 16)
        comp = (owner.dsem, owner.dcnt)
        self.dsems[owner.dsem.num] = comp
        for b in reads:
            b.r[owner.dsem.num] = comp
        for b in writes:
            b.w = comp
            b.r = {}
        E.nins += 1
        return ins

    def barrier(self):
        assert not self.E["pe"].pend_r and not self.E["pe"].pend_w
        deps = [(X.sem, X.cnt) for X in self.E.values() if X.cnt > 0]
        deps += list(self.dsems.values())
        for X in self.E.values():
            self._wait(X, [d for d in deps if d[0] is not X.sem])

    def finish(self):
        self.barrier()

    def stats(self):
        return {n: (X.nins, X.nwait) for n, X in self.E.items()}


class Ring:
    def __init__(self, kb, name, shape, dt, n):
        self.t = [kb.sb(f"{name}{i}", shape, dt) for i in range(n)]
        self.b = [kb.buf(f"{name}{i}") for i in range(n)]
        self.i = 0

    def next(self):
        i = self.i
        self.i = (i + 1) % len(self.t)
        return self.t[i], self.b[i]


class Prog:
    def __init__(self, kind):
        self.kind = kind
        self.nc = bass.Bass("TRN2", target_bir_lowering=False)
        self.dr = {}

    def din(self, name, shape, dt=F32):
        ap = self.nc.dram_tensor(name, list(shape), dt, kind="ExternalInput").ap()
        self.dr[name] = ap
        return ap

    def dout(self, name, shape, dt=F32):
        ap = self.nc.dram_tensor(name, list(shape), dt, kind="ExternalOutput").ap()
        self.dr[name] = ap
        return ap


def c1024(c, a, b):
    return slice(c * 1024 + a, c * 1024 + b)


def build(kind):
    P = Prog(kind)
    nc = P.nc
    doA = {"A0": 0, "B0A1": 1, "B1": None}[kind]
    doB = {"A0": None, "B0A1": 0, "B1": 1}[kind]
    with ExitStack() as st:
        st.enter_context(nc.allow_low_precision("bf16 matmul operands by design"))
        kb = KB(nc, st)
        P.kb = kb
        xT = kb.sb("xT", [128, 16 * 1024], F32)
        xB = [[kb.buf(f"x{c}_{tb}") for tb in range(2)] for c in range(16)]
        PS = kb.ps("PS", [128, 8 * 512])
        psB = [kb.buf(f"ps{i}") for i in range(8)]
        P.bank = lambda b, n=512, o=0, p=128: PS[0:p, b * 512 + o:b * 512 + o + n]
        ones = kb.sb("ones", [128, 128], BF16)
        onesB = kb.buf("ones")
        kb.op("dve", lambda e: e.memset(ones[:], 1.0), writes=[onesB])
        ones32 = kb.sb("ones32", [128, 128], F32)
        P.ones32, P.ones32B = ones32, kb.buf("ones32")
        kb.op("dve", lambda e: e.memset(ones32[:], 1.0), writes=[P.ones32B])
        tmpR = Ring(kb, "tmp", [128, 512], F32, 6)
        tbfR = Ring(kb, "tbf", [128, 512], BF16, 4)
        P.xT, P.xB, P.psB, P.ones, P.onesB, P.tmpR, P.tbfR = xT, xB, psB, ones, onesB, tmpR, tbfR
        P.grot = 0

        xin = P.din("xT_in", [D, TPC])
        xl = kb.buf("xload")
        for c in range(16):
            kb.dma("sp", xT[:, c * 1024:(c + 1) * 1024], xin[c * 128:(c + 1) * 128, :],
                   writes=[xB[c][0], xB[c][1]], owner=xl)
        for c in range(16):
            for tb in range(2):
                xB[c][tb].w = (xl.dsem, xl.dcnt)

        P.mod = {}
        for l in sorted({x for x in (doA, doB) if x is not None}):
            mod = kb.sb(f"mod{l}", [128, 96], F32)
            modp = kb.sb(f"modp{l}", [128, 96], F32)
            P.mod[l] = (mod, modp, kb.buf(f"mod{l}"))
        with nc.named_scope('ada'):
            if kind == "A0":
                emit_ada(P)
            else:
                for l in P.mod:
                    emit_mod_load(P, l)

        if doB is not None:
            emit_B(P, doB)
        if doA is not None:
            with nc.named_scope('phaseA'):
                emit_A(P, doA)
        if kind == "B1" or kind == "B0A1":
            xo = P.dout("xT_out", [D, TPC])
            xs = kb.buf("xstore")
            for c in range(16):
                kb.dma("sp", xo[c * 128:(c + 1) * 128, :], xT[:, c * 1024:(c + 1) * 1024],
                       reads=[xB[c][0], xB[c][1]], owner=xs)
        kb.finish()
        P.stats = kb.stats()
        P.nsem = kb.nsem
    return P


def gbank(P):
    b = P.grot
    P.grot = (b + 1) % 4
    return b


NSHARE = 20


def emit_ada(P):
    kb, nc = P.kb, P.nc
    cin = P.din("cT", [128, 16])
    NOWN, NSHR = 32 * 128, NSHARE * 128
    mso = P.dout("modshare", [1, NSHR])
    with kb.scope():
        cT = kb.sb("cT", [128, 16], F32)
        cB = kb.buf("cT")
        cond = kb.sb("cond", [128, 16], F32)
        condB = kb.buf("cond")
        kb.dma("sp", cT[:], cin, writes=[cB])
        kb.op("act", lambda e: e.activation(out=cond[:], in_=cT[:], func=AF.Silu), reads=[cB], writes=[condB])
        one1 = kb.sb("one1", [1, 1], F32)
        one1B = kb.buf("one1")
        kb.op("dve", lambda e: e.memset(one1[:], 1.0), writes=[one1B])
        wr = Ring(kb, "wada", [128, 16 * 512], F32, 2)
        mod, modp, mB = P.mod[0]
        for (wname, bname, ncols, is_own) in (("wada_own", "bada_own", NOWN, True), ("wada_shr", "bada_shr", NSHR, False)):
            wd = P.din(wname, [(ncols // 512) * 128, 16 * 512])
            bd = P.din(bname, [1, ncols])
            row = kb.sb("row_" + wname, [1, ncols], F32)
            rowB = kb.buf("row_" + wname)
            brow = kb.sb("brow_" + wname, [1, ncols], F32)
            browB = kb.buf("brow_" + wname)
            kb.dma("sp", brow[:], bd, writes=[browB])
            for g in range(ncols // 512):
                wt, wB = wr.next()
                kb.dma("sp", wt[:], wd[g * 128:(g + 1) * 128, :], writes=[wB])
                bk = gbank(P)
                for kc in range(16):
                    kb.op("pe", lambda e: e.matmul(P.bank(bk, 512, 0, 1), cond[:, kc:kc + 1], wt[:, kc * 512:(kc + 1) * 512],
                                                   start=(kc == 0), stop=(kc == 15)),
                          reads=[wB, condB], writes=[P.psB[bk]], inc=(kc == 15))
                kb.op("dve", lambda e: e.tensor_tensor(out=row[:, g * 512:(g + 1) * 512], in0=P.bank(bk, 512, 0, 1),
                                                       in1=brow[:, g * 512:(g + 1) * 512], op=ALU.add),
                      reads=[P.psB[bk], browB], writes=[rowB])
            if is_own:
                bk = 7
                for c in range(32):
                    kb.op("pe", lambda e: e.matmul(P.bank(bk, 1, c), row[:, c * 128:(c + 1) * 128], one1[:], start=True, stop=True),
                          reads=[rowB, one1B], writes=[P.psB[bk]], inc=(c == 31))
                kb.op("act", lambda e: e.activation(out=mod[:, 0:32], in_=P.bank(bk, 32), func=AF.Copy), reads=[P.psB[bk]], writes=[mB])
                kb.op("dve", lambda e: e.tensor_scalar_add(out=modp[:, 0:32], in0=mod[:, 0:32], scalar1=1.0), reads=[mB], writes=[mB])
            else:
                kb.dma("sp", mso, row[:], reads=[rowB], owner=rowB)


def emit_mod_load(P, l):
    kb = P.kb
    md = P.din(f"modfull{l}", [128, 96])
    mod, modp, mB = P.mod[l]
    kb.dma("sp", mod[:], md, writes=[mB])
    kb.op("dve", lambda e: e.tensor_scalar_add(out=modp[:], in0=mod[:], scalar1=1.0), reads=[mB], writes=[mB])


def mcol(P, l, j, c, plus1=False):
    mod, modp, _ = P.mod[l]
    t = modp if plus1 else mod
    return t[:, j * 16 + c:j * 16 + c + 1]


def emit_rstd(P, psbank, scale, eps, out_t, out_b):
    kb = P.kb
    t, tB = P.tmpR.next()
    kb.op("act", lambda e: e.activation(out=t[:], in_=P.bank(psbank), func=AF.Ln, bias=eps, scale=scale),
          reads=[P.psB[psbank]], writes=[tB])
    kb.op("act", lambda e: e.activation(out=out_t, in_=t[:], func=AF.Exp, scale=-0.5), reads=[tB], writes=[out_b])


def emit_A(P, l):
    kb, nc = P.kb, P.nc
    xT, xB = P.xT, P.xB
    mB = P.mod[l][2]
    o = {}
    for name, shp in [("qaT", [512, TPC]), ("kaT", [512, TPC]), ("va", [TPC, 512]),
                      ("qcT", [768, TPC]), ("kcT", [256, TPC]), ("vc", [TPC, 256]),
                      ("qnT", [768, TPC]), ("qpT", [384, TPC]), ("knT", [768, TPC]),
                      ("kpeT", [128, TPC]), ("vb", [TPC, 768])]:
        o[name] = P.dout(f"{name}", shp, BF16)
    win = P.din(f"winfm{l}", [24 * 128, 2048])
    wv = P.din(f"winv{l}", [128, 16 * 768])
    wuqn = P.din(f"wuqn{l}", [128, 4 * 768])
    wuqr = P.din(f"wuqr{l}", [128, 4 * 384])
    wuqs = P.din(f"wuqs{l}", [128, 4 * 384])
    wukk = P.din(f"wukk{l}", [128, 2 * 768])
    wukv = P.din(f"wukv{l}", [128, 2 * 768])
    gq = P.din(f"gq{l}", [128, 4])
    gkv = P.din(f"gkv{l}", [128, 2])
    cosd = P.din("cosT", [128, TPC])
    sind = P.din("ssinT", [128, TPC])
    with kb.scope():
        uT = kb.sb("uT", [128, 16 * 1024], BF16)
        uB = [[kb.buf(f"u{c}_{tb}") for tb in range(2)] for c in range(16)]
        for c in range(16):
            for tb in range(2):
                kb.op("dve", lambda e: e.tensor_scalar(out=uT[:, c1024(c, tb * 512, tb * 512 + 512)],
                                                       in0=xT[:, c1024(c, tb * 512, tb * 512 + 512)],
                                                       scalar1=mcol(P, l, 1, c, True), scalar2=mcol(P, l, 0, c),
                                                       op0=ALU.mult, op1=ALU.add),
                      reads=[xB[c][tb], mB], writes=[uB[c][tb]])
        with kb.scope():
            wvt = kb.sb("wv", [128, 16 * 768], BF16)
            wvB = kb.buf("A_wv")
            kb.dma("pool", wvt[:], wv, writes=[wvB])
            st2R = Ring(kb, "stg2", [128, 768], BF16, 2)
            for tt in range(8):
                tb, off = tt // 4, (tt % 4) * 128 + (tt // 4) * 512
                b1, b2 = gbank(P), gbank(P)
                for (bk, c0, n) in ((b1, 0, 512), (b2, 512, 256)):
                    for kc in range(16):
                        kb.op("pe", lambda e: e.matmul(P.bank(bk, n), uT[:, c1024(kc, off, off + 128)],
                                                       wvt[:, kc * 768 + c0:kc * 768 + c0 + n],
                                                       start=(kc == 0), stop=(kc == 15)),
                              reads=[wvB, uB[kc][tb]], writes=[P.psB[bk]], inc=(kc == 15))
                stg, sB = st2R.next()
                kb.op("act", lambda e: e.activation(out=stg[:, 0:512], in_=P.bank(b1), func=AF.Copy), reads=[P.psB[b1]], writes=[sB])
                kb.op("act", lambda e: e.activation(out=stg[:, 512:768], in_=P.bank(b2, 256), func=AF.Copy), reads=[P.psB[b2]], writes=[sB])
                kb.dma("sp", o["va"][tt * 128:(tt + 1) * 128, :], stg[:, 0:512], reads=[sB], owner=sB)
                kb.dma("sp", o["vc"][tt * 128:(tt + 1) * 128, :], stg[:, 512:768], reads=[sB], owner=sB)
        sm = {}
        for nm, dd, shp in [("gq", gq, [128, 4]), ("gkv", gkv, [128, 2]), ("cos", cosd, [128, TPC]), ("sin", sind, [128, TPC])]:
            t = kb.sb(nm, shp, F32)
            b = kb.buf("A_" + nm)
            kb.dma("sp", t[:], dd, writes=[b])
            sm[nm] = (t, b)
        wsm = {}
        for nm, dd, shp in [("wuqn", wuqn, [128, 4 * 768]), ("wuqr", wuqr, [128, 4 * 384]), ("wuqs", wuqs, [128, 4 * 384]),
                            ("wukk", wukk, [128, 2 * 768]), ("wukv", wukv, [128, 2 * 768])]:
            t = kb.sb(nm, shp, BF16)
            b = kb.buf("A_" + nm)
            kb.dma("pool", t[:], dd, writes=[b])
            wsm[nm] = (t, b)
        cqT = kb.sb("cqT", [128, 4 * 1024], F32)
        ckvT = kb.sb("ckvT", [128, 2 * 1024], F32)
        krT = kb.sb("krT", [128, 2 * 1024], F32)
        cqB = [[kb.buf(f"cq{c}_{tb}") for tb in range(2)] for c in range(4)]
        ckvB = [[kb.buf(f"ckv{c}_{tb}") for tb in range(2)] for c in range(2)]
        krB = [[kb.buf(f"kr{c}_{tb}") for tb in range(2)] for c in range(2)]
        wR = Ring(kb, "win", [128, 2048], BF16, 3)
        stR = Ring(kb, "stg", [128, 1024], BF16, 3)
        dest = ([("qaT", i) for i in range(4)] + [("kaT", i) for i in range(4)] + [("qcT", i) for i in range(6)] +
                [("kcT", i) for i in range(2)] + [("cq", i) for i in range(4)] + [("ckv", i) for i in range(2)] +
                [("kr", 0), ("kr", 1)])
        for oc, (dn, di) in enumerate(dest):
            wt, wB = wR.next()
            kb.dma("pool", wt[:], win[oc * 128:(oc + 1) * 128, :], writes=[wB])
            if dn in o:
                stg, sB = stR.next()
            for tb in range(2):
                bk = gbank(P)
                for kc in range(16):
                    kb.op("pe", lambda e: e.matmul(P.bank(bk), wt[:, kc * 128:(kc + 1) * 128],
                                                   uT[:, c1024(kc, tb * 512, tb * 512 + 512)],
                                                   start=(kc == 0), stop=(kc == 15)),
                          reads=[wB, uB[kc][tb]], writes=[P.psB[bk]], inc=(kc == 15))
                if dn in o:
                    kb.op("act", lambda e: e.activation(out=stg[:, tb * 512:(tb + 1) * 512], in_=P.bank(bk), func=AF.Copy),
                          reads=[P.psB[bk]], writes=[sB])
                else:
                    tt, bb = {"cq": (cqT, cqB), "ckv": (ckvT, ckvB), "kr": (krT, krB)}[dn]
                    kb.op("act", lambda e: e.activation(out=tt[:, c1024(di, tb * 512, tb * 512 + 512)], in_=P.bank(bk), func=AF.Copy),
                          reads=[P.psB[bk]], writes=[bb[di][tb]])
            if dn in o:
                kb.dma("sp", o[dn][di * 128:(di + 1) * 128, :], stg[:], reads=[sB], owner=sB)
        cqn = kb.sb("cqn", [128, 4 * 1024], BF16)
        ckvn = kb.sb("ckvn", [128, 2 * 1024], BF16)
        cqnB = [kb.buf(f"cqn{tb}") for tb in range(2)]
        ckvnB = [kb.buf(f"ckvn{tb}") for tb in range(2)]
        rstd = kb.sb("rstdA", [128, 512], F32)
        rB = kb.buf("rstdA")
        for (src, sB_, nchunk, dst, dB, gname) in ((cqT, cqB, 4, cqn, cqnB, "gq"), (ckvT, ckvB, 2, ckvn, ckvnB, "gkv")):
            gt, gB = sm[gname]
            for tb in range(2):
                bk = gbank(P)
                for c in range(nchunk):
                    t, tB = P.tbfR.next()
                    kb.op("act", lambda e: e.activation(out=t[:], in_=src[:, c1024(c, tb * 512, tb * 512 + 512)], func=AF.Square),
                          reads=[sB_[c][tb]], writes=[tB])
                    kb.op("pe", lambda e: e.matmul(P.bank(bk), P.ones[:], t[:], start=(c == 0), stop=(c == nchunk - 1)),
                          reads=[tB, P.onesB], writes=[P.psB[bk]], inc=True)
                emit_rstd(P, bk, 1.0 / (nchunk * 128), RMS_EPS, rstd[:], rB)
                for c in range(nchunk):
                    kb.op("dve", lambda e: e.scalar_tensor_tensor(out=dst[:, c1024(c, tb * 512, tb * 512 + 512)],
                                                                  in0=src[:, c1024(c, tb * 512, tb * 512 + 512)],
                                                                  scalar=gt[:, c:c + 1], in1=rstd[:],
                                                                  op0=ALU.mult, op1=ALU.mult),
                          reads=[sB_[c][tb], gB, rB], writes=[dB[tb]])
        cost, cosB = sm["cos"]
        sint, sinB = sm["sin"]

        def rope(out_ap, a_ap, s_ap, rows, tb, reads, wbuf):
            t1, t1B = P.tmpR.next()
            t2, t2B = P.tmpR.next()
            kb.op("dve", lambda e: e.tensor_tensor(out=t1[0:rows, :], in0=a_ap, in1=cost[0:rows, tb * 512:(tb + 1) * 512], op=ALU.mult),
                  reads=reads + [cosB], writes=[t1B])
            kb.op("dve", lambda e: e.tensor_tensor(out=t2[0:rows, :], in0=s_ap, in1=sint[0:rows, tb * 512:(tb + 1) * 512], op=ALU.mult),
                  reads=reads + [sinB], writes=[t2B])
            kb.op("dve", lambda e: e.tensor_tensor(out=out_ap, in0=t1[0:rows, :], in1=t2[0:rows, :], op=ALU.add),
                  reads=[t1B, t2B], writes=[wbuf])

        stg, sB = stR.next()
        for tb in range(2):
            rope(stg[:, tb * 512:(tb + 1) * 512], krT[:, c1024(0, tb * 512, tb * 512 + 512)],
                 krT[:, c1024(1, tb * 512, tb * 512 + 512)], 128, tb, [krB[0][tb], krB[1][tb]], sB)
        kb.dma("sp", o["kpeT"], stg[:], reads=[sB], owner=sB)
        for (wname, nk, srcn, srcB, oname) in (("wuqn", 4, cqn, cqnB, "qnT"), ("wukk", 2, ckvn, ckvnB, "knT")):
            wt, wB = wsm[wname]
            for h in range(6):
                stg, sB = stR.next()
                for tb in range(2):
                    bk = gbank(P)
                    for kc in range(nk):
                        kb.op("pe", lambda e: e.matmul(P.bank(bk), wt[:, kc * 768 + h * 128:kc * 768 + (h + 1) * 128],
                                                       srcn[:, c1024(kc, tb * 512, tb * 512 + 512)],
                                                       start=(kc == 0), stop=(kc == nk - 1)),
                              reads=[wB, srcB[tb]], writes=[P.psB[bk]], inc=(kc == nk - 1))
                    kb.op("act", lambda e: e.activation(out=stg[:, tb * 512:(tb + 1) * 512], in_=P.bank(bk), func=AF.Copy),
                          reads=[P.psB[bk]], writes=[sB])
                kb.dma("sp", o[oname][h * 128:(h + 1) * 128, :], stg[:], reads=[sB], owner=sB)
        wr_, wrB = wsm["wuqr"]
        ws_, wsB = wsm["wuqs"]
        for h in range(6):
            stg, sB = stR.next()
            for tb in range(2):
                b1, b2 = gbank(P), gbank(P)
                for (bk, wt, wB) in ((b1, wr_, wrB), (b2, ws_, wsB)):
                    for kc in range(4):
                        kb.op("pe", lambda e: e.matmul(P.bank(bk, 512, 0, 64), wt[:, kc * 384 + h * 64:kc * 384 + (h + 1) * 64],
                                                       cqn[:, c1024(kc, tb * 512, tb * 512 + 512)],
                                                       start=(kc == 0), stop=(kc == 3)),
                              reads=[wB, cqnB[tb]], writes=[P.psB[bk]], inc=(kc == 3))
                t3, t3B = P.tmpR.next()
                kb.op("act", lambda e: e.activation(out=t3[0:64, :], in_=P.bank(b2, 512, 0, 64), func=AF.Copy), reads=[P.psB[b2]], writes=[t3B])
                rope(stg[0:64, tb * 512:(tb + 1) * 512], P.bank(b1, 512, 0, 64), t3[0:64, :], 64, tb, [P.psB[b1], t3B], sB)
            kb.dma("sp", o["qpT"][h * 64:(h + 1) * 64, :], stg[0:64, :], reads=[sB], owner=sB)
        st2R = Ring(kb, "stg3", [128, 768], BF16, 2)
        wt, wB = wsm["wukv"]
        for tt in range(8):
            tb, off = tt // 4, (tt % 4) * 128 + (tt // 4) * 512
            b1, b2 = gbank(P), gbank(P)
            for (bk, c0, n) in ((b1, 0, 512), (b2, 512, 256)):
                for kc in range(2):
                    kb.op("pe", lambda e: e.matmul(P.bank(bk, n), ckvn[:, c1024(kc, off, off + 128)],
                                                   wt[:, kc * 768 + c0:kc * 768 + c0 + n],
                                                   start=(kc == 0), stop=(kc == 1)),
                          reads=[wB, ckvnB[tb]], writes=[P.psB[bk]], inc=(kc == 1))
            stg, sB = st2R.next()
            kb.op("act", lambda e: e.activation(out=stg[:, 0:512], in_=P.bank(b1), func=AF.Copy), reads=[P.psB[b1]], writes=[sB])
            kb.op("act", lambda e: e.activation(out=stg[:, 512:768], in_=P.bank(b2, 256), func=AF.Copy), reads=[P.psB[b2]], writes=[sB])
            kb.dma("sp", o["vb"][tt * 128:(tt + 1) * 128, :], stg[:], reads=[sB], owner=sB)


def emit_local_attn(P, tb, yT, yB, spec):
    kb = P.kb
    nkt = spec["nkt"]
    nwin = 4 + nkt - 1
    W = nkt * 128
    kR = Ring(kb, spec["n"] + "k", [128, nwin * 128], BF16, 2)
    vR = Ring(kb, spec["n"] + "v", [128, nwin * 128], BF16, 2)
    qR = Ring(kb, spec["n"] + "q", [128, 512], BF16, 2)
    bR = Ring(kb, spec["n"] + "b", [128, W], F32, 3)
    sR = Ring(kb, spec["n"] + "s", [128, W], F32, 3)
    pR = Ring(kb, spec["n"] + "p", [128, W], BF16, 4)
    rR = Ring(kb, spec["n"] + "r", [128, 128], F32, 3)
    class _Rot:
        def __init__(self, lst):
            self.l = lst

        def __getitem__(self, k):
            return self.l[k % len(self.l)]
    NS = 6
    oB = _Rot([P.psB[4], P.psB[5], P.psB[6]])
    dB = _Rot([P.psB[7]])
    oAP = lambda i: P.bank(4 + (i % 3), 128)
    dAP = lambda i: P.bank(7, 128)
    state = {"kv": None}
    its = []

    def stA(it):
        h, qt, i = it["h"], it["qt"], it["i"]
        kvh = spec["kvh"](h)
        if kvh != state["kv"]:
            kt_, kB_ = kR.next()
            vt_, vB_ = vR.next()
            kb.dma("sp", kt_[:], spec["kd"][kvh * 128:(kvh + 1) * 128, tb * 512:tb * 512 + nwin * 128], writes=[kB_])
            kb.dma("sp", vt_[:], spec["vd"][kvh * 128:(kvh + 1) * 128, tb * 512:tb * 512 + nwin * 128], writes=[vB_])
            state["kv"] = kvh
            state["kvt"] = (kt_, kB_, vt_, vB_)
        kt_, kB_, vt_, vB_ = state["kvt"]
        if qt == 0:
            qt_, qB_ = qR.next()
            kb.dma("sp", qt_[:], spec["qd"][h * 128:(h + 1) * 128, tb * 512:(tb + 1) * 512], writes=[qB_])
            state["q"] = (qt_, qB_)
        qt_, qB_ = state["q"]
        j = tb * 4 + qt
        bt_, bB_ = bR.next()
        row = (j * spec["nh"] + h) * 128
        kb.dma("sp", bt_[:], spec["bd"][row:row + 128, :], writes=[bB_])
        base = (i % 2) * 2
        for kt in range(nkt):
            bk = base + (kt // 4)
            kb.op("pe", lambda e: e.matmul(P.bank(bk, 128, (kt % 4) * 128),
                                           kt_[:, (qt + kt) * 128:(qt + kt + 1) * 128],
                                           qt_[:, qt * 128:(qt + 1) * 128], start=True, stop=True),
                  reads=[kB_, qB_], writes=[P.psB[bk]], inc=(kt == nkt - 1 or kt == 3))
        st_, sB_ = sR.next()
        n0 = min(W, 512)
        kb.op("dve", lambda e: e.scalar_tensor_tensor(out=st_[:, 0:n0], in0=P.bank(base, n0), scalar=spec["scale"],
                                                      in1=bt_[:, 0:n0], op0=ALU.mult, op1=ALU.add),
              reads=[P.psB[base], bB_], writes=[sB_])
        if W > 512:
            kb.op("dve", lambda e: e.scalar_tensor_tensor(out=st_[:, 512:W], in0=P.bank(base + 1, W - 512), scalar=spec["scale"],
                                                          in1=bt_[:, 512:W], op0=ALU.mult, op1=ALU.add),
                  reads=[P.psB[base + 1], bB_], writes=[sB_])
        it.update(st=st_, sB=sB_, vt=vt_, vB=vB_)

    def stB(it):
        pt_, pB_ = pR.next()
        st_, sB_ = it["st"], it["sB"]
        kb.op("act", lambda e: e.activation(out=pt_[:], in_=st_[:], func=AF.Exp), reads=[sB_], writes=[pB_])
        it.update(pt=pt_, pB=pB_)

    def stC(it):
        i, qt, vt_, vB_, pt_, pB_ = it["i"], it["qt"], it["vt"], it["vB"], it["pt"], it["pB"]
        for kt in range(nkt):
            kb.op("pe", lambda e: e.matmul(oAP(i), vt_[:, (qt + kt) * 128:(qt + kt + 1) * 128],
                                           pt_[:, kt * 128:(kt + 1) * 128], start=(kt == 0), stop=(kt == nkt - 1)),
                  reads=[vB_, pB_], writes=[oB[i % NS]], inc=(kt == nkt - 1))
        for kt in range(nkt):
            kb.op("pe", lambda e: e.matmul(dAP(i), P.ones[:], pt_[:, kt * 128:(kt + 1) * 128],
                                           start=(kt == 0), stop=(kt == nkt - 1)),
                  reads=[P.onesB, pB_], writes=[dB[i % NS]], inc=(kt == nkt - 1))

    def stD(it):
        i, h = it["i"], it["h"]
        rt_, rB_ = rR.next()
        if spec["esink"] is not None:
            es, esB = spec["esink"]
            kb.op("act", lambda e: e.activation(out=rt_[:], in_=dAP(i), func=AF.Ln, bias=es[:, h:h + 1], scale=1.0),
                  reads=[dB[i % NS], esB], writes=[rB_])
        else:
            kb.op("act", lambda e: e.activation(out=rt_[:], in_=dAP(i), func=AF.Ln), reads=[dB[i % NS]], writes=[rB_])
        kb.op("act", lambda e: e.activation(out=rt_[:], in_=rt_[:], func=AF.Exp, scale=-1.0), reads=[rB_], writes=[rB_])
        it.update(rt=rt_, rB=rB_)

    def stE(it):
        i, h, qt, rt_, rB_ = it["i"], it["h"], it["qt"], it["rt"], it["rB"]
        ch = spec["ychunk0"] + h
        kb.op("dve", lambda e: e.tensor_tensor(out=yT[:, ch * 512 + qt * 128:ch * 512 + (qt + 1) * 128],
                                               in0=oAP(i), in1=rt_[:], op=ALU.mult),
              reads=[oB[i % NS], rB_], writes=[yB[ch]])

    for h in range(spec["nh"]):
        for qt in range(4):
            its.append(dict(h=h, qt=qt, i=len(its)))
    n = len(its)
    stages = [stA, stB, stC, stD, stE]
    for t in range(n + len(stages) - 1):
        for si in reversed(range(len(stages))):
            k = t - si
            if 0 <= k < n:
                stages[si](its[k])


def emit_mla(P, tb, yT, yB, dd, kpe, kpeB):
    kb = P.kb
    scale = float(192 ** -0.5)
    knR = Ring(kb, "mkn", [128, 4096], BF16, 2)
    vR = Ring(kb, "mv", [128, 4096], BF16, 2)
    qnR = Ring(kb, "mqn", [128, 512], BF16, 2)
    qpR = Ring(kb, "mqp", [128, 512], BF16, 2)
    for t_, b_ in zip(qpR.t, qpR.b):
        kb.op("pool", lambda e: e.memset(t_[64:128, :], 0.0), writes=[kb.buf(b_.name + "_z")])
    pR = Ring(kb, "mp", [128, 512], BF16, 8)
    accR = Ring(kb, "macc", [128, 512], F32, 2)
    prR = Ring(kb, "mpr", [128, 512], BF16, 2)
    rt_ = kb.sb("mr", [128, 512], F32)
    rB_ = kb.buf("mr")
    srot = 0
    for h in range(6):
        qn, qnB = qnR.next()
        qp, qpB = qpR.next()
        kb.dma("sp", qn[:], dd["qnT"][h * 128:(h + 1) * 128, tb * 512:(tb + 1) * 512], writes=[qnB])
        kb.dma("sp", qp[0:64, :], dd["qpT"][h * 64:(h + 1) * 64, tb * 512:(tb + 1) * 512], writes=[qpB])
        qpZ = kb.buf(qpB.name + "_z")
        ob, db = 4 + (h % 2), 6 + (h % 2)
        acc, accB = accR.next()
        stash = []
        pendq = []

        def pv(item, first, last):
            pt, pB, kt, vt, vB = item
            kb.op("pe", lambda e: e.matmul(P.bank(ob), vt[:, (kt % 32) * 128:(kt % 32 + 1) * 128], pt[:], start=first, stop=last),
                  reads=[vB, pB], writes=[P.psB[ob]], inc=True)
            if kt % 2 == 0:
                stash.append((pt, pB))
            else:
                p0, p0B = stash.pop()
                t2, t2B = prR.next()
                kb.op("dve", lambda e: e.tensor_tensor(out=t2[:], in0=p0[:], in1=pt[:], op=ALU.add), reads=[p0B, pB], writes=[t2B])
                if kt == 1:
                    kb.op("dve", lambda e: e.tensor_copy(out=acc[:], in_=t2[:]), reads=[t2B], writes=[accB])
                else:
                    kb.op("dve", lambda e: e.tensor_tensor(out=acc[:], in0=acc[:], in1=t2[:], op=ALU.add), reads=[t2B, accB], writes=[accB])

        for half in range(2):
            kn, knB = knR.next()
            vt, vB = vR.next()
            kb.dma("sp", kn[:], dd["knT_all"][h * 128:(h + 1) * 128, half * 4096:(half + 1) * 4096], writes=[knB])
            kb.dma("sp", vt[:], dd["vb_all"][h * 128:(h + 1) * 128, half * 4096:(half + 1) * 4096], writes=[vB])
            for k32 in range(32):
                kt = half * 32 + k32
                bk = srot % 4
                srot += 1
                kb.op("pe", lambda e: e.matmul(P.bank(bk), kn[:, k32 * 128:(k32 + 1) * 128], qn[:], start=True, stop=False),
                      reads=[knB, qnB], writes=[P.psB[bk]], inc=False)
                kb.op("pe", lambda e: e.matmul(P.bank(bk), kpe[:, kt * 128:(kt + 1) * 128], qp[:], start=False, stop=True),
                      reads=kpeB + [qpB, qpZ], writes=[P.psB[bk]], inc=True)
                pt, pB = pR.next()
                kb.op("act", lambda e: e.activation(out=pt[:], in_=P.bank(bk), func=AF.Exp, scale=scale),
                      reads=[P.psB[bk]], writes=[pB])
                pendq.append((pt, pB, kt, vt, vB))
                if len(pendq) > MLA_LAG:
                    it = pendq.pop(0)
                    pv(it, it[2] == 0, False)
        while pendq:
            it = pendq.pop(0)
            pv(it, it[2] == 0, it[2] == 63)
        kb.op("pe", lambda e: e.matmul(P.bank(db), P.ones32[:], acc[:], start=True, stop=True),
              reads=[P.ones32B, accB], writes=[P.psB[db]], inc=True)
        kb.op("act", lambda e: e.activation(out=rt_[:], in_=P.bank(db), func=AF.Ln), reads=[P.psB[db]], writes=[rB_])
        kb.op("act", lambda e: e.activation(out=rt_[:], in_=rt_[:], func=AF.Exp, scale=-1.0), reads=[rB_], writes=[rB_])
        ch = 4 + h
        kb.op("dve", lambda e: e.tensor_tensor(out=yT[:, ch * 512:(ch + 1) * 512], in0=P.bank(ob), in1=rt_[:], op=ALU.mult),
              reads=[P.psB[ob], rB_], writes=[yB[ch]])


def emit_ln(P, l, tb, gt, bt, gbB):
    kb = P.kb
    xT, xB = P.xT, P.xB
    sb_, qb_ = 4, 6
    for c in range(16):
        t1, t1B = P.tbfR.next()
        t2, t2B = P.tbfR.next()
        xs = xT[:, c1024(c, tb * 512, tb * 512 + 512)]
        kb.op("act", lambda e: e.activation(out=t1[:], in_=xs, func=AF.Copy), reads=[xB[c][tb]], writes=[t1B])
        kb.op("act", lambda e: e.activation(out=t2[:], in_=xs, func=AF.Square), reads=[xB[c][tb]], writes=[t2B])
        kb.op("pe", lambda e: e.matmul(P.bank(sb_), P.ones[:], t1[:], start=(c == 0), stop=(c == 15)),
              reads=[P.onesB, t1B], writes=[P.psB[sb_]], inc=True)
        kb.op("pe", lambda e: e.matmul(P.bank(qb_), P.ones[:], t2[:], start=(c == 0), stop=(c == 15)),
              reads=[P.onesB, t2B], writes=[P.psB[qb_]], inc=True)
    mean, mB = P.tmpR.next()
    msq, qB = P.tmpR.next()
    var, vB = P.tmpR.next()
    rstd, rB = P.tmpR.next()
    kb.op("act", lambda e: e.mul(out=mean[:], in_=P.bank(sb_), mul=1.0 / D), reads=[P.psB[sb_]], writes=[mB])
    kb.op("dve", lambda e: e.tensor_tensor(out=msq[:], in0=mean[:], in1=mean[:], op=ALU.mult), reads=[mB], writes=[qB])
    kb.op("dve", lambda e: e.scalar_tensor_tensor(out=var[:], in0=P.bank(qb_), scalar=1.0 / D, in1=msq[:],
                                                  op0=ALU.mult, op1=ALU.subtract),
          reads=[P.psB[qb_], qB], writes=[vB])
    t, tB = P.tmpR.next()
    kb.op("act", lambda e: e.activation(out=t[:], in_=var[:], func=AF.Ln, bias=LN_EPS, scale=1.0), reads=[vB], writes=[tB])
    kb.op("act", lambda e: e.activation(out=rstd[:], in_=t[:], func=AF.Exp, scale=-0.5), reads=[tB], writes=[rB])
    for c in range(16):
        xs = xT[:, c1024(c, tb * 512, tb * 512 + 512)]
        kb.op("dve", lambda e: e.tensor_tensor(out=xs, in0=xs, in1=mean[:], op=ALU.subtract), reads=[xB[c][tb], mB], writes=[xB[c][tb]])
        kb.op("dve", lambda e: e.tensor_tensor(out=xs, in0=xs, in1=rstd[:], op=ALU.mult), reads=[xB[c][tb], rB], writes=[xB[c][tb]])
        kb.op("act", lambda e: e.activation(out=xs, in_=xs, func=AF.Identity, scale=gt[:, c:c + 1], bias=bt[:, c:c + 1]),
              reads=[xB[c][tb], gbB], writes=[xB[c][tb]])


def emit_resid(P, l, jg, oc, tb, bk):
    kb = P.kb
    xs = P.xT[:, c1024(oc, tb * 512, tb * 512 + 512)]
    t, tB = P.tmpR.next()
    kb.op("act", lambda e: e.mul(out=t[:], in_=xs, mul=ALPHA), reads=[P.xB[oc][tb]], writes=[tB])
    kb.op("dve", lambda e: e.scalar_tensor_tensor(out=xs, in0=P.bank(bk), scalar=mcol(P, l, jg, oc, True), in1=t[:],
                                                  op0=ALU.mult, op1=ALU.add),
          reads=[P.psB[bk], tB, P.mod[l][2]], writes=[P.xB[oc][tb]])


def emit_B(P, l):
    kb, nc = P.kb, P.nc
    xT, xB = P.xT, P.xB
    mB = P.mod[l][2]
    dd = {}
    for name, shp in [("qaT", [512, TPC]), ("qcT", [768, TPC]), ("qnT", [768, TPC]), ("qpT", [384, TPC]),
                      ("kaT_win", [512, 14 * 128]), ("va_win", [512, 14 * 128]),
                      ("kcT_win", [256, 10 * 128]), ("vc_win", [256, 10 * 128]),
                      ("knT_all", [768, S]), ("kpeT_all", [64, S]), ("vb_all", [768, S])]:
        dd[name] = P.din("B_" + name, shp, BF16)
    dd["nab"] = P.din("B_nab", [NQT * 4 * 128, NA_KT * 128])
    dd["swb"] = P.din("B_swb", [NQT * 6 * 128, SW_KT * 128])
    sinkd = P.din("B_sink", [128, 6])
    vecs = {}
    with kb.scope():
        for nm in ["gn", "ln1g", "ln1b", "ln2g", "ln2b"]:
            d_ = P.din(f"B_{nm}", [128, 16])
            t = kb.sb(nm, [128, 16], F32)
            b = kb.buf("B_" + nm)
            kb.dma("sp", t[:], d_, writes=[b])
            vecs[nm] = (t, b)
        sk = kb.sb("sink", [128, 6], F32)
        skB = kb.buf("B_sink")
        kb.dma("sp", sk[:], sinkd, writes=[skB])
        es = kb.sb("esink", [128, 6], F32)
        esB = kb.buf("B_esink")
        kb.op("act", lambda e: e.activation(out=es[:], in_=sk[:], func=AF.Exp), reads=[skB], writes=[esB])
        wo = P.din(f"wo{l}", [16 * 128, 2048])
        wgu = P.din(f"wgu{l}", [NFC * 128, 2 * 2048])
        wdn = P.din(f"wdn{l}", [2 * 16 * 128, (NFC // 2) * 128])
        ynB = [kb.buf(f"yn{c}") for c in range(16)]
        for tb in range(2):
          with kb.scope():
            ynT = kb.sb("ynT", [128, 16 * 512], BF16)
            with kb.scope():
                yT = kb.sb("yT", [128, 16 * 512], F32)
                yB = [kb.buf(f"y{c}") for c in range(16)]
                with kb.scope(), nc.named_scope('local_attn'):
                    emit_local_attn(P, tb, yT, yB, dict(n="na", nh=4, kvh=lambda h: h, nkt=NA_KT, qd=dd["qaT"], kd=dd["kaT_win"],
                                                        vd=dd["va_win"], bd=dd["nab"], scale=float(128 ** -0.5), ychunk0=0, esink=None))
                    emit_local_attn(P, tb, yT, yB, dict(n="sw", nh=6, kvh=lambda h: h // 3, nkt=SW_KT, qd=dd["qcT"], kd=dd["kcT_win"],
                                                        vd=dd["vc_win"], bd=dd["swb"], scale=float(128 ** -0.5), ychunk0=10, esink=(es, esB)))
                with kb.scope(), nc.named_scope('mla'):
                    kpe = kb.sb("kpe", [128, S], BF16)
                    kpeB = kb.buf("kpe")
                    kpeB2 = kb.buf("kpe_z")
                    kb.op("pool", lambda e: e.memset(kpe[64:128, :], 0.0), writes=[kpeB2])
                    kb.dma("sp", kpe[0:64, :], dd["kpeT_all"], writes=[kpeB])
                    emit_mla(P, tb, yT, yB, dd, kpe, [kpeB, kpeB2])
                gt, gB = vecs["gn"]
                rstd = kb.sb("rstdB", [128, 512], F32)
                rB = kb.buf("rstdB")
                for (c0, c1) in ((0, 4), (4, 10), (10, 16)):
                    bk = gbank(P)
                    for c in range(c0, c1):
                        t, tB = P.tbfR.next()
                        kb.op("act", lambda e: e.activation(out=t[:], in_=yT[:, c * 512:(c + 1) * 512], func=AF.Square),
                              reads=[yB[c]], writes=[tB])
                        kb.op("pe", lambda e: e.matmul(P.bank(bk), P.ones[:], t[:], start=(c == c0), stop=(c == c1 - 1)),
                              reads=[P.onesB, tB], writes=[P.psB[bk]], inc=True)
                    emit_rstd(P, bk, 1.0 / ((c1 - c0) * 128), RMS_EPS, rstd[:], rB)
                    for c in range(c0, c1):
                        kb.op("dve", lambda e: e.scalar_tensor_tensor(out=ynT[:, c * 512:(c + 1) * 512], in0=yT[:, c * 512:(c + 1) * 512],
                                                                      scalar=gt[:, c:c + 1], in1=rstd[:], op0=ALU.mult, op1=ALU.mult),
                              reads=[yB[c], gB, rB], writes=[ynB[c]])
            with kb.scope(), nc.named_scope('oproj_ln1'):
                wR = Ring(kb, "wo", [128, 2048], BF16, 3)
                for oc in range(16):
                    wt, wB = wR.next()
                    kb.dma("pool", wt[:], wo[oc * 128:(oc + 1) * 128, :], writes=[wB])
                    bk = gbank(P)
                    for kc in range(16):
                        kb.op("pe", lambda e: e.matmul(P.bank(bk), wt[:, kc * 128:(kc + 1) * 128], ynT[:, kc * 512:(kc + 1) * 512],
                                                       start=(kc == 0), stop=(kc == 15)),
                              reads=[wB, ynB[kc]], writes=[P.psB[bk]], inc=(kc == 15))
                    emit_resid(P, l, 2, oc, tb, bk)
                emit_ln(P, l, tb, vecs["ln1g"][0], vecs["ln1b"][0], vecs["ln1g"][1])
        with kb.scope(), nc.named_scope('ffn'):
            NH = NFC // 2
            u2 = kb.sb("u2T", [128, 16 * 1024], BF16)
            u2B = [[kb.buf(f"u2_{c}_{tb}") for tb in range(2)] for c in range(16)]
            hT = kb.sb("hT", [128, NH * 1024], BF16)
            hB = [[kb.buf(f"h{c}_{tb}") for tb in range(2)] for c in range(NH)]
            for c in range(16):
                for tb in range(2):
                    kb.op("dve", lambda e: e.tensor_scalar(out=u2[:, c1024(c, tb * 512, tb * 512 + 512)],
                                                           in0=xT[:, c1024(c, tb * 512, tb * 512 + 512)],
                                                           scalar1=mcol(P, l, 4, c, True), scalar2=mcol(P, l, 3, c),
                                                           op0=ALU.mult, op1=ALU.add),
                          reads=[xB[c][tb], mB], writes=[u2B[c][tb]])
            wR = Ring(kb, "wgu", [128, 4096], BF16, 3)
            wR2 = Ring(kb, "wdn", [128, NH * 128], BF16, 3)
            for half in range(2):
                for f in range(NH):
                    fc = half * NH + f
                    wt, wB = wR.next()
                    kb.dma("pool", wt[:], wgu[fc * 128:(fc + 1) * 128, :], writes=[wB])
                    for tb in range(2):
                        bg, bu = gbank(P), gbank(P)
                        for (bk, off) in ((bg, 0), (bu, 2048)):
                            for kc in range(16):
                                kb.op("pe", lambda e: e.matmul(P.bank(bk), wt[:, off + kc * 128:off + (kc + 1) * 128],
                                                               u2[:, c1024(kc, tb * 512, tb * 512 + 512)],
                                                               start=(kc == 0), stop=(kc == 15)),
                                      reads=[wB, u2B[kc][tb]], writes=[P.psB[bk]], inc=(kc == 15))
                        t, tB = P.tmpR.next()
                        kb.op("act", lambda e: e.activation(out=t[:], in_=P.bank(bg), func=AF.Silu), reads=[P.psB[bg]], writes=[tB])
                        kb.op("dve", lambda e: e.tensor_tensor(out=hT[:, c1024(f, tb * 512, tb * 512 + 512)], in0=P.bank(bu), in1=t[:], op=ALU.mult),
                              reads=[P.psB[bu], tB], writes=[hB[f][tb]])
                for oc in range(16):
                    wt, wB = wR2.next()
                    row = (half * 16 + oc) * 128
                    kb.dma("pool", wt[:], wdn[row:row + 128, :], writes=[wB])
                    for tb in range(2):
                        bk = gbank(P)
                        for f in range(NH):
                            kb.op("pe", lambda e: e.matmul(P.bank(bk), wt[:, f * 128:(f + 1) * 128], hT[:, c1024(f, tb * 512, tb * 512 + 512)],
                                                           start=(f == 0), stop=(f == NH - 1)),
                                  reads=[wB, hB[f][tb]], writes=[P.psB[bk]], inc=(f == NH - 1))
                        if half == 0:
                            emit_resid(P, l, 5, oc, tb, bk)
                        else:
                            xs = xT[:, c1024(oc, tb * 512, tb * 512 + 512)]
                            kb.op("dve", lambda e: e.scalar_tensor_tensor(out=xs, in0=P.bank(bk), scalar=mcol(P, l, 5, oc, True), in1=xs,
                                                                          op0=ALU.mult, op1=ALU.add),
                                  reads=[P.psB[bk], xB[oc][tb], mB], writes=[xB[oc][tb]])
            for tb in range(2):
                emit_ln(P, l, tb, vecs["ln2g"][0], vecs["ln2b"][0], vecs["ln2g"][1])


def fm_vec(v):
    return np.ascontiguousarray(v.reshape(-1, 128).T)


def lhs_layout(w, cols=None):
    K, N = w.shape
    nk, no = K // 128, N // 128
    return np.ascontiguousarray(w.reshape(nk, 128, no, 128).transpose(2, 1, 0, 3).reshape(no * 128, nk * 128))


def rhs_layout(w):
    K, N = w.shape
    nk = K // 128
    return np.ascontiguousarray(w.reshape(nk, 128, N).transpose(1, 0, 2).reshape(128, nk * N))


def prep_layer(inp, l):
    w = {}
    w_in = inp["w_in"][l]
    swap64 = np.concatenate([np.arange(32, 64), np.arange(0, 32)])
    kr_cols = np.arange(2304, 2368)
    cols = np.concatenate([np.arange(0, 512), np.arange(512, 1024), np.arange(2368, 3136), np.arange(3136, 3392),
                           np.arange(1536, 2048), np.arange(2048, 2304), kr_cols, kr_cols, kr_cols[swap64], kr_cols[swap64]])
    w[f"winfm{l}"] = lhs_layout(w_in[:, cols])
    vcols = np.concatenate([np.arange(1024, 1536), np.arange(3392, 3648)])
    w[f"winv{l}"] = rhs_layout(w_in[:, vcols])
    uq = inp["mla_w_uq"][l].reshape(512, 6, 192)
    w[f"wuqn{l}"] = rhs_layout(np.ascontiguousarray(uq[:, :, :128]).reshape(512, 768))
    w[f"wuqr{l}"] = rhs_layout(np.ascontiguousarray(uq[:, :, 128:]).reshape(512, 384))
    w[f"wuqs{l}"] = rhs_layout(np.ascontiguousarray(uq[:, :, 128:][:, :, swap64]).reshape(512, 384))
    ukv = inp["mla_w_ukv"][l].reshape(256, 6, 256)
    w[f"wukk{l}"] = rhs_layout(np.ascontiguousarray(ukv[:, :, :128]).reshape(256, 768))
    w[f"wukv{l}"] = rhs_layout(np.ascontiguousarray(ukv[:, :, 128:]).reshape(256, 768))
    w[f"gq{l}"] = fm_vec(inp["mla_q_norm"][l])
    w[f"gkv{l}"] = fm_vec(inp["mla_kv_norm"][l])
    return w


def prep_layer_B(inp, l):
    w = {}
    w[f"wo{l}"] = lhs_layout(inp["w_o"][l])
    wg = lhs_layout(inp["w_gu"][l][:, :DFF]).reshape(NFC, 128, 2048)
    wu = lhs_layout(inp["w_gu"][l][:, DFF:]).reshape(NFC, 128, 2048)
    w[f"wgu{l}"] = np.ascontiguousarray(np.concatenate([wg, wu], axis=2).reshape(NFC * 128, 4096))
    hd = DFF // 2
    w[f"wdn{l}"] = np.ascontiguousarray(np.concatenate([lhs_layout(inp["w_down"][l][0:hd]), lhs_layout(inp["w_down"][l][hd:])], axis=0))
    for nm, key in [("gn", "out_norm_g"), ("ln1g", "ln1_g"), ("ln1b", "ln1_b"), ("ln2g", "ln2_g"), ("ln2b", "ln2_b")]:
        w["B_" + nm] = fm_vec(inp[key][l])
    w["B_sink"] = np.ascontiguousarray(np.broadcast_to(inp["swa_sink"][l][None, :], (128, 6)))
    return w


def ada_chunks(inp, chunks):
    w = np.concatenate([inp["w_ada"][l][:, c * 128:(c + 1) * 128] for l, c in chunks], axis=1)
    b = np.concatenate([inp["b_ada"][l][c * 128:(c + 1) * 128] for l, c in chunks])[None, :]
    G = w.shape[1] // 512
    w = w.reshape(16, 128, G, 512).transpose(2, 1, 0, 3).reshape(G * 128, 16 * 512)
    return np.ascontiguousarray(w), np.ascontiguousarray(b)


ADA_REST = [(0, c) for c in range(32, 96)] + [(1, c) for c in range(96)]


def gather_mod(r1):
    allc = np.concatenate([r["modshare"].reshape(NSHARE, 128).T for r in r1], axis=1)
    m0 = np.zeros((128, 96), np.float32)
    m0[:, 32:96] = allc[:, 0:64]
    m1 = np.ascontiguousarray(allc[:, 64:160])
    return m0, m1


def rope_tables(core):
    half = 32
    inv = (10000.0 ** (-np.arange(half, dtype=np.float32) / half)).astype(np.float32)
    pos = np.arange(core * TPC, (core + 1) * TPC, dtype=np.float32)
    ang = (pos[:, None] * inv[None, :]).astype(np.float32)
    cos = np.cos(ang).astype(np.float32).T
    sin = np.sin(ang).astype(np.float32).T
    c64 = np.concatenate([cos, cos], 0)
    s64 = np.concatenate([-sin, sin], 0)
    return (np.ascontiguousarray(np.concatenate([c64, c64], 0)), np.ascontiguousarray(np.concatenate([s64, s64], 0)))


def na_bias_tables(rpb, core):
    out = np.full((NQT, 4, 128, NA_KT, 128), NEG, dtype=np.float32)
    qi = np.arange(128)
    for j in range(NQT):
        r_even = core * 16 + 2 * j
        qrow = r_even + qi // 64
        qcol = qi % 64
        r0 = np.clip(qrow - 4, 0, 120)
        c0 = np.clip(qcol - 8, 0, 48)
        for kt in range(NA_KT):
            ktile = core * 8 + j - NA_HALO + kt
            if ktile < 0 or ktile >= 64:
                continue
            ki = np.arange(128)
            krow = 2 * ktile + ki // 64
            kcol = ki % 64
            rin = (krow[:, None] >= r0[None, :]) & (krow[:, None] < r0[None, :] + 8)
            cin = (kcol[:, None] >= c0[None, :]) & (kcol[:, None] < c0[None, :] + 16)
            roff = np.clip(krow[:, None] - qrow[None, :] + 7, 0, 14)
            coff = np.clip(kcol[:, None] - qcol[None, :], -15, 15) + 15
            m = rin & cin
            for h in range(4):
                vals = rpb[h][roff, coff]
                out[j, h, :, kt, :] = np.where(m, vals, np.float32(NEG))
    return out.reshape(NQT * 4 * 128, NA_KT * 128)


def sw_bias_tables(core):
    out = np.full((NQT, 6, 128, SW_KT, 128), NEG, dtype=np.float32)
    slopes = np.array([2.0 ** (-8.0 * (i + 1) / 6) for i in range(6)], dtype=np.float32)
    qi = np.arange(128)
    for j in range(NQT):
        qpos = (core * 8 + j) * 128 + qi
        for kt in range(SW_KT):
            ktile = core * 8 + j - 1 + kt
            if ktile < 0 or ktile >= 64:
                continue
            kpos = ktile * 128 + np.arange(128)
            dist = np.abs(qpos[None, :] - kpos[:, None])
            m = dist <= 128
            for h in range(6):
                out[j, h, :, kt, :] = np.where(m, -slopes[h] * dist.astype(np.float32), np.float32(NEG))
    return out.reshape(NQT * 6 * 128, SW_KT * 128)


def v_layout(v_tok, nh):
    T = v_tok.shape[0]
    return np.ascontiguousarray(v_tok.reshape(T // 128, 128, nh, 128).transpose(2, 1, 0, 3).reshape(nh * 128, T))


def exchange(outs):
    cat = lambda k, ax: np.concatenate([o[k] for o in outs], axis=ax)
    kaT = cat("kaT", 1)
    kcT = cat("kcT", 1)
    va = v_layout(cat("va", 0), 4)
    vc = v_layout(cat("vc", 0), 2)
    knT = cat("knT", 1)
    kpeT = np.ascontiguousarray(cat("kpeT", 1)[0:64])
    vb = v_layout(cat("vb", 0), 6)

    def win(a, core, halo, n):
        t0 = core * 8 - halo
        res = np.zeros((a.shape[0], n * 128), dtype=a.dtype)
        lo, hi = max(t0, 0), min(t0 + n, 64)
        res[:, (lo - t0) * 128:(hi - t0) * 128] = a[:, lo * 128:hi * 128]
        return res

    res = []
    for i in range(NCORE):
        d = {"B_qaT": outs[i]["qaT"], "B_qcT": outs[i]["qcT"], "B_qnT": outs[i]["qnT"], "B_qpT": outs[i]["qpT"],
             "B_kaT_win": win(kaT, i, NA_HALO, 14), "B_va_win": win(va, i, NA_HALO, 14),
             "B_kcT_win": win(kcT, i, SW_HALO, 10), "B_vc_win": win(vc, i, SW_HALO, 10),
             "B_knT_all": knT, "B_kpeT_all": kpeT, "B_vb_all": vb}
        res.append(d)
    return res


_PROGS = {}


def get_prog(kind):
    if kind not in _PROGS:
        _PROGS[kind] = build(kind)
    return _PROGS[kind]


TRACE = False
LAST = {}


def run(kind, in_maps):
    P = get_prog(kind)
    maps = [{k: np.ascontiguousarray(m[k]) for k in P.dr if k in m} for m in in_maps]
    if TRACE:
        res = run_bass_kernel_spmd(P.nc, maps, core_ids=list(range(NCORE)), trace=True)
        LAST[kind] = res
    else:
        res = run_bass_kernel_spmd(P.nc, maps, core_ids=list(range(NCORE)))
    return res.results


def kernel(**inp):
    inp = {k: np.asarray(v) for k, v in inp.items()}
    x = inp["x"][0]
    xT = np.ascontiguousarray(x.T)
    cT = fm_vec(inp["c"][0])
    ropes = [rope_tables(i) for i in range(NCORE)]
    swb = [sw_bias_tables(i) for i in range(NCORE)]
    com = {"cT": cT}
    com["wada_own"], com["bada_own"] = ada_chunks(inp, [(0, c) for c in range(32)])
    com.update(prep_layer(inp, 0))
    maps = []
    for i in range(NCORE):
        m = dict(com)
        m["xT_in"] = xT[:, i * TPC:(i + 1) * TPC]
        m["cosT"], m["ssinT"] = ropes[i]
        m["wada_shr"], m["bada_shr"] = ada_chunks(inp, ADA_REST[i * NSHARE:(i + 1) * NSHARE])
        maps.append(m)
    r1 = run("A0", maps)
    modfull0, modfull1 = gather_mod(r1)
    ex = exchange(r1)
    com = {"modfull0": modfull0, "modfull1": modfull1}
    com.update(prep_layer_B(inp, 0))
    com.update(prep_layer(inp, 1))
    maps = []
    for i in range(NCORE):
        m = dict(com)
        m.update(ex[i])
        m["xT_in"] = xT[:, i * TPC:(i + 1) * TPC]
        m["cosT"], m["ssinT"] = ropes[i]
        m["B_nab"] = na_bias_tables(inp["na_rpb"][0], i)
        m["B_swb"] = swb[i]
        maps.append(m)
    r2 = run("B0A1", maps)
    ex = exchange(r2)
    com = {"modfull1": modfull1}
    com.update(prep_layer_B(inp, 1))
    maps = []
    for i in range(NCORE):
        m = dict(com)
        m.update(ex[i])
        m["xT_in"] = r2[i]["xT_out"]
        m["B_nab"] = na_bias_tables(inp["na_rpb"][1], i)
        m["B_swb"] = swb[i]
        maps.append(m)
    r3 = run("B1", maps)
    outT = np.concatenate([r["xT_out"] for r in r3], axis=1)
    return np.ascontiguousarray(outT.T)[None].astype(np.float32)
```
